# Optimizing a Trainium2 kernel written in Bass

```python
import math
import jax, jax.numpy as jnp
from jax import lax
import numpy as np

D_MODEL = 4096
BATCH = 2
SEQ = 4096
DEPTH = 1

CHUNK = 64
PLE_DIM = 256
D_MIX = D_MODEL
HEAD_DIM = 128
ATT_WIDTH = D_MIX // 2
N_ATT_HEADS = ATT_WIDTH // HEAD_DIM
LRU_WIDTH = D_MIX - ATT_WIDTH
N_LRU_BLOCKS = 16
LRU_BLOCK = LRU_WIDTH // N_LRU_BLOCKS
D_IN_PROJ = 3 * ATT_WIDTH + 2 * LRU_WIDTH
REC_CONV = 4
RG_C = 8.0
D_FF = 3 * D_MODEL
FF_CONV = 3
Q_BLOCK = 128
EPS = 1e-6

kernel_name = "hybrid_stickbreak_rglru_convffn_block"


def rmsnorm(x, g):
    xf = x.astype(jnp.float32)
    y = xf * lax.rsqrt(jnp.mean(xf * xf, axis=-1, keepdims=True) + EPS)
    return (y * g.astype(jnp.float32)).astype(x.dtype)


def causal_depthwise_conv(x, w, b):
    K, C = w.shape
    y = lax.conv_general_dilated(
        x, w[:, None, :].astype(x.dtype), window_strides=(1,), padding=[(K - 1, 0)],
        dimension_numbers=("NWC", "WIO", "NWC"), feature_group_count=C)
    return y + b.astype(x.dtype)


def stick_breaking_attention(q, k, v):
    S = q.shape[2]
    q = q * (1.0 / math.sqrt(HEAD_DIM))
    outs = []
    for start in range(0, S, Q_BLOCK):
        end = start + Q_BLOCK
        qb = q[:, :, start:end]
        kb = k[:, :, :end]
        vb = v[:, :, :end]
        z = jnp.einsum("bhqd,bhtd->bhqt", qb, kb).astype(jnp.float32)
        t_idx = start + jnp.arange(Q_BLOCK)
        s_idx = jnp.arange(end)
        mask = s_idx[None, :] < t_idx[:, None]
        log_beta = jax.nn.log_sigmoid(z)
        log_1m = jnp.where(mask, log_beta - z, 0.0)
        suffix = lax.cumsum(log_1m, axis=3, reverse=True) - log_1m
        a = jnp.where(mask, jnp.exp(log_beta + suffix), 0.0)
        outs.append(jnp.einsum("bhqt,bhtd->bhqd", a.astype(vb.dtype), vb))
    return jnp.concatenate(outs, axis=2)


def chunked_linear_scan(a, b):
    B, S, W = a.shape
    n = S // CHUNK
    a = a.reshape(B, n, CHUNK, W)
    b = b.reshape(B, n, CHUNK, W)

    def comb(l, r):
        return (l[0] * r[0], r[0] * l[1] + r[1])

    a_cum, h_loc = lax.associative_scan(comb, (a, b), axis=2)

    def step(carry, inp):
        ac, hc = inp
        h = hc + ac * carry[:, None, :]
        return h[:, -1], h

    _, h = lax.scan(step, jnp.zeros((B, W), jnp.float32),
                    (jnp.swapaxes(a_cum, 0, 1), jnp.swapaxes(h_loc, 0, 1)))
    return jnp.swapaxes(h, 0, 1).reshape(B, S, W)


def rg_lru(xc, w_a, b_a, w_x, b_x, lam):
    B, S, W = xc.shape
    xb = xc.reshape(B, S, N_LRU_BLOCKS, LRU_BLOCK)
    r = jax.nn.sigmoid((jnp.einsum("bsnc,ncd->bsnd", xb, w_a).reshape(B, S, W) + b_a).astype(jnp.float32))
    i = jax.nn.sigmoid((jnp.einsum("bsnc,ncd->bsnd", xb, w_x).reshape(B, S, W) + b_x).astype(jnp.float32))
    log_a = -RG_C * r * jax.nn.softplus(-lam.astype(jnp.float32))
    a = jnp.exp(log_a)
    mult = jnp.sqrt(-jnp.expm1(2.0 * log_a))
    h = chunked_linear_scan(a, mult * (i * xc.astype(jnp.float32)))
    return h.astype(xc.dtype)


def setup_inputs(seed: int = 0) -> dict:
    key = jax.random.key(seed)
    ks = jax.random.split(key, 32)
    f32 = jnp.float32

    def nrm(k, shape, fan_in):
        return jax.random.normal(k, shape, f32) * (fan_in ** -0.5)

    def gain(k, shape):
        return 1.0 + 0.02 * jax.random.normal(k, shape, f32)

    def bias(k, shape):
        return 0.01 * jax.random.normal(k, shape, f32)

    a0 = jax.random.uniform(ks[9], (DEPTH, LRU_WIDTH), f32, 0.9, 0.999) ** (1.0 / RG_C)
    lam = jnp.log(a0) - jnp.log1p(-a0)
    return {
        "x": jax.random.normal(ks[0], (BATCH, SEQ, D_MODEL), f32),
        "p": jax.random.normal(ks[1], (DEPTH, BATCH, SEQ, PLE_DIM), f32),
        "g_mix": gain(ks[2], (DEPTH, D_MODEL)),
        "w_in": nrm(ks[3], (DEPTH, D_MODEL, D_IN_PROJ), D_MODEL),
        "w_rconv": nrm(ks[4], (DEPTH, REC_CONV, LRU_WIDTH), REC_CONV),
        "b_rconv": bias(ks[5], (DEPTH, LRU_WIDTH)),
        "w_rg_a": nrm(ks[6], (DEPTH, N_LRU_BLOCKS, LRU_BLOCK, LRU_BLOCK), LRU_BLOCK),
        "b_rg_a": bias(ks[7], (DEPTH, LRU_WIDTH)),
        "w_rg_x": nrm(ks[8], (DEPTH, N_LRU_BLOCKS, LRU_BLOCK, LRU_BLOCK), LRU_BLOCK),
        "b_rg_x": bias(ks[10], (DEPTH, LRU_WIDTH)),
        "lam": lam,
        "g_att_out": gain(ks[11], (DEPTH, ATT_WIDTH)),
        "g_rec_out": gain(ks[12], (DEPTH, LRU_WIDTH)),
        "w_out": nrm(ks[13], (DEPTH, D_MIX, D_MODEL), D_MIX),
        "g_ffn": gain(ks[14], (DEPTH, D_MODEL)),
        "w_up": nrm(ks[15], (DEPTH, D_MODEL, 2 * D_FF), D_MODEL),
        "w_ffconv": nrm(ks[16], (DEPTH, FF_CONV, 2 * D_FF), FF_CONV),
        "b_ffconv": bias(ks[17], (DEPTH, 2 * D_FF)),
        "w_down": nrm(ks[18], (DEPTH, D_FF, D_MODEL), D_FF),
        "g_ple": gain(ks[19], (DEPTH, D_MODEL)),
        "w_ple": nrm(ks[20], (DEPTH, PLE_DIM, D_MODEL), PLE_DIM),
        "w_ple_gate": nrm(ks[21], (DEPTH, D_MODEL, D_MODEL), D_MODEL),
        "g_final": gain(ks[22], (D_MODEL,)),
    }


def reference(x, p, g_mix, w_in, w_rconv, b_rconv, w_rg_a, b_rg_a, w_rg_x, b_rg_x, lam,
              g_att_out, g_rec_out, w_out, g_ffn, w_up, w_ffconv, b_ffconv, w_down,
              g_ple, w_ple, w_ple_gate, g_final):
    B, S, _ = x.shape
    h = x
    for i in range(DEPTH):
        a_in = rmsnorm(h, g_mix[i])
        proj = a_in @ w_in[i]
        q, k, v, xr, yr = jnp.split(
            proj, [ATT_WIDTH, 2 * ATT_WIDTH, 3 * ATT_WIDTH, 3 * ATT_WIDTH + LRU_WIDTH], axis=-1)
        to_heads = lambda t: t.reshape(B, S, N_ATT_HEADS, HEAD_DIM).transpose(0, 2, 1, 3)
        att = stick_breaking_attention(to_heads(q), to_heads(k), to_heads(v))
        att = att.transpose(0, 2, 1, 3).reshape(B, S, ATT_WIDTH)
        xc = causal_depthwise_conv(xr, w_rconv[i], b_rconv[i])
        rec = jax.nn.gelu(yr) * rg_lru(xc, w_rg_a[i], b_rg_a[i], w_rg_x[i], b_rg_x[i], lam[i])
        mixed = jnp.concatenate([rmsnorm(att, g_att_out[i]), rmsnorm(rec, g_rec_out[i])], axis=-1)
        h = h + mixed @ w_out[i]
        m = rmsnorm(h, g_ffn[i])
        u = causal_depthwise_conv(m @ w_up[i], w_ffconv[i], b_ffconv[i])
        gate, up = jnp.split(u, 2, axis=-1)
        h = h + (jax.nn.gelu(gate) * up) @ w_down[i]
        ple_gate = jax.nn.sigmoid(rmsnorm(h, g_ple[i]) @ w_ple_gate[i])
        h = h + (p[i] @ w_ple[i]) * ple_gate
    return rmsnorm(h, g_final)
```

```python
import numpy as np
from contextlib import ExitStack
import concourse.bass as bass
import concourse.mybir as mybir
from concourse.bass_utils import run_bass_kernel_spmd

F32, BF16 = mybir.dt.float32, mybir.dt.bfloat16
AF = mybir.ActivationFunctionType
ALU = mybir.AluOpType

D = 4096
SW = 4096
OH = 1152
OH0 = SW - OH
OWN = 1024
NQ = 384
EPS = 1e-6
NEG = -30000.0
ENG = ["pe", "act", "dve", "pool", "sp"]


class Sched:
    def __init__(self):
        self.ops = {e: [] for e in ENG}
        self.cnt = {e: 0 for e in ENG}
        self.dcnt = {}
        self.waited = {e: {} for e in ENG}
        self.buf = {}

    def _need(self, eng, tok, waits):
        kind, name, val = tok
        if kind == "e" and name == eng:
            return
        if kind == "d":
            val = 16 * self.dcnt[name]
        key = (kind, name)
        if self.waited[eng].get(key, 0) >= val:
            return
        self.waited[eng][key] = val
        waits.append((key, val))

    def _deps(self, eng, reads, writes):
        waits = []
        for k in reads:
            st = self.buf.get(k)
            if st and st["w"] is not None:
                self._need(eng, st["w"], waits)
        for k in writes:
            st = self.buf.get(k)
            if st:
                if st["w"] is not None:
                    self._need(eng, st["w"], waits)
                for t in st["r"].values():
                    self._need(eng, t, waits)
        return waits

    def _commit(self, tok, reads, writes):
        for k in reads:
            st = self.buf.setdefault(k, {"w": None, "r": {}})
            st["r"][(tok[0], tok[1])] = tok
        for k in writes:
            self.buf[k] = {"w": tok, "r": {}}

    def op(self, eng, fn, reads=(), writes=(), signal=True):
        waits = self._deps(eng, reads, writes)
        if signal:
            self.cnt[eng] += 1
            tok = ("e", eng, self.cnt[eng])
        else:
            tok = ("e", eng, self.cnt[eng] + 1)
        self._commit(tok, reads, writes)
        self.ops[eng].append((waits, fn, ("e", eng) if signal else None))

    def dma(self, q, out, in_, semkey, reads=(), writes=()):
        waits = self._deps(q, reads, writes)
        self.dcnt[semkey] = self.dcnt.get(semkey, 0) + 1
        tok = ("d", semkey, 16 * self.dcnt[semkey])
        self._commit(tok, reads, writes)
        self.ops[q].append((waits, (lambda e, o=out, i=in_: e.dma_start(out=o, in_=i)), ("d", semkey)))

    def barrier(self):
        for e in ENG:
            waits = []
            for o in ENG:
                if o != e and self.cnt[o] > 0:
                    self._need(e, ("e", o, self.cnt[o]), waits)
            for k in self.dcnt:
                self._need(e, ("d", k, 0), waits)
            self.ops[e].append((waits, None, None))
        self.buf = {}

    def semkeys(self):
        return [("e", e) for e in ENG] + [("d", k) for k in self.dcnt]

    def emit(self, name, e, sems):
        for waits, fn, inc in self.ops[name]:
            for key, val in waits:
                e.wait_ge(sems[key], val)
            if fn is None:
                continue
            ins = fn(e)
            if inc is not None:
                ins.then_inc(sems[inc], 1 if inc[0] == "e" else 16)


class Arena:
    def __init__(self, ap, nbytes):
        self.ap, self.n, self.off = ap, nbytes, 0

    def alloc(self, nelem, dt):
        sz = 4 if dt == F32 else 2
        nb = (nelem * sz + 63) // 64 * 64
        assert self.off + nb <= self.n, ("arena overflow", self.off, nb, self.n)
        a = self.ap[:, self.off // 4:(self.off + nb) // 4]
        self.off += nb
        if dt != F32:
            a = a.bitcast(dt)
        return a[:, :nelem]

    def mark(self):
        return self.off

    def release(self, m):
        self.off = m


def build_program(debug=False):
    nc = bass.Bass("TRN2", target_bir_lowering=False)
    S = Sched()

    def din(name, shape):
        return nc.dram_tensor(name, list(shape), F32, kind="ExternalInput").ap()

    xw = din("xw", [SW, D])
    p_own = din("p_own", [OWN, 256])
    kbias_d = din("kbias", [128, 32])
    tokvalid_d = din("tokvalid", [128, SW])
    ident_d = din("ident", [128, 128])
    negtri_d = din("negtri", [128, 128])
    negones_d = din("negones", [128, 128])
    cmask_d = din("cmask", [128, 3, NQ])
    g_mix_d = din("g_mix", [128, 32])
    w_in = din("w_in", [D, 10240])
    w_rconv_d = din("w_rconv", [128, 16, 4])
    b_rconv_d = din("b_rconv", [128, 16])
    w_rg_a = din("w_rg_a", [16, 128, 128])
    b_rg_a_d = din("b_rg_a", [128, 16])
    w_rg_x = din("w_rg_x", [16, 128, 128])
    b_rg_x_d = din("b_rg_x", [128, 16])
    lam_d = din("lam", [128, 16])
    g_att_d = din("g_att_out", [128, 16])
    g_rec_d = din("g_rec_out", [128, 16])
    w_out = din("w_out", [D, D])
    g_ffn_d = din("g_ffn", [128, 32])
    w_up = din("w_up", [D, 24576])
    w_ffconv_d = din("w_ffconv", [128, 192, 3])
    b_ffconv_d = din("b_ffconv", [128, 192])
    w_down = din("w_down", [12288, D])
    g_ple_d = din("g_ple", [128, 32])
    w_ple = din("w_ple", [256, D])
    w_ple_gate = din("w_ple_gate", [D, D])
    g_final_d = din("g_final", [128, D])
    out_d = nc.dram_tensor("out", [OWN, D], F32, kind="ExternalOutput").ap()

    def dscr(name, shape, dt):
        if debug:
            return nc.dram_tensor(name, list(shape), dt, kind="ExternalOutput").ap()
        return nc.dram_tensor(name, list(shape), dt).ap()

    aT_s = dscr("aT_s", [32, 128, SW], BF16)
    kT_s = dscr("kT_s", [16, 128, SW], BF16)
    v_s = dscr("v_s", [SW, 2048], BF16)
    qT_s = dscr("qT_s", [16, 128, OH], BF16)
    xrT_s = dscr("xrT_s", [16, 128, SW], F32)
    yrT_s = dscr("yrT_s", [16, 128, OH], F32)
    attT_s = dscr("attT_s", [16, 128, OH], F32)
    recT_s = dscr("recT_s", [16, 128, OH], F32)
    mixT_s = dscr("mixT_s", [32, 128, OH], BF16)
    h1_s = dscr("h1_s", [OH, D], F32)
    hS_s = dscr("hS_s", [OWN, D], F32)

    stack = ExitStack()
    ARENA_BYTES = 200 * 1024
    arena_t = stack.enter_context(nc.sbuf_tensor("arena", [128, ARENA_BYTES // 4], F32))
    cst_t = stack.enter_context(nc.sbuf_tensor("cst", [128, 1536], F32))
    ps_t = stack.enter_context(nc.psum_tensor("ps", [128, 8, 512], F32))
    A = Arena(arena_t[:, :], ARENA_BYTES)
    C = Arena(cst_t[:, :], 1536 * 4)

    def bank(b):
        return ps_t[:, b, :]

    identb = C.alloc(128, BF16)
    negtri = C.alloc(128, BF16)
    negones = C.alloc(128, BF16)
    onesb = C.alloc(128, BF16)
    onecol = C.alloc(1, F32)
    epscol = C.alloc(1, F32)
    kbias = C.alloc(32, F32)
    S.dma("pool", identb, ident_d[:, :], "c0", writes=["identb"])
    S.dma("pool", negtri, negtri_d[:, :], "c0", writes=["negtri"])
    S.dma("pool", negones, negones_d[:, :], "c0", writes=["negones"])
    S.dma("sp", kbias, kbias_d[:, :], "c1", writes=["kbias"])
    S.op("dve", lambda e: e.memset(onesb, 1.0), writes=["onesb"])
    S.op("dve", lambda e: e.memset(onecol, 1.0), writes=["onecol"])
    S.op("dve", lambda e: e.memset(epscol, EPS), writes=["epscol"])
    S.barrier()

    evac_flip = [0]

    def evac_copy(out, in_, reads, writes, scale=None):
        evac_flip[0] ^= 1
        if evac_flip[0] or scale is not None:
            if scale is None:
                S.op("act", lambda e: e.activation(out=out, in_=in_, func=AF.Copy), reads=reads, writes=writes)
            else:
                S.op("act", lambda e: e.activation(out=out, in_=in_, func=AF.Copy, scale=scale), reads=reads, writes=writes)
        else:
            S.op("dve", lambda e: e.tensor_copy(out=out, in_=in_), reads=reads, writes=writes)

    def load_w(slot_ap, slot_key, semkey, w_ap, kc0, nkc, c0, ncols):
        wv = w_ap.rearrange("(c p) f -> p c f", p=128)
        step = 8
        for k in range(0, nkc, step):
            n = min(step, nkc - k)
            S.dma("pool", slot_ap[:, k:k + n, :], wv[:, kc0 + k:kc0 + k + n, c0:c0 + ncols], semkey, writes=[slot_key])

    def norm_T(ntiles, src_fn, g_dram, dst_res=None, dst_scr=None, tag="n"):
        gsb = A.alloc(32, F32)
        S.dma("sp", gsb, g_dram[:, :], tag + "g", writes=[tag + "g"])
        xt = [A.alloc(D, F32) for _ in range(2)]
        junk = A.alloc(D, BF16)
        xs = [A.alloc(D, BF16) for _ in range(2)]
        stat = A.alloc(8, F32)
        aTt = None
        if dst_res is None:
            aTt = [A.alloc(32 * 128, BF16).rearrange("p (c t) -> p c t", c=32) for _ in range(2)]
        for tt in range(ntiles):
            s = tt % 2
            S.dma("sp", xt[s], src_fn(tt), (tag + "x", s), writes=[(tag + "xt", s)])
            ms, sd, rs = stat[:, 3 * s:3 * s + 1], stat[:, 3 * s + 1:3 * s + 2], stat[:, 3 * s + 2:3 * s + 3]
            S.op("act", lambda e, s=s, ms=ms: e.activation(out=junk, in_=xt[s], func=AF.Square, scale=1.0 / 64.0, accum_out=ms),
                 reads=[(tag + "xt", s)], writes=[tag + "junk", (tag + "ms", s)])
            S.op("act", lambda e, ms=ms, sd=sd: e.activation(out=sd, in_=ms, func=AF.Sqrt, bias=epscol, scale=1.0),
                 reads=[(tag + "ms", s), "epscol"], writes=[(tag + "sd", s)])
            S.op("dve", lambda e, sd=sd, rs=rs: e.reciprocal(out=rs, in_=sd), reads=[(tag + "sd", s)], writes=[(tag + "rs", s)])
            S.op("act", lambda e, s=s, rs=rs: e.activation(out=xs[s], in_=xt[s], func=AF.Copy, scale=rs),
                 reads=[(tag + "xt", s), (tag + "rs", s)], writes=[(tag + "xs", s)])
            if dst_res is not None:
                dst = dst_res[:, :, tt * 128:(tt + 1) * 128]
                dkey = (tag + "res", tt)
            else:
                dst = aTt[s]
                dkey = (tag + "aTt", s)
            for g4 in range(4):
                b = (tt * 4 + g4) % 2
                pb = bank(b).bitcast(BF16)[:, 0:1024].rearrange("p (c t) -> p c t", c=8)
                for i in range(8):
                    dc = g4 * 8 + i
                    S.op("pe", lambda e, pb=pb, i=i, dc=dc, s=s: e.transpose(out=pb[:, i, :], in_=xs[s][:, dc * 128:(dc + 1) * 128], identity=identb),
                         reads=[(tag + "xs", s), "identb"], writes=[("ps", b)], signal=(i == 7))
                gb = gsb[:, g4 * 8:g4 * 8 + 8].unsqueeze(2).broadcast_to([128, 8, 128])
                S.op("dve", lambda e, pb=pb, dst=dst, g4=g4, gb=gb: e.tensor_tensor(out=dst[:, g4 * 8:g4 * 8 + 8, :], in0=pb, in1=gb, op=ALU.mult),
                     reads=[("ps", b), tag + "g"], writes=[dkey])
            if dst_res is None:
                S.dma("sp", dst_scr[:, :, tt * 128:(tt + 1) * 128].rearrange("c p t -> p c t"), aTt[s], (tag + "st", s),
                      reads=[dkey], writes=[(tag + "scr", tt)])

    norm_T(32, lambda tt: xw[tt * 128:(tt + 1) * 128, :], g_mix_d, dst_scr=aT_s, tag="s0")
    S.barrier()
    A.release(0)

    def s1():
        at = [A.alloc(32 * 512, BF16).rearrange("p (c t) -> p c t", c=32) for _ in range(2)]
        wsl = [A.alloc(32 * 512, BF16).rearrange("p (c t) -> p c t", c=32) for _ in range(2)]
        stg_f = [A.alloc(512, F32) for _ in range(4)]
        stg_b = [A.alloc(512, BF16) for _ in range(4)]
        allblk = [(i * 512, 512) for i in range(8)]
        ownblk = [(OH0 + i * NQ, NQ) for i in range(3)]
        groups = []
        for g in range(4):
            groups.append(("k", 2048 + g * 512, allblk, "fm"))
        for g in range(4):
            groups.append(("v", 4096 + g * 512, allblk, "tm"))
        for g in range(4):
            groups.append(("xr", 6144 + g * 512, allblk, "fm"))
        for g in range(4):
            groups.append(("q", g * 512, ownblk, "fm"))
        for g in range(4):
            groups.append(("yr", 8192 + g * 512, ownblk, "fm"))
        work = [(gi, bi) for gi, gr in enumerate(groups) for bi in range(len(gr[2]))]
        nat = [0]
        nst = [0]
        pbk = [0]

        def load_at(idx):
            gi, bi = work[idx]
            t0, nt = groups[gi][2][bi]
            s = idx % 2
            for k in range(0, 32, 8):
                S.dma("sp", at[s][:, k:k + 8, :nt], aT_s[k:k + 8, :, t0:t0 + nt].rearrange("c p t -> p c t"), ("at", s), writes=[("at", s)])

        def load_wg(gi):
            load_w(wsl[gi % 2], ("wsl", gi % 2), ("wsl", gi % 2), w_in, 0, 32, groups[gi][1], 512)

        load_wg(0)
        load_at(0)
        for idx, (gi, bi) in enumerate(work):
            name, c0, blks, lay = groups[gi]
            t0, nt = blks[bi]
            s = idx % 2
            if bi == 0 and gi + 1 < len(groups):
                load_wg(gi + 1)
            if idx + 1 < len(work):
                load_at(idx + 1)
            W = wsl[gi % 2]
            g = (c0 % 2048) // 512
            for sub in range(4):
                b = pbk[0] % 4
                pbk[0] += 1
                if lay == "fm":
                    for dc in range(32):
                        S.op("pe", lambda e, b=b, W=W, sub=sub, dc=dc, s=s, nt=nt: e.matmul(bank(b)[:, :nt], W[:, dc, sub * 128:(sub + 1) * 128], at[s][:, dc, :nt], start=(dc == 0), stop=(dc == 31)),
                             reads=[("wsl", gi % 2), ("at", s)], writes=[("ps", b)], signal=(dc == 31))
                    fc = g * 4 + sub
                    k = nst[0] % 4
                    nst[0] += 1
                    if name in ("k", "q"):
                        stg = stg_b[k]
                        sc = float(1.0 / np.sqrt(128.0)) if name == "q" else None
                        evac_copy(stg[:, :nt], bank(b)[:, :nt], [("ps", b)], [("stg", k)], scale=sc)
                        if name == "k":
                            dst = kT_s[fc, :, t0:t0 + nt]
                        else:
                            dst = qT_s[fc, :, t0 - OH0:t0 - OH0 + nt]
                    else:
                        stg = stg_f[k]
                        evac_copy(stg[:, :nt], bank(b)[:, :nt], [("ps", b)], [("stg", k)])
                        if name == "xr":
                            dst = xrT_s[fc, :, t0:t0 + nt]
                        else:
                            dst = yrT_s[fc, :, t0 - OH0:t0 - OH0 + nt]
                    S.dma("sp", dst, stg[:, :nt], ("stg", k), reads=[("stg", k)], writes=[("s1o", name, fc, t0)])
                else:
                    for dc in range(32):
                        S.op("pe", lambda e, b=b, W=W, sub=sub, dc=dc, s=s: e.matmul(bank(b)[:, :512], at[s][:, dc, sub * 128:(sub + 1) * 128], W[:, dc, :], start=(dc == 0), stop=(dc == 31)),
                             reads=[("wsl", gi % 2), ("at", s)], writes=[("ps", b)], signal=(dc == 31))
                    k = nst[0] % 4
                    nst[0] += 1
                    evac_copy(stg_b[k], bank(b)[:, :512], [("ps", b)], [("stg", k)])
                    S.dma("sp", v_s[t0 + sub * 128:t0 + (sub + 1) * 128, g * 512:(g + 1) * 512], stg_b[k], ("stg", k),
                          reads=[("stg", k)], writes=[("s1o", "v", g, t0, sub)])

    s1()
    S.barrier()
    A.release(0)

    def s2():
        cm = A.alloc(3 * NQ, BF16).rearrange("p (m t) -> p m t", m=3)
        S.dma("pool", cm, cmask_d[:, :, :], "cm", writes=["cm"])
        kT = [A.alloc(SW, BF16) for _ in range(2)]
        vh = [A.alloc(32 * 128, BF16).rearrange("p (k d) -> p k d", k=32) for _ in range(2)]
        qT = [A.alloc(OH, BF16) for _ in range(2)]
        eb = [A.alloc(NQ, F32) for _ in range(2)]
        Lh = [A.alloc(NQ, BF16) for _ in range(3)]
        ab = [A.alloc(NQ, BF16) for _ in range(3)]
        Rb = [A.alloc(NQ, BF16) for _ in range(3)]
        ost = [A.alloc(NQ, F32) for _ in range(2)]

        def load_head(h):
            s = h % 2
            S.dma("sp", kT[s], kT_s[h, :, :], ("kT", s), writes=[("kT", s)])
            S.dma("sp", vh[s], v_s[:, h * 128:(h + 1) * 128].rearrange("(k p) d -> p k d", p=128), ("vh", s), writes=[("vh", s)])
            S.dma("sp", qT[s], qT_s[h, :, :], ("qT", s), writes=[("qT", s)])

        pairs = []
        for h in range(16):
            for qb in range(3):
                t0 = OH0 + qb * NQ
                kmax = (t0 + NQ - 2) // 128
                for kb in range(kmax, -1, -1):
                    Dd = t0 - 128 * kb
                    mi = (-Dd // 128) if Dd in (0, -128, -256) else None
                    pairs.append(dict(h=h, qb=qb, kb=kb, mi=mi, first=(kb == kmax), last=(kb == 0)))
        n = len(pairs)
        gidx = [0] * n
        c = 0
        for i, p in enumerate(pairs):
            if p["first"]:
                c = 0
            gidx[i] = c
            c += 1

        def zmm(i, b):
            p = pairs[i]
            s = p["h"] % 2
            kb, qo = p["kb"], p["qb"] * NQ
            return s, kb, qo

        def stageA(i):
            p = pairs[i]
            if i == 0:
                load_head(0)
            if p["qb"] == 0 and gidx[i] == 4 and p["h"] + 1 < 16:
                load_head(p["h"] + 1)
            s, kb, qo = zmm(i, 0)
            b = i % 2
            part = p["mi"] is not None
            S.op("pe", lambda e: e.matmul(bank(b)[:, :NQ], kT[s][:, kb * 128:(kb + 1) * 128], qT[s][:, qo:qo + NQ], start=True, stop=not part),
                 reads=[("kT", s), ("qT", s)], writes=[("ps", b)], signal=not part)
            if part:
                mi = p["mi"]
                S.op("pe", lambda e: e.matmul(bank(b)[:, :NQ], identb, cm[:, mi, :], start=False, stop=True),
                     reads=["identb", "cm"], writes=[("ps", b)], signal=True)
            S.op("act", lambda e: e.activation(out=eb[b], in_=bank(b)[:, :NQ], func=AF.Exp, bias=kbias[:, kb:kb + 1], scale=1.0),
                 reads=[("ps", b), "kbias"], writes=[("eb", b)])
            l = i % 3
            S.op("act", lambda e: e.activation(out=Lh[l], in_=eb[b], func=AF.Ln, bias=onecol, scale=1.0),
                 reads=[("eb", b), "onecol"], writes=[("Lh", l)])
            if not p["last"]:
                rn = (i + 1) % 3
                ro = i % 3
                if p["first"]:
                    S.op("dve", lambda e: e.tensor_copy(out=Rb[rn], in_=Lh[l]), reads=[("Lh", l)], writes=[("Rb", rn)])
                else:
                    S.op("dve", lambda e: e.tensor_tensor(out=Rb[rn], in0=Rb[ro], in1=Lh[l], op=ALU.add),
                         reads=[("Lh", l), ("Rb", ro)], writes=[("Rb", rn)])

        def stageB(i):
            p = pairs[i]
            s, kb, qo = zmm(i, 0)
            b = 2 + i % 2
            l = i % 3
            gi = gidx[i]
            part = p["mi"] is not None
            S.op("pe", lambda e: e.matmul(bank(b)[:, :NQ], kT[s][:, kb * 128:(kb + 1) * 128], qT[s][:, qo:qo + NQ], start=True, stop=False),
                 reads=[("kT", s), ("qT", s)], writes=[("ps", b)], signal=False)
            if part:
                mi = p["mi"]
                S.op("pe", lambda e: e.matmul(bank(b)[:, :NQ], identb, cm[:, mi, :], start=False, stop=False),
                     reads=["identb", "cm"], writes=[("ps", b)], signal=False)
            S.op("pe", lambda e: e.matmul(bank(b)[:, :NQ], negtri, Lh[l], start=False, stop=p["first"]),
                 reads=["negtri", ("Lh", l)], writes=[("ps", b)], signal=p["first"])
            if not p["first"]:
                S.op("pe", lambda e: e.matmul(bank(b)[:, :NQ], negones, Rb[i % 3], start=False, stop=True),
                     reads=["negones", ("Rb", i % 3)], writes=[("ps", b)], signal=True)
            S.op("act", lambda e: e.activation(out=ab[l], in_=bank(b)[:, :NQ], func=AF.Exp, bias=kbias[:, kb:kb + 1], scale=1.0),
                 reads=[("ps", b), "kbias"], writes=[("ab", l)])

        grp = [0]

        def stageO(i):
            p = pairs[i]
            s, kb, qo = zmm(i, 0)
            l = i % 3
            if p["first"]:
                grp[0] += 1
            b = 4 + grp[0] % 2
            S.op("pe", lambda e: e.matmul(bank(b)[:, :NQ], vh[s][:, kb, :], ab[l], start=p["first"], stop=p["last"]),
                 reads=[("vh", s), ("ab", l)], writes=[("ps", b)], signal=p["last"])
            if p["last"]:
                k = grp[0] % 2
                evac_copy(ost[k], bank(b)[:, :NQ], [("ps", b)], [("ost", k)])
                S.dma("sp", attT_s[p["h"], :, qo:qo + NQ], ost[k], ("ost", k), reads=[("ost", k)], writes=[("att", p["h"], qo)])

        for st in range(n + 2):
            if st < n:
                stageA(st)
            if 0 <= st - 1 < n:
                stageB(st - 1)
            if 0 <= st - 2 < n:
                stageO(st - 2)

    s2()
    S.barrier()
    A.release(0)

    def s3():
        SEG = 1024
        tv = A.alloc(SW, F32)
        S.dma("sp", tv, tokvalid_d[:, :], "tv", writes=["tv"])
        wrc = A.alloc(64, F32).rearrange("p (c k) -> p c k", c=16)
        brc = A.alloc(16, F32)
        bga = A.alloc(16, F32)
        bgx = A.alloc(16, F32)
        lam = A.alloc(16, F32)
        c8 = A.alloc(16, F32)
        c16 = A.alloc(16, F32)
        S.dma("sp", wrc, w_rconv_d[:, :, :], "s3c", writes=["s3c"])
        S.dma("sp", brc, b_rconv_d[:, :], "s3c", writes=["s3c1"])
        S.dma("sp", bga, b_rg_a_d[:, :], "s3c", writes=["s3c2"])
        S.dma("sp", bgx, b_rg_x_d[:, :], "s3c", writes=["s3c3"])
        S.dma("sp", lam, lam_d[:, :], "s3c", writes=["s3c4"])
        S.op("act", lambda e: e.activation(out=c8, in_=lam, func=AF.Exp, scale=-1.0), reads=["s3c4"], writes=["c8"])
        S.op("act", lambda e: e.activation(out=c8, in_=c8, func=AF.Ln, bias=onecol, scale=1.0), reads=["c8", "onecol"], writes=["c8"])
        S.op("dve", lambda e: e.tensor_scalar_mul(out=c16, in0=c8, scalar1=-16.0), reads=["c8"], writes=["c16"])
        S.op("dve", lambda e: e.tensor_scalar_mul(out=c8, in0=c8, scalar1=-8.0), reads=["c8", "c16"], writes=["c8"])
        xr = [A.alloc(3 + SW, F32) for _ in range(2)]
        for s in range(2):
            S.op("dve", lambda e, s=s: e.memset(xr[s][:, 0:3], 0.0), writes=[("xr", s)])
        yr = [A.alloc(OH, F32) for _ in range(2)]
        Wa = [A.alloc(128, BF16) for _ in range(2)]
        Wx = [A.alloc(128, BF16) for _ in range(2)]
        hb = [A.alloc(SW, F32) for _ in range(2)]
        xc = [A.alloc(SEG, F32) for _ in range(2)]
        xcb = [A.alloc(SEG, BF16) for _ in range(2)]
        rr = [A.alloc(SEG, F32) for _ in range(2)]
        ii = [A.alloc(SEG, F32) for _ in range(2)]
        aa = [A.alloc(SEG, F32) for _ in range(2)]
        mm = [A.alloc(SEG, F32) for _ in range(2)]
        gy = [A.alloc(OH, F32) for _ in range(2)]
        seg_i = [0]

        def loadc(c):
            s = c % 2
            S.dma("sp", xr[s][:, 3:], xrT_s[c, :, :], ("xr", s), writes=[("xr", s)])
            S.dma("sp", yr[s], yrT_s[c, :, :], ("yr", s), writes=[("yr", s)])
            S.dma("pool", Wa[s], w_rg_a[c, :, :], ("Wg", s), writes=[("Wa", s)])
            S.dma("pool", Wx[s], w_rg_x[c, :, :], ("Wg", s), writes=[("Wx", s)])

        loadc(0)
        for c in range(16):
            s = c % 2
            if c + 1 < 16:
                loadc(c + 1)
            for sg in range(4):
                u = seg_i[0] % 2
                seg_i[0] += 1
                s0 = sg * SEG
                X = xr[s]
                S.op("act", lambda e, X=X, u=u, s0=s0, c=c: e.activation(out=xc[u], in_=X[:, 3 + s0:3 + s0 + SEG], func=AF.Identity, bias=brc[:, c:c + 1], scale=wrc[:, c, 3:4]),
                     reads=[("xr", s), "s3c", "s3c1"], writes=[("xc", u)])
                for k in range(3):
                    S.op("dve", lambda e, X=X, u=u, s0=s0, c=c, k=k: e.scalar_tensor_tensor(out=xc[u], in0=X[:, k + s0:k + s0 + SEG], scalar=wrc[:, c, k:k + 1], in1=xc[u], op0=ALU.mult, op1=ALU.add),
                         reads=[("xr", s), ("xc", u), "s3c"], writes=[("xc", u)])
                S.op("pool", lambda e, u=u: e.tensor_copy(out=xcb[u], in_=xc[u]), reads=[("xc", u)], writes=[("xcb", u)])
                for half in range(2):
                    ba, bx = (half * 2) % 4, (half * 2 + 1) % 4
                    sl = slice(half * 512, (half + 1) * 512)
                    S.op("pe", lambda e, ba=ba, u=u, sl=sl, s=s: e.matmul(bank(ba)[:, :512], Wa[s], xcb[u][:, sl], start=True, stop=True),
                         reads=[("Wa", s), ("xcb", u)], writes=[("ps", ba)])
                    S.op("pe", lambda e, bx=bx, u=u, sl=sl, s=s: e.matmul(bank(bx)[:, :512], Wx[s], xcb[u][:, sl], start=True, stop=True),
                         reads=[("Wx", s), ("xcb", u)], writes=[("ps", bx)])
                    S.op("act", lambda e, ba=ba, u=u, sl=sl, c=c: e.activation(out=rr[u][:, sl], in_=bank(ba)[:, :512], func=AF.Sigmoid, bias=bga[:, c:c + 1], scale=1.0),
                         reads=[("ps", ba), "s3c2"], writes=[("rr", u, half)])
                    S.op("act", lambda e, bx=bx, u=u, sl=sl, c=c: e.activation(out=ii[u][:, sl], in_=bank(bx)[:, :512], func=AF.Sigmoid, bias=bgx[:, c:c + 1], scale=1.0),
                         reads=[("ps", bx), "s3c3"], writes=[("ii", u, half)])
                S.op("act", lambda e, u=u, c=c: e.activation(out=aa[u], in_=rr[u], func=AF.Exp, scale=c8[:, c:c + 1]),
                     reads=[("rr", u, 0), ("rr", u, 1), "c8"], writes=[("aa", u)])
                S.op("act", lambda e, u=u, c=c: e.activation(out=mm[u], in_=rr[u], func=AF.Exp, scale=c16[:, c:c + 1]),
                     reads=[("rr", u, 0), ("rr", u, 1), "c16"], writes=[("mm", u)])
                S.op("act", lambda e, u=u: e.activation(out=mm[u], in_=mm[u], func=AF.Sqrt, bias=onecol, scale=-1.0),
                     reads=[("mm", u), "onecol"], writes=[("mm", u)])
                S.op("pool", lambda e, u=u: e.tensor_tensor(out=ii[u], in0=ii[u], in1=xc[u], op=ALU.mult),
                     reads=[("ii", u, 0), ("ii", u, 1), ("xc", u)], writes=[("ii", u, 0), ("ii", u, 1)])
                S.op("pool", lambda e, u=u, s0=s0: e.tensor_tensor(out=ii[u], in0=ii[u], in1=tv[:, s0:s0 + SEG], op=ALU.mult),
                     reads=[("ii", u, 0), ("ii", u, 1), "tv"], writes=[("ii", u, 0), ("ii", u, 1)])
                S.op("dve", lambda e, u=u: e.tensor_tensor(out=mm[u], in0=mm[u], in1=ii[u], op=ALU.mult),
                     reads=[("mm", u), ("ii", u, 0), ("ii", u, 1)], writes=[("mm", u)])
                H = hb[s]
                if sg == 0:
                    S.op("dve", lambda e, u=u, H=H: e.tensor_tensor_scan(out=H[:, 0:SEG], data0=aa[u], data1=mm[u], initial=0.0, op0=ALU.mult, op1=ALU.add),
                         reads=[("aa", u), ("mm", u)], writes=[("hb", s)])
                else:
                    S.op("dve", lambda e, u=u, H=H, s0=s0: e.tensor_tensor_scan(out=H[:, s0:s0 + SEG], data0=aa[u], data1=mm[u], initial=H[:, s0 - 1:s0], op0=ALU.mult, op1=ALU.add),
                         reads=[("aa", u), ("mm", u), ("hb", s)], writes=[("hb", s)])
            S.op("act", lambda e, s=s: e.activation(out=gy[s], in_=yr[s], func=AF.Gelu_apprx_tanh), reads=[("yr", s)], writes=[("gy", s)])
            S.op("dve", lambda e, s=s: e.tensor_tensor(out=gy[s], in0=gy[s], in1=hb[s][:, OH0:SW], op=ALU.mult),
                 reads=[("gy", s), ("hb", s)], writes=[("gy", s)])
            S.dma("sp", recT_s[c, :, :], gy[s], ("gy", s), reads=[("gy", s)], writes=[("rec", c)])

    s3()
    S.barrier()
    A.release(0)

    def s4():
        ga = A.alloc(16, F32)
        gr = A.alloc(16, F32)
        S.dma("sp", ga, g_att_d[:, :], "s4g", writes=["ga"])
        S.dma("sp", gr, g_rec_d[:, :], "s4g", writes=["gr"])
        blk = [A.alloc(16 * NQ, F32).rearrange("p (c t) -> p c t", c=16) for _ in range(2)]
        sq = [A.alloc(16 * NQ, BF16).rearrange("p (c t) -> p c t", c=16) for _ in range(2)]
        mx = [A.alloc(16 * NQ, BF16).rearrange("p (c t) -> p c t", c=16) for _ in range(2)]
        rs = [A.alloc(NQ, F32) for _ in range(2)]
        it = 0
        for grp_i, (src, gv, gk) in enumerate(((attT_s, ga, "ga"), (recT_s, gr, "gr"))):
            for tb in range(3):
                s = it % 2
                it += 1
                t0 = tb * NQ
                S.dma("sp", blk[s], src[:, :, t0:t0 + NQ].rearrange("c p t -> p c t"), ("blk", s), writes=[("blk", s)])
                S.op("act", lambda e, s=s: e.activation(out=sq[s], in_=blk[s], func=AF.Square), reads=[("blk", s)], writes=[("sq", s)])
                b = s
                for c in range(16):
                    S.op("pe", lambda e, s=s, c=c, b=b: e.matmul(bank(b)[:, :NQ], onesb, sq[s][:, c, :], start=(c == 0), stop=(c == 15)),
                         reads=["onesb", ("sq", s)], writes=[("ps", b)], signal=(c == 15))
                S.op("act", lambda e, s=s, b=b: e.activation(out=rs[s], in_=bank(b)[:, :NQ], func=AF.Sqrt, bias=epscol, scale=1.0 / 2048.0),
                     reads=[("ps", b), "epscol"], writes=[("rs", s)])
                S.op("dve", lambda e, s=s: e.reciprocal(out=rs[s], in_=rs[s]), reads=[("rs", s)], writes=[("rs", s)])
                for c in range(16):
                    S.op("dve", lambda e, s=s, c=c, gv=gv: e.scalar_tensor_tensor(out=mx[s][:, c, :], in0=blk[s][:, c, :], scalar=gv[:, c:c + 1], in1=rs[s], op0=ALU.mult, op1=ALU.mult),
                         reads=[("blk", s), ("rs", s), gk], writes=[("mx", s)])
                S.dma("sp", mixT_s[grp_i * 16:(grp_i + 1) * 16, :, t0:t0 + NQ].rearrange("c p t -> p c t"), mx[s], ("mx", s),
                      reads=[("mx", s)], writes=[("mix", grp_i, tb)])

    s4()
    S.barrier()
    A.release(0)

    def s5():
        mixT = A.alloc(32 * OH, BF16).rearrange("p (c t) -> p c t", c=32)
        for k in range(0, 32, 8):
            S.dma("sp", mixT[:, k:k + 8, :], mixT_s[k:k + 8, :, :].rearrange("c p t -> p c t"), "mixT", writes=["mixT"])
        wsl = [A.alloc(32 * 512, BF16).rearrange("p (c t) -> p c t", c=32) for _ in range(2)]
        xin = [A.alloc(512, F32) for _ in range(3)]
        load_w(wsl[0], ("wsl", 0), ("wsl", 0), w_out, 0, 32, 0, 512)
        it = 0
        for cg in range(8):
            if cg + 1 < 8:
                load_w(wsl[(cg + 1) % 2], ("wsl", (cg + 1) % 2), ("wsl", (cg + 1) % 2), w_out, 0, 32, (cg + 1) * 512, 512)
            W = wsl[cg % 2]
            for tt in range(9):
                k = it % 3
                b = it % 4
                it += 1
                S.dma("sp", xin[k], xw[OH0 + tt * 128:OH0 + (tt + 1) * 128, cg * 512:(cg + 1) * 512], ("xin", k), writes=[("xin", k)])
                for dc in range(32):
                    S.op("pe", lambda e, b=b, W=W, dc=dc, tt=tt: e.matmul(bank(b)[:, :512], mixT[:, dc, tt * 128:(tt + 1) * 128], W[:, dc, :], start=(dc == 0), stop=(dc == 31)),
                         reads=["mixT", ("wsl", cg % 2)], writes=[("ps", b)], signal=(dc == 31))
                S.op("dve", lambda e, b=b, k=k: e.tensor_tensor(out=xin[k], in0=bank(b)[:, :512], in1=xin[k], op=ALU.add),
                     reads=[("ps", b), ("xin", k)], writes=[("xin", k)])
                S.dma("sp", h1_s[tt * 128:(tt + 1) * 128, cg * 512:(cg + 1) * 512], xin[k], ("xin", k), reads=[("xin", k)], writes=[("h1", tt, cg)])

    s5()
    S.barrier()
    A.release(0)

    def ffn():
        mT = A.alloc(32 * OH, BF16).rearrange("p (c t) -> p c t", c=32)
        m0 = A.mark()
        norm_T(9, lambda tt: h1_s[tt * 128:(tt + 1) * 128, :], g_ffn_d, dst_res=mT, tag="s6")
        S.barrier()
        A.release(m0)
        wfc = A.alloc(192 * 3, F32).rearrange("p (c k) -> p c k", c=192)
        bfc = A.alloc(192, F32)
        S.dma("sp", wfc, w_ffconv_d[:, :, :], "fc", writes=["wfc"])
        S.dma("sp", bfc, b_ffconv_d[:, :], "fc", writes=["bfc"])
        NP, HP = 8, 12
        wg = [A.alloc(32 * 128, BF16).rearrange("p (c t) -> p c t", c=32) for _ in range(2)]
        wu = [A.alloc(32 * 128, BF16).rearrange("p (c t) -> p c t", c=32) for _ in range(2)]
        hidT = A.alloc(HP * OWN, BF16).rearrange("p (c t) -> p c t", c=HP)
        wd = [A.alloc(HP * 512, BF16).rearrange("p (c t) -> p c t", c=HP) for _ in range(2)]
        ug = [A.alloc(2 + OH, F32) for _ in range(2)]
        uu = [A.alloc(2 + OH, F32) for _ in range(2)]
        gc = A.alloc(OWN, F32)
        uc = A.alloc(OWN, F32)
        gl = A.alloc(OWN, F32)
        hin = [A.alloc(512, F32) for _ in range(4)]

        def load_up(hc):
            s = hc % 2
            load_w(wg[s], ("wg", s), ("wg", s), w_up, 0, 32, hc * 128, 128)
            load_w(wu[s], ("wu", s), ("wu", s), w_up, 0, 32, 12288 + hc * 128, 128)

        def load_dn(part, cg):
            s = (part * 8 + cg) % 2
            load_w(wd[s], ("wd", s), ("wd", s), w_down, part * HP, HP, cg * 512, 512)

        pbk = [0]
        hit = [0]
        load_up(0)
        for part in range(NP):
            for hcl in range(HP):
                hc = part * HP + hcl
                s = hc % 2
                if hc + 1 < 96:
                    load_up(hc + 1)
                for tb in range(3):
                    bg = (pbk[0] * 2) % 4
                    bu = bg + 1
                    pbk[0] += 1
                    sl = slice(tb * NQ, (tb + 1) * NQ)
                    for dc in range(32):
                        S.op("pe", lambda e, bg=bg, s=s, dc=dc, sl=sl: e.matmul(bank(bg)[:, :NQ], wg[s][:, dc, :], mT[:, dc, sl], start=(dc == 0), stop=(dc == 31)),
                             reads=[("wg", s)], writes=[("ps", bg)], signal=(dc == 31))
                    for dc in range(32):
                        S.op("pe", lambda e, bu=bu, s=s, dc=dc, sl=sl: e.matmul(bank(bu)[:, :NQ], wu[s][:, dc, :], mT[:, dc, sl], start=(dc == 0), stop=(dc == 31)),
                             reads=[("wu", s)], writes=[("ps", bu)], signal=(dc == 31))
                    S.op("act", lambda e, bg=bg, s=s, tb=tb: e.activation(out=ug[s][:, 2 + tb * NQ:2 + (tb + 1) * NQ], in_=bank(bg)[:, :NQ], func=AF.Copy),
                         reads=[("ps", bg)], writes=[("ug", s, tb)])
                    S.op("dve", lambda e, bu=bu, s=s, tb=tb: e.tensor_copy(out=uu[s][:, 2 + tb * NQ:2 + (tb + 1) * NQ], in_=bank(bu)[:, :NQ]),
                         reads=[("ps", bu)], writes=[("uu", s, tb)])
                ugk = [("ug", s, t) for t in range(3)]
                uuk = [("uu", s, t) for t in range(3)]
                fg, fu = hc, 96 + hc
                S.op("act", lambda e, s=s, fg=fg: e.activation(out=gc, in_=ug[s][:, 130:130 + OWN], func=AF.Identity, bias=bfc[:, fg:fg + 1], scale=wfc[:, fg, 2:3]),
                     reads=ugk + ["wfc", "bfc"], writes=["gc"])
                for k in range(2):
                    S.op("dve", lambda e, s=s, fg=fg, k=k: e.scalar_tensor_tensor(out=gc, in0=ug[s][:, 128 + k:128 + k + OWN], scalar=wfc[:, fg, k:k + 1], in1=gc, op0=ALU.mult, op1=ALU.add),
                         reads=ugk + ["gc", "wfc"], writes=["gc"])
                S.op("act", lambda e, s=s, fu=fu: e.activation(out=uc, in_=uu[s][:, 130:130 + OWN], func=AF.Identity, bias=bfc[:, fu:fu + 1], scale=wfc[:, fu, 2:3]),
                     reads=uuk + ["wfc", "bfc"], writes=["uc"])
                for k in range(2):
                    S.op("dve", lambda e, s=s, fu=fu, k=k: e.scalar_tensor_tensor(out=uc, in0=uu[s][:, 128 + k:128 + k + OWN], scalar=wfc[:, fu, k:k + 1], in1=uc, op0=ALU.mult, op1=ALU.add),
                         reads=uuk + ["uc", "wfc"], writes=["uc"])
                S.op("act", lambda e: e.activation(out=gl, in_=gc, func=AF.Gelu_apprx_tanh), reads=["gc"], writes=["gl"])
                S.op("dve", lambda e, hcl=hcl: e.tensor_tensor(out=hidT[:, hcl, :], in0=gl, in1=uc, op=ALU.mult),
                     reads=["gl", "uc"], writes=["hidT"])
            load_dn(part, 0)
            for cg in range(8):
                if cg + 1 < 8:
                    load_dn(part, cg + 1)
                sW = (part * 8 + cg) % 2
                for tt in range(8):
                    k = hit[0] % 4
                    hit[0] += 1
                    b = 4 + k
                    if part == 0:
                        src = h1_s[128 + tt * 128:128 + (tt + 1) * 128, cg * 512:(cg + 1) * 512]
                        rk = []
                    else:
                        src = hS_s[tt * 128:(tt + 1) * 128, cg * 512:(cg + 1) * 512]
                        rk = [("hS", tt, cg)]
                    S.dma("sp", hin[k], src, ("hin", k), reads=rk, writes=[("hin", k)])
                    for hcl in range(HP):
                        S.op("pe", lambda e, b=b, hcl=hcl, tt=tt, sW=sW: e.matmul(bank(b)[:, :512], hidT[:, hcl, tt * 128:(tt + 1) * 128], wd[sW][:, hcl, :], start=(hcl == 0), stop=(hcl == HP - 1)),
                             reads=["hidT", ("wd", sW)], writes=[("ps", b)], signal=(hcl == HP - 1))
                    S.op("dve", lambda e, b=b, k=k: e.tensor_tensor(out=hin[k], in0=bank(b)[:, :512], in1=hin[k], op=ALU.add),
                         reads=[("ps", b), ("hin", k)], writes=[("hin", k)])
                    S.dma("sp", hS_s[tt * 128:(tt + 1) * 128, cg * 512:(cg + 1) * 512], hin[k], ("hin", k), reads=[("hin", k)], writes=[("hS", tt, cg)])

    ffn()
    S.barrier()
    A.release(0)

    def ple():
        nT = A.alloc(32 * OWN, BF16).rearrange("p (c t) -> p c t", c=32)
        m0 = A.mark()
        norm_T(8, lambda tt: hS_s[tt * 128:(tt + 1) * 128, :], g_ple_d, dst_res=nT, tag="s9")
        S.barrier()
        A.release(m0)
        pT = A.alloc(2 * OWN, BF16).rearrange("p (c t) -> p c t", c=2)
        pin = [A.alloc(256, F32) for _ in range(2)]
        pbf = [A.alloc(256, BF16) for _ in range(2)]
        for tt in range(8):
            s = tt % 2
            S.dma("sp", pin[s], p_own[tt * 128:(tt + 1) * 128, :], ("pin", s), writes=[("pin", s)])
            S.op("dve", lambda e, s=s: e.tensor_copy(out=pbf[s], in_=pin[s]), reads=[("pin", s)], writes=[("pbf", s)])
            b = s
            pb = bank(b).bitcast(BF16)[:, 0:256].rearrange("p (c t) -> p c t", c=2)
            for c in range(2):
                S.op("pe", lambda e, pb=pb, c=c, s=s: e.transpose(out=pb[:, c, :], in_=pbf[s][:, c * 128:(c + 1) * 128], identity=identb),
                     reads=[("pbf", s), "identb"], writes=[("ps", b)], signal=(c == 1))
            S.op("dve", lambda e, pb=pb, tt=tt: e.tensor_copy(out=pT[:, :, tt * 128:(tt + 1) * 128], in_=pb), reads=[("ps", b)], writes=["pT"])
        wsl = [A.alloc(32 * 512, BF16).rearrange("p (c t) -> p c t", c=32) for _ in range(2)]
        wp = [A.alloc(2 * 512, BF16).rearrange("p (c t) -> p c t", c=2) for _ in range(2)]
        sg = [A.alloc(512, F32) for _ in range(2)]
        hin = [A.alloc(512, F32) for _ in range(3)]

        def loadw(cg):
            s = cg % 2
            load_w(wsl[s], ("wsl", s), ("wsl", s), w_ple_gate, 0, 32, cg * 512, 512)
            load_w(wp[s], ("wp", s), ("wp", s), w_ple, 0, 2, cg * 512, 512)

        loadw(0)
        it = 0
        for cg in range(8):
            if cg + 1 < 8:
                loadw(cg + 1)
            s = cg % 2
            for tt in range(8):
                k = it % 3
                u = it % 2
                bg = (it % 2) * 2
                bp = bg + 1
                it += 1
                S.dma("sp", hin[k], hS_s[tt * 128:(tt + 1) * 128, cg * 512:(cg + 1) * 512], ("hin", k), writes=[("hin", k)])
                for dc in range(32):
                    S.op("pe", lambda e, bg=bg, dc=dc, tt=tt, s=s: e.matmul(bank(bg)[:, :512], nT[:, dc, tt * 128:(tt + 1) * 128], wsl[s][:, dc, :], start=(dc == 0), stop=(dc == 31)),
                         reads=[("wsl", s)], writes=[("ps", bg)], signal=(dc == 31))
                for c in range(2):
                    S.op("pe", lambda e, bp=bp, c=c, tt=tt, s=s: e.matmul(bank(bp)[:, :512], pT[:, c, tt * 128:(tt + 1) * 128], wp[s][:, c, :], start=(c == 0), stop=(c == 1)),
                         reads=["pT", ("wp", s)], writes=[("ps", bp)], signal=(c == 1))
                S.op("act", lambda e, bg=bg, u=u: e.activation(out=sg[u], in_=bank(bg)[:, :512], func=AF.Sigmoid), reads=[("ps", bg)], writes=[("sg", u)])
                S.op("dve", lambda e, bp=bp, u=u: e.tensor_tensor(out=sg[u], in0=bank(bp)[:, :512], in1=sg[u], op=ALU.mult),
                     reads=[("ps", bp), ("sg", u)], writes=[("sg", u)])
                S.op("dve", lambda e, u=u, k=k: e.tensor_tensor(out=hin[k], in0=sg[u], in1=hin[k], op=ALU.add),
                     reads=[("sg", u), ("hin", k)], writes=[("hin", k)])
                S.dma("sp", h1_s[tt * 128:(tt + 1) * 128, cg * 512:(cg + 1) * 512], hin[k], ("hin", k), reads=[("hin", k)], writes=[("h3", tt, cg)])

    ple()
    S.barrier()
    A.release(0)

    def final():
        gf = A.alloc(D, F32)
        S.dma("sp", gf, g_final_d[:, :], "gf", writes=["gf"])
        xt = [A.alloc(D, F32) for _ in range(2)]
        junk = A.alloc(D, BF16)
        ot = [A.alloc(D, F32) for _ in range(2)]
        stat = A.alloc(8, F32)
        for tt in range(8):
            s = tt % 2
            S.dma("sp", xt[s], h1_s[tt * 128:(tt + 1) * 128, :], ("fx", s), writes=[("fx", s)])
            ms, sd, rs = stat[:, 3 * s:3 * s + 1], stat[:, 3 * s + 1:3 * s + 2], stat[:, 3 * s + 2:3 * s + 3]
            S.op("act", lambda e, s=s, ms=ms: e.activation(out=junk, in_=xt[s], func=AF.Square, scale=1.0 / 64.0, accum_out=ms),
                 reads=[("fx", s)], writes=["fjunk", ("fms", s)])
            S.op("act", lambda e, ms=ms, sd=sd: e.activation(out=sd, in_=ms, func=AF.Sqrt, bias=epscol, scale=1.0),
                 reads=[("fms", s), "epscol"], writes=[("fsd", s)])
            S.op("dve", lambda e, sd=sd, rs=rs: e.reciprocal(out=rs, in_=sd), reads=[("fsd", s)], writes=[("frs", s)])
            S.op("dve", lambda e, s=s, rs=rs: e.scalar_tensor_tensor(out=ot[s], in0=xt[s], scalar=rs, in1=gf, op0=ALU.mult, op1=ALU.mult),
                 reads=[("fx", s), ("frs", s), "gf"], writes=[("fo", s)])
            S.dma("sp", out_d[tt * 128:(tt + 1) * 128, :], ot[s], ("fo", s), reads=[("fo", s)], writes=[("out", tt)])

    final()
    S.barrier()

    keys = S.semkeys()
    sems = {}
    for i, k in enumerate(keys):
        sems[k] = stack.enter_context(nc.semaphore("s%d" % i))
    with nc.Block() as block:
        @block.tensor
        def _(e):
            S.emit("pe", e, sems)

        @block.scalar
        def _(e):
            S.emit("act", e, sems)

        @block.vector
        def _(e):
            S.emit("dve", e, sems)

        @block.gpsimd
        def _(e):
            S.emit("pool", e, sems)

        @block.sync
        def _(e):
            S.emit("sp", e, sems)
    stack.close()
    return nc


def _pc(v, n):
    v = np.asarray(v, np.float32).reshape(n // 128, 128)
    return np.ascontiguousarray(v.T)


def kernel(_debug=False, **inp):
    x = np.asarray(inp["x"], np.float32)
    p = np.asarray(inp["p"], np.float32)
    ident = np.eye(128, dtype=np.float32)
    jj, ss = np.meshgrid(np.arange(128), np.arange(128), indexing="ij")
    negtri = np.where(jj >= ss, -1.0, 0.0).astype(np.float32)
    negones = -np.ones((128, 128), np.float32)
    cmask = np.zeros((128, 3, NQ), np.float32)
    sp_ = np.arange(128)[:, None]
    tq = np.arange(NQ)[None, :]
    for mi in range(3):
        Dd = -128 * mi
        cmask[:, mi, :] = np.where(sp_ - tq < Dd, 0.0, NEG)
    common = {
        "ident": ident, "negtri": negtri, "negones": negones, "cmask": cmask,
        "g_mix": _pc(inp["g_mix"][0], 4096),
        "w_in": np.ascontiguousarray(inp["w_in"][0], np.float32),
        "w_rconv": np.ascontiguousarray(np.asarray(inp["w_rconv"][0], np.float32).reshape(4, 16, 128).transpose(2, 1, 0)),
        "b_rconv": _pc(inp["b_rconv"][0], 2048),
        "w_rg_a": np.ascontiguousarray(inp["w_rg_a"][0], np.float32),
        "b_rg_a": _pc(inp["b_rg_a"][0], 2048),
        "w_rg_x": np.ascontiguousarray(inp["w_rg_x"][0], np.float32),
        "b_rg_x": _pc(inp["b_rg_x"][0], 2048),
        "lam": _pc(inp["lam"][0], 2048),
        "g_att_out": _pc(inp["g_att_out"][0], 2048),
        "g_rec_out": _pc(inp["g_rec_out"][0], 2048),
        "w_out": np.ascontiguousarray(inp["w_out"][0], np.float32),
        "g_ffn": _pc(inp["g_ffn"][0], 4096),
        "w_up": np.ascontiguousarray(inp["w_up"][0], np.float32),
        "w_ffconv": np.ascontiguousarray(np.asarray(inp["w_ffconv"][0], np.float32).reshape(3, 192, 128).transpose(2, 1, 0)),
        "b_ffconv": _pc(inp["b_ffconv"][0], 24576),
        "w_down": np.ascontiguousarray(inp["w_down"][0], np.float32),
        "g_ple": _pc(inp["g_ple"][0], 4096),
        "w_ple": np.ascontiguousarray(inp["w_ple"][0], np.float32),
        "w_ple_gate": np.ascontiguousarray(inp["w_ple_gate"][0], np.float32),
        "g_final": np.ascontiguousarray(np.broadcast_to(np.asarray(inp["g_final"], np.float32)[None, :], (128, D))),
    }
    in_maps = []
    for c in range(8):
        b, j = c // 4, c % 4
        end = 1024 * (j + 1)
        P = SW - end
        xwin = np.zeros((SW, D), np.float32)
        xwin[P:] = x[b, :end]
        kb = np.zeros((128, 32), np.float32)
        kb[:, : P // 128] = NEG
        tv = np.ones((128, SW), np.float32)
        tv[:, :P] = 0.0
        m = dict(common)
        m.update({"xw": xwin, "p_own": np.ascontiguousarray(p[0, b, end - 1024:end]), "kbias": kb, "tokvalid": tv})
        in_maps.append(m)
    nc = build_program(debug=_debug)
    res = run_bass_kernel_spmd(nc, in_maps, core_ids=list(range(8)))
    if _debug:
        return res
    out = np.zeros((2, 4096, D), np.float32)
    for c in range(8):
        b, j = c // 4, c % 4
        out[b, 1024 * j:1024 * (j + 1)] = res.results[c]["out"]
    return out
```

```python
import numpy as np
from contextlib import ExitStack
import concourse.bass as bass
import concourse.mybir as mybir
from concourse.bass_utils import run_bass_kernel_spmd

F32, BF16 = mybir.dt.float32, mybir.dt.bfloat16
AF = mybir.ActivationFunctionType
ALU = mybir.AluOpType

D = 4096
SW = 4096
OH = 1152
OH0 = SW - OH
OWN = 1024
NQ = 384
EPS = 1e-6
NEG = -30000.0
ENG = ["pe", "act", "dve", "pool", "sp"]


class Sched:
    def __init__(self):
        self.ops = {e: [] for e in ENG}
        self.cnt = {e: 0 for e in ENG}
        self.dcnt = {}
        self.waited = {e: {} for e in ENG}
        self.buf = {}

    def _need(self, eng, tok, waits):
        kind, name, val = tok
        if kind == "e" and name == eng:
            return
        if kind == "d":
            val = 16 * self.dcnt[name]
        key = (kind, name)
        if self.waited[eng].get(key, 0) >= val:
            return
        self.waited[eng][key] = val
        waits.append((key, val))

    def _deps(self, eng, reads, writes):
        waits = []
        for k in reads:
            st = self.buf.get(k)
            if st and st["w"] is not None:
                self._need(eng, st["w"], waits)
        for k in writes:
            st = self.buf.get(k)
            if st:
                if st["w"] is not None:
                    self._need(eng, st["w"], waits)
                for t in st["r"].values():
                    self._need(eng, t, waits)
        return waits

    def _commit(self, tok, reads, writes):
        for k in reads:
            st = self.buf.setdefault(k, {"w": None, "r": {}})
            st["r"][(tok[0], tok[1])] = tok
        for k in writes:
            self.buf[k] = {"w": tok, "r": {}}

    def op(self, eng, fn, reads=(), writes=(), signal=True):
        waits = self._deps(eng, reads, writes)
        if signal:
            self.cnt[eng] += 1
            tok = ("e", eng, self.cnt[eng])
        else:
            tok = ("e", eng, self.cnt[eng] + 1)
        self._commit(tok, reads, writes)
        self.ops[eng].append((waits, fn, ("e", eng) if signal else None))

    def dma(self, q, out, in_, semkey, reads=(), writes=()):
        waits = self._deps(q, reads, writes)
        self.dcnt[semkey] = self.dcnt.get(semkey, 0) + 1
        tok = ("d", semkey, 16 * self.dcnt[semkey])
        self._commit(tok, reads, writes)
        self.ops[q].append((waits, (lambda e, o=out, i=in_: e.dma_start(out=o, in_=i)), ("d", semkey)))

    def barrier(self):
        for e in ENG:
            waits = []
            for o in ENG:
                if o != e and self.cnt[o] > 0:
                    self._need(e, ("e", o, self.cnt[o]), waits)
            for k in self.dcnt:
                self._need(e, ("d", k, 0), waits)
            self.ops[e].append((waits, None, None))
        self.buf = {}

    def semkeys(self):
        return [("e", e) for e in ENG] + [("d", k) for k in self.dcnt]

    def emit(self, name, e, sems):
        for waits, fn, inc in self.ops[name]:
            for key, val in waits:
                e.wait_ge(sems[key], val)
            if fn is None:
                continue
            ins = fn(e)
            if inc is not None:
                ins.then_inc(sems[inc], 1 if inc[0] == "e" else 16)


class Arena:
    def __init__(self, ap, nbytes):
        self.ap, self.n, self.off = ap, nbytes, 0

    def alloc(self, nelem, dt):
        sz = 4 if dt == F32 else 2
        nb = (nelem * sz + 63) // 64 * 64
        assert self.off + nb <= self.n, ("arena overflow", self.off, nb, self.n)
        a = self.ap[:, self.off // 4:(self.off + nb) // 4]
        self.off += nb
        if dt != F32:
            a = a.bitcast(dt)
        return a[:, :nelem]

    def mark(self):
        return self.off

    def release(self, m):
        self.off = m


def build_program(debug=False):
    nc = bass.Bass("TRN2", target_bir_lowering=False)
    S = Sched()

    def din(name, shape):
        return nc.dram_tensor(name, list(shape), F32, kind="ExternalInput").ap()

    xw = din("xw", [SW, D])
    p_own = din("p_own", [OWN, 256])
    kbias_d = din("kbias", [128, 32])
    tokvalid_d = din("tokvalid", [128, SW])
    ident_d = din("ident", [128, 128])
    negtri_d = din("negtri", [128, 128])
    negones_d = din("negones", [128, 128])
    cmask_d = din("cmask", [128, 3, NQ])
    g_mix_d = din("g_mix", [128, 32])
    w_in = din("w_in", [D, 10240])
    w_rconv_d = din("w_rconv", [128, 16, 4])
    b_rconv_d = din("b_rconv", [128, 16])
    w_rg_a = din("w_rg_a", [16, 128, 128])
    b_rg_a_d = din("b_rg_a", [128, 16])
    w_rg_x = din("w_rg_x", [16, 128, 128])
    b_rg_x_d = din("b_rg_x", [128, 16])
    lam_d = din("lam", [128, 16])
    g_att_d = din("g_att_out", [128, 16])
    g_rec_d = din("g_rec_out", [128, 16])
    w_out = din("w_out", [D, D])
    g_ffn_d = din("g_ffn", [128, 32])
    w_up = din("w_up", [D, 24576])
    w_ffconv_d = din("w_ffconv", [128, 192, 3])
    b_ffconv_d = din("b_ffconv", [128, 192])
    w_down = din("w_down", [12288, D])
    g_ple_d = din("g_ple", [128, 32])
    w_ple = din("w_ple", [256, D])
    w_ple_gate = din("w_ple_gate", [D, D])
    g_final_d = din("g_final", [128, D])
    out_d = nc.dram_tensor("out", [OWN, D], F32, kind="ExternalOutput").ap()

    def dscr(name, shape, dt):
        if debug:
            return nc.dram_tensor(name, list(shape), dt, kind="ExternalOutput").ap()
        return nc.dram_tensor(name, list(shape), dt).ap()

    aT_s = dscr("aT_s", [32, 128, SW], BF16)
    kT_s = dscr("kT_s", [16, 128, SW], BF16)
    v_s = dscr("v_s", [SW, 2048], BF16)
    qT_s = dscr("qT_s", [16, 128, OH], BF16)
    xrT_s = dscr("xrT_s", [16, 128, SW], F32)
    yrT_s = dscr("yrT_s", [16, 128, OH], F32)
    attT_s = dscr("attT_s", [16, 128, OH], F32)
    recT_s = dscr("recT_s", [16, 128, OH], F32)
    mixT_s = dscr("mixT_s", [32, 128, OH], BF16)
    h1_s = dscr("h1_s", [OH, D], F32)
    hS_s = dscr("hS_s", [OWN, D], F32)

    stack = ExitStack()
    ARENA_BYTES = 200 * 1024
    arena_t = stack.enter_context(nc.sbuf_tensor("arena", [128, ARENA_BYTES // 4], F32))
    cst_t = stack.enter_context(nc.sbuf_tensor("cst", [128, 1536], F32))
    ps_t = stack.enter_context(nc.psum_tensor("ps", [128, 8, 512], F32))
    A = Arena(arena_t[:, :], ARENA_BYTES)
    C = Arena(cst_t[:, :], 1536 * 4)

    def bank(b):
        return ps_t[:, b, :]

    identb = C.alloc(128, BF16)
    negtri = C.alloc(128, BF16)
    negones = C.alloc(128, BF16)
    onesb = C.alloc(128, BF16)
    onecol = C.alloc(1, F32)
    epscol = C.alloc(1, F32)
    kbias = C.alloc(32, F32)
    S.dma("pool", identb, ident_d[:, :], "c0", writes=["identb"])
    S.dma("pool", negtri, negtri_d[:, :], "c0", writes=["negtri"])
    S.dma("pool", negones, negones_d[:, :], "c0", writes=["negones"])
    S.dma("sp", kbias, kbias_d[:, :], "c1", writes=["kbias"])
    S.op("dve", lambda e: e.memset(onesb, 1.0), writes=["onesb"])
    S.op("dve", lambda e: e.memset(onecol, 1.0), writes=["onecol"])
    S.op("dve", lambda e: e.memset(epscol, EPS), writes=["epscol"])
    S.barrier()

    evac_flip = [0]

    def evac_copy(out, in_, reads, writes, scale=None):
        evac_flip[0] ^= 1
        if evac_flip[0] or scale is not None:
            if scale is None:
                S.op("act", lambda e: e.activation(out=out, in_=in_, func=AF.Copy), reads=reads, writes=writes)
            else:
                S.op("act", lambda e: e.activation(out=out, in_=in_, func=AF.Copy, scale=scale), reads=reads, writes=writes)
        else:
            S.op("dve", lambda e: e.tensor_copy(out=out, in_=in_), reads=reads, writes=writes)

    def load_w(slot_ap, slot_key, semkey, w_ap, kc0, nkc, c0, ncols):
        wv = w_ap.rearrange("(c p) f -> p c f", p=128)
        step = 8
        for k in range(0, nkc, step):
            n = min(step, nkc - k)
            S.dma("pool", slot_ap[:, k:k + n, :], wv[:, kc0 + k:kc0 + k + n, c0:c0 + ncols], semkey, writes=[slot_key])

    def norm_T(ntiles, src_fn, g_dram, dst_res=None, dst_scr=None, tag="n"):
        gsb = A.alloc(32, F32)
        S.dma("sp", gsb, g_dram[:, :], tag + "g", writes=[tag + "g"])
        xt = [A.alloc(D, F32) for _ in range(2)]
        junk = A.alloc(D, BF16)
        xs = [A.alloc(D, BF16) for _ in range(2)]
        stat = A.alloc(8, F32)
        aTt = None
        if dst_res is None:
            aTt = [A.alloc(32 * 512, BF16).rearrange("p (c t) -> p c t", c=32) for _ in range(2)]
        for tt in range(ntiles):
            s = tt % 2
            S.dma("sp", xt[s], src_fn(tt), (tag + "x", s), writes=[(tag + "xt", s)])
            ms, sd, rs = stat[:, 3 * s:3 * s + 1], stat[:, 3 * s + 1:3 * s + 2], stat[:, 3 * s + 2:3 * s + 3]
            S.op("act", lambda e, s=s, ms=ms: e.activation(out=junk, in_=xt[s], func=AF.Square, scale=1.0 / 64.0, accum_out=ms),
                 reads=[(tag + "xt", s)], writes=[tag + "junk", (tag + "ms", s)])
            S.op("act", lambda e, ms=ms, sd=sd: e.activation(out=sd, in_=ms, func=AF.Sqrt, bias=epscol, scale=1.0),
                 reads=[(tag + "ms", s), "epscol"], writes=[(tag + "sd", s)])
            S.op("dve", lambda e, sd=sd, rs=rs: e.reciprocal(out=rs, in_=sd), reads=[(tag + "sd", s)], writes=[(tag + "rs", s)])
            S.op("act", lambda e, s=s, rs=rs: e.activation(out=xs[s], in_=xt[s], func=AF.Copy, scale=rs),
                 reads=[(tag + "xt", s), (tag + "rs", s)], writes=[(tag + "xs", s)])
            if dst_res is not None:
                dst = dst_res[:, :, tt * 128:(tt + 1) * 128]
                dkey = (tag + "res", tt)
            else:
                gs = (tt // 4) % 2
                dst = aTt[gs][:, :, (tt % 4) * 128:(tt % 4 + 1) * 128]
                dkey = (tag + "aTt", gs)
            for g4 in range(4):
                b = (tt * 4 + g4) % 2
                pb = bank(b).bitcast(BF16)[:, 0:1024].rearrange("p (c t) -> p c t", c=8)
                for i in range(8):
                    dc = g4 * 8 + i
                    S.op("pe", lambda e, pb=pb, i=i, dc=dc, s=s: e.transpose(out=pb[:, i, :], in_=xs[s][:, dc * 128:(dc + 1) * 128], identity=identb),
                         reads=[(tag + "xs", s), "identb"], writes=[("ps", b)], signal=(i == 7))
                gb = gsb[:, g4 * 8:g4 * 8 + 8].unsqueeze(2).broadcast_to([128, 8, 128])
                S.op("dve", lambda e, pb=pb, dst=dst, g4=g4, gb=gb: e.tensor_tensor(out=dst[:, g4 * 8:g4 * 8 + 8, :], in0=pb, in1=gb, op=ALU.mult),
                     reads=[("ps", b), tag + "g"], writes=[dkey])
            if dst_res is None and tt % 4 == 3:
                t0 = (tt // 4) * 512
                for k8 in range(0, 32, 8):
                    S.dma("sp", dst_scr[k8:k8 + 8, :, t0:t0 + 512].rearrange("c p t -> p c t"), aTt[gs][:, k8:k8 + 8, :], (tag + "st", gs),
                          reads=[dkey], writes=[(tag + "scr", tt, k8)])

    norm_T(32, lambda tt: xw[tt * 128:(tt + 1) * 128, :], g_mix_d, dst_scr=aT_s, tag="s0")
    S.barrier()
    A.release(0)

    def s1():
        at = [A.alloc(32 * 512, BF16).rearrange("p (c t) -> p c t", c=32) for _ in range(2)]
        wsl = [A.alloc(32 * 512, BF16).rearrange("p (c t) -> p c t", c=32) for _ in range(2)]
        stg_f = [A.alloc(512, F32) for _ in range(4)]
        stg_b = [A.alloc(512, BF16) for _ in range(4)]
        allblk = [(i * 512, 512) for i in range(8)]
        ownblk = [(OH0 + i * NQ, NQ) for i in range(3)]
        groups = []
        for g in range(4):
            groups.append(("k", 2048 + g * 512, allblk, "fm"))
        for g in range(4):
            groups.append(("v", 4096 + g * 512, allblk, "tm"))
        for g in range(4):
            groups.append(("xr", 6144 + g * 512, allblk, "fm"))
        for g in range(4):
            groups.append(("q", g * 512, ownblk, "fm"))
        for g in range(4):
            groups.append(("yr", 8192 + g * 512, ownblk, "fm"))
        work = [(gi, bi) for gi, gr in enumerate(groups) for bi in range(len(gr[2]))]
        nat = [0]
        nst = [0]
        pbk = [0]

        def load_at(idx):
            gi, bi = work[idx]
            t0, nt = groups[gi][2][bi]
            s = idx % 2
            for k in range(0, 32, 8):
                S.dma("sp", at[s][:, k:k + 8, :nt], aT_s[k:k + 8, :, t0:t0 + nt].rearrange("c p t -> p c t"), ("at", s), writes=[("at", s)])

        def load_wg(gi):
            load_w(wsl[gi % 2], ("wsl", gi % 2), ("wsl", gi % 2), w_in, 0, 32, groups[gi][1], 512)

        load_wg(0)
        load_at(0)
        for idx, (gi, bi) in enumerate(work):
            name, c0, blks, lay = groups[gi]
            t0, nt = blks[bi]
            s = idx % 2
            if bi == 0 and gi + 1 < len(groups):
                load_wg(gi + 1)
            if idx + 1 < len(work):
                load_at(idx + 1)
            W = wsl[gi % 2]
            g = (c0 % 2048) // 512
            for sub in range(4):
                b = pbk[0] % 4
                pbk[0] += 1
                if lay == "fm":
                    for dc in range(32):
                        S.op("pe", lambda e, b=b, W=W, sub=sub, dc=dc, s=s, nt=nt: e.matmul(bank(b)[:, :nt], W[:, dc, sub * 128:(sub + 1) * 128], at[s][:, dc, :nt], start=(dc == 0), stop=(dc == 31)),
                             reads=[("wsl", gi % 2), ("at", s)], writes=[("ps", b)], signal=(dc == 31))
                    fc = g * 4 + sub
                    k = nst[0] % 4
                    nst[0] += 1
                    if name in ("k", "q"):
                        stg = stg_b[k]
                        sc = float(1.0 / np.sqrt(128.0)) if name == "q" else None
                        evac_copy(stg[:, :nt], bank(b)[:, :nt], [("ps", b)], [("stg", k)], scale=sc)
                        if name == "k":
                            dst = kT_s[fc, :, t0:t0 + nt]
                        else:
                            dst = qT_s[fc, :, t0 - OH0:t0 - OH0 + nt]
                    else:
                        stg = stg_f[k]
                        evac_copy(stg[:, :nt], bank(b)[:, :nt], [("ps", b)], [("stg", k)])
                        if name == "xr":
                            dst = xrT_s[fc, :, t0:t0 + nt]
                        else:
                            dst = yrT_s[fc, :, t0 - OH0:t0 - OH0 + nt]
                    S.dma("sp", dst, stg[:, :nt], ("stg", k), reads=[("stg", k)], writes=[("s1o", name, fc, t0)])
                else:
                    for dc in range(32):
                        S.op("pe", lambda e, b=b, W=W, sub=sub, dc=dc, s=s: e.matmul(bank(b)[:, :512], at[s][:, dc, sub * 128:(sub + 1) * 128], W[:, dc, :], start=(dc == 0), stop=(dc == 31)),
                             reads=[("wsl", gi % 2), ("at", s)], writes=[("ps", b)], signal=(dc == 31))
                    k = nst[0] % 4
                    nst[0] += 1
                    evac_copy(stg_b[k], bank(b)[:, :512], [("ps", b)], [("stg", k)])
                    S.dma("sp", v_s[t0 + sub * 128:t0 + (sub + 1) * 128, g * 512:(g + 1) * 512], stg_b[k], ("stg", k),
                          reads=[("stg", k)], writes=[("s1o", "v", g, t0, sub)])

    s1()
    S.barrier()
    A.release(0)

    def s2():
        cm = A.alloc(3 * NQ, BF16).rearrange("p (m t) -> p m t", m=3)
        S.dma("pool", cm, cmask_d[:, :, :], "cm", writes=["cm"])
        kT = [A.alloc(SW, BF16) for _ in range(2)]
        vh = [A.alloc(32 * 128, BF16).rearrange("p (k d) -> p k d", k=32) for _ in range(2)]
        qT = [A.alloc(OH, BF16) for _ in range(2)]
        eb = [A.alloc(NQ, F32) for _ in range(2)]
        Lh = [A.alloc(NQ, BF16) for _ in range(3)]
        ab = [A.alloc(NQ, BF16) for _ in range(3)]
        Rb = [A.alloc(NQ, BF16) for _ in range(3)]
        ost = [A.alloc(NQ, F32) for _ in range(2)]

        def load_head(h):
            s = h % 2
            S.dma("sp", kT[s], kT_s[h, :, :], ("kT", s), writes=[("kT", s)])
            S.dma("sp", vh[s], v_s[:, h * 128:(h + 1) * 128].rearrange("(k p) d -> p k d", p=128), ("vh", s), writes=[("vh", s)])
            S.dma("sp", qT[s], qT_s[h, :, :], ("qT", s), writes=[("qT", s)])

        pairs = []
        for h in range(16):
            for qb in range(3):
                t0 = OH0 + qb * NQ
                kmax = (t0 + NQ - 2) // 128
                for kb in range(kmax, -1, -1):
                    Dd = t0 - 128 * kb
                    mi = (-Dd // 128) if Dd in (0, -128, -256) else None
                    pairs.append(dict(h=h, qb=qb, kb=kb, mi=mi, first=(kb == kmax), last=(kb == 0)))
        n = len(pairs)
        gidx = [0] * n
        c = 0
        for i, p in enumerate(pairs):
            if p["first"]:
                c = 0
            gidx[i] = c
            c += 1

        def zmm(i, b):
            p = pairs[i]
            s = p["h"] % 2
            kb, qo = p["kb"], p["qb"] * NQ
            return s, kb, qo

        def stageA(i):
            p = pairs[i]
            if i == 0:
                load_head(0)
            if p["qb"] == 0 and gidx[i] == 4 and p["h"] + 1 < 16:
                load_head(p["h"] + 1)
            s, kb, qo = zmm(i, 0)
            b = i % 2
            part = p["mi"] is not None
            S.op("pe", lambda e: e.matmul(bank(b)[:, :NQ], kT[s][:, kb * 128:(kb + 1) * 128], qT[s][:, qo:qo + NQ], start=True, stop=not part),
                 reads=[("kT", s), ("qT", s)], writes=[("ps", b)], signal=not part)
            if part:
                mi = p["mi"]
                S.op("pe", lambda e: e.matmul(bank(b)[:, :NQ], identb, cm[:, mi, :], start=False, stop=True),
                     reads=["identb", "cm"], writes=[("ps", b)], signal=True)
            S.op("act", lambda e: e.activation(out=eb[b], in_=bank(b)[:, :NQ], func=AF.Exp, bias=kbias[:, kb:kb + 1], scale=1.0),
                 reads=[("ps", b), "kbias"], writes=[("eb", b)])
            l = i % 3
            S.op("act", lambda e: e.activation(out=Lh[l], in_=eb[b], func=AF.Ln, bias=onecol, scale=1.0),
                 reads=[("eb", b), "onecol"], writes=[("Lh", l)])
            if not p["last"]:
                rn = (i + 1) % 3
                ro = i % 3
                if p["first"]:
                    S.op("dve", lambda e: e.tensor_copy(out=Rb[rn], in_=Lh[l]), reads=[("Lh", l)], writes=[("Rb", rn)])
                else:
                    S.op("dve", lambda e: e.tensor_tensor(out=Rb[rn], in0=Rb[ro], in1=Lh[l], op=ALU.add),
                         reads=[("Lh", l), ("Rb", ro)], writes=[("Rb", rn)])

        def stageB(i):
            p = pairs[i]
            s, kb, qo = zmm(i, 0)
            b = 2 + i % 2
            l = i % 3
            gi = gidx[i]
            part = p["mi"] is not None
            S.op("pe", lambda e: e.matmul(bank(b)[:, :NQ], kT[s][:, kb * 128:(kb + 1) * 128], qT[s][:, qo:qo + NQ], start=True, stop=False),
                 reads=[("kT", s), ("qT", s)], writes=[("ps", b)], signal=False)
            if part:
                mi = p["mi"]
                S.op("pe", lambda e: e.matmul(bank(b)[:, :NQ], identb, cm[:, mi, :], start=False, stop=False),
                     reads=["identb", "cm"], writes=[("ps", b)], signal=False)
            S.op("pe", lambda e: e.matmul(bank(b)[:, :NQ], negtri, Lh[l], start=False, stop=p["first"]),
                 reads=["negtri", ("Lh", l)], writes=[("ps", b)], signal=p["first"])
            if not p["first"]:
                S.op("pe", lambda e: e.matmul(bank(b)[:, :NQ], negones, Rb[i % 3], start=False, stop=True),
                     reads=["negones", ("Rb", i % 3)], writes=[("ps", b)], signal=True)
            S.op("act", lambda e: e.activation(out=ab[l], in_=bank(b)[:, :NQ], func=AF.Exp, bias=kbias[:, kb:kb + 1], scale=1.0),
                 reads=[("ps", b), "kbias"], writes=[("ab", l)])

        grp = [0]

        def stageO(i):
            p = pairs[i]
            s, kb, qo = zmm(i, 0)
            l = i % 3
            if p["first"]:
                grp[0] += 1
            b = 4 + grp[0] % 2
            S.op("pe", lambda e: e.matmul(bank(b)[:, :NQ], vh[s][:, kb, :], ab[l], start=p["first"], stop=p["last"]),
                 reads=[("vh", s), ("ab", l)], writes=[("ps", b)], signal=p["last"])
            if p["last"]:
                k = grp[0] % 2
                evac_copy(ost[k], bank(b)[:, :NQ], [("ps", b)], [("ost", k)])
                S.dma("sp", attT_s[p["h"], :, qo:qo + NQ], ost[k], ("ost", k), reads=[("ost", k)], writes=[("att", p["h"], qo)])

        for st in range(n + 2):
            if st < n:
                stageA(st)
            if 0 <= st - 1 < n:
                stageB(st - 1)
            if 0 <= st - 2 < n:
                stageO(st - 2)

    s2()
    S.barrier()
    A.release(0)

    def s3():
        SEG = 1024
        tv = A.alloc(SW, F32)
        S.dma("sp", tv, tokvalid_d[:, :], "tv", writes=["tv"])
        wrc = A.alloc(64, F32).rearrange("p (c k) -> p c k", c=16)
        brc = A.alloc(16, F32)
        bga = A.alloc(16, F32)
        bgx = A.alloc(16, F32)
        lam = A.alloc(16, F32)
        c8 = A.alloc(16, F32)
        c16 = A.alloc(16, F32)
        S.dma("sp", wrc, w_rconv_d[:, :, :], "s3c", writes=["s3c"])
        S.dma("sp", brc, b_rconv_d[:, :], "s3c", writes=["s3c1"])
        S.dma("sp", bga, b_rg_a_d[:, :], "s3c", writes=["s3c2"])
        S.dma("sp", bgx, b_rg_x_d[:, :], "s3c", writes=["s3c3"])
        S.dma("sp", lam, lam_d[:, :], "s3c", writes=["s3c4"])
        S.op("act", lambda e: e.activation(out=c8, in_=lam, func=AF.Exp, scale=-1.0), reads=["s3c4"], writes=["c8"])
        S.op("act", lambda e: e.activation(out=c8, in_=c8, func=AF.Ln, bias=onecol, scale=1.0), reads=["c8", "onecol"], writes=["c8"])
        S.op("dve", lambda e: e.tensor_scalar_mul(out=c16, in0=c8, scalar1=-16.0), reads=["c8"], writes=["c16"])
        S.op("dve", lambda e: e.tensor_scalar_mul(out=c8, in0=c8, scalar1=-8.0), reads=["c8", "c16"], writes=["c8"])
        xr = [A.alloc(3 + SW, F32) for _ in range(2)]
        for s in range(2):
            S.op("dve", lambda e, s=s: e.memset(xr[s][:, 0:3], 0.0), writes=[("xr", s)])
        yr = [A.alloc(OH, F32) for _ in range(2)]
        Wa = [A.alloc(128, BF16) for _ in range(2)]
        Wx = [A.alloc(128, BF16) for _ in range(2)]
        hb = [A.alloc(SW, F32) for _ in range(2)]
        xc = [A.alloc(SEG, F32) for _ in range(2)]
        xcb = [A.alloc(SEG, BF16) for _ in range(2)]
        rr = [A.alloc(SEG, F32) for _ in range(2)]
        ii = [A.alloc(SEG, F32) for _ in range(2)]
        aa = [A.alloc(SEG, F32) for _ in range(2)]
        mm = [A.alloc(SEG, F32) for _ in range(2)]
        gy = [A.alloc(OH, F32) for _ in range(2)]
        seg_i = [0]

        def loadc(c):
            s = c % 2
            S.dma("sp", xr[s][:, 3:], xrT_s[c, :, :], ("xr", s), writes=[("xr", s)])
            S.dma("sp", yr[s], yrT_s[c, :, :], ("yr", s), writes=[("yr", s)])
            S.dma("pool", Wa[s], w_rg_a[c, :, :], ("Wg", s), writes=[("Wa", s)])
            S.dma("pool", Wx[s], w_rg_x[c, :, :], ("Wg", s), writes=[("Wx", s)])

        loadc(0)
        for c in range(16):
            s = c % 2
            if c + 1 < 16:
                loadc(c + 1)
            for sg in range(4):
                u = seg_i[0] % 2
                seg_i[0] += 1
                s0 = sg * SEG
                X = xr[s]
                S.op("act", lambda e, X=X, u=u, s0=s0, c=c: e.activation(out=xc[u], in_=X[:, 3 + s0:3 + s0 + SEG], func=AF.Identity, bias=brc[:, c:c + 1], scale=wrc[:, c, 3:4]),
                     reads=[("xr", s), "s3c", "s3c1"], writes=[("xc", u)])
                for k in range(3):
                    S.op("dve", lambda e, X=X, u=u, s0=s0, c=c, k=k: e.scalar_tensor_tensor(out=xc[u], in0=X[:, k + s0:k + s0 + SEG], scalar=wrc[:, c, k:k + 1], in1=xc[u], op0=ALU.mult, op1=ALU.add),
                         reads=[("xr", s), ("xc", u), "s3c"], writes=[("xc", u)])
                S.op("pool", lambda e, u=u: e.tensor_copy(out=xcb[u], in_=xc[u]), reads=[("xc", u)], writes=[("xcb", u)])
                for half in range(2):
                    ba, bx = (half * 2) % 4, (half * 2 + 1) % 4
                    sl = slice(half * 512, (half + 1) * 512)
                    S.op("pe", lambda e, ba=ba, u=u, sl=sl, s=s: e.matmul(bank(ba)[:, :512], Wa[s], xcb[u][:, sl], start=True, stop=True),
                         reads=[("Wa", s), ("xcb", u)], writes=[("ps", ba)])
                    S.op("pe", lambda e, bx=bx, u=u, sl=sl, s=s: e.matmul(bank(bx)[:, :512], Wx[s], xcb[u][:, sl], start=True, stop=True),
                         reads=[("Wx", s), ("xcb", u)], writes=[("ps", bx)])
                    S.op("act", lambda e, ba=ba, u=u, sl=sl, c=c: e.activation(out=rr[u][:, sl], in_=bank(ba)[:, :512], func=AF.Sigmoid, bias=bga[:, c:c + 1], scale=1.0),
                         reads=[("ps", ba), "s3c2"], writes=[("rr", u, half)])
                    S.op("act", lambda e, bx=bx, u=u, sl=sl, c=c: e.activation(out=ii[u][:, sl], in_=bank(bx)[:, :512], func=AF.Sigmoid, bias=bgx[:, c:c + 1], scale=1.0),
                         reads=[("ps", bx), "s3c3"], writes=[("ii", u, half)])
                S.op("act", lambda e, u=u, c=c: e.activation(out=aa[u], in_=rr[u], func=AF.Exp, scale=c8[:, c:c + 1]),
                     reads=[("rr", u, 0), ("rr", u, 1), "c8"], writes=[("aa", u)])
                S.op("act", lambda e, u=u, c=c: e.activation(out=mm[u], in_=rr[u], func=AF.Exp, scale=c16[:, c:c + 1]),
                     reads=[("rr", u, 0), ("rr", u, 1), "c16"], writes=[("mm", u)])
                S.op("act", lambda e, u=u: e.activation(out=mm[u], in_=mm[u], func=AF.Sqrt, bias=onecol, scale=-1.0),
                     reads=[("mm", u), "onecol"], writes=[("mm", u)])
                S.op("pool", lambda e, u=u: e.tensor_tensor(out=ii[u], in0=ii[u], in1=xc[u], op=ALU.mult),
                     reads=[("ii", u, 0), ("ii", u, 1), ("xc", u)], writes=[("ii", u, 0), ("ii", u, 1)])
                S.op("pool", lambda e, u=u, s0=s0: e.tensor_tensor(out=ii[u], in0=ii[u], in1=tv[:, s0:s0 + SEG], op=ALU.mult),
                     reads=[("ii", u, 0), ("ii", u, 1), "tv"], writes=[("ii", u, 0), ("ii", u, 1)])
                S.op("dve", lambda e, u=u: e.tensor_tensor(out=mm[u], in0=mm[u], in1=ii[u], op=ALU.mult),
                     reads=[("mm", u), ("ii", u, 0), ("ii", u, 1)], writes=[("mm", u)])
                H = hb[s]
                if sg == 0:
                    S.op("dve", lambda e, u=u, H=H: e.tensor_tensor_scan(out=H[:, 0:SEG], data0=aa[u], data1=mm[u], initial=0.0, op0=ALU.mult, op1=ALU.add),
                         reads=[("aa", u), ("mm", u)], writes=[("hb", s)])
                else:
                    S.op("dve", lambda e, u=u, H=H, s0=s0: e.tensor_tensor_scan(out=H[:, s0:s0 + SEG], data0=aa[u], data1=mm[u], initial=H[:, s0 - 1:s0], op0=ALU.mult, op1=ALU.add),
                         reads=[("aa", u), ("mm", u), ("hb", s)], writes=[("hb", s)])
            S.op("act", lambda e, s=s: e.activation(out=gy[s], in_=yr[s], func=AF.Gelu_apprx_tanh), reads=[("yr", s)], writes=[("gy", s)])
            S.op("dve", lambda e, s=s: e.tensor_tensor(out=gy[s], in0=gy[s], in1=hb[s][:, OH0:SW], op=ALU.mult),
                 reads=[("gy", s), ("hb", s)], writes=[("gy", s)])
            S.dma("sp", recT_s[c, :, :], gy[s], ("gy", s), reads=[("gy", s)], writes=[("rec", c)])

    s3()
    S.barrier()
    A.release(0)

    def s4():
        ga = A.alloc(16, F32)
        gr = A.alloc(16, F32)
        S.dma("sp", ga, g_att_d[:, :], "s4g", writes=["ga"])
        S.dma("sp", gr, g_rec_d[:, :], "s4g", writes=["gr"])
        blk = [A.alloc(16 * NQ, F32).rearrange("p (c t) -> p c t", c=16) for _ in range(2)]
        sq = [A.alloc(16 * NQ, BF16).rearrange("p (c t) -> p c t", c=16) for _ in range(2)]
        mx = [A.alloc(16 * NQ, BF16).rearrange("p (c t) -> p c t", c=16) for _ in range(2)]
        rs = [A.alloc(NQ, F32) for _ in range(2)]
        it = 0
        for grp_i, (src, gv, gk) in enumerate(((attT_s, ga, "ga"), (recT_s, gr, "gr"))):
            for tb in range(3):
                s = it % 2
                it += 1
                t0 = tb * NQ
                S.dma("sp", blk[s], src[:, :, t0:t0 + NQ].rearrange("c p t -> p c t"), ("blk", s), writes=[("blk", s)])
                S.op("act", lambda e, s=s: e.activation(out=sq[s], in_=blk[s], func=AF.Square), reads=[("blk", s)], writes=[("sq", s)])
                b = s
                for c in range(16):
                    S.op("pe", lambda e, s=s, c=c, b=b: e.matmul(bank(b)[:, :NQ], onesb, sq[s][:, c, :], start=(c == 0), stop=(c == 15)),
                         reads=["onesb", ("sq", s)], writes=[("ps", b)], signal=(c == 15))
                S.op("act", lambda e, s=s, b=b: e.activation(out=rs[s], in_=bank(b)[:, :NQ], func=AF.Sqrt, bias=epscol, scale=1.0 / 2048.0),
                     reads=[("ps", b), "epscol"], writes=[("rs", s)])
                S.op("dve", lambda e, s=s: e.reciprocal(out=rs[s], in_=rs[s]), reads=[("rs", s)], writes=[("rs", s)])
                for c in range(16):
                    S.op("dve", lambda e, s=s, c=c, gv=gv: e.scalar_tensor_tensor(out=mx[s][:, c, :], in0=blk[s][:, c, :], scalar=gv[:, c:c + 1], in1=rs[s], op0=ALU.mult, op1=ALU.mult),
                         reads=[("blk", s), ("rs", s), gk], writes=[("mx", s)])
                S.dma("sp", mixT_s[grp_i * 16:(grp_i + 1) * 16, :, t0:t0 + NQ].rearrange("c p t -> p c t"), mx[s], ("mx", s),
                      reads=[("mx", s)], writes=[("mix", grp_i, tb)])

    s4()
    S.barrier()
    A.release(0)

    def s5():
        mixT = A.alloc(32 * OH, BF16).rearrange("p (c t) -> p c t", c=32)
        for k in range(0, 32, 8):
            S.dma("sp", mixT[:, k:k + 8, :], mixT_s[k:k + 8, :, :].rearrange("c p t -> p c t"), "mixT", writes=["mixT"])
        wsl = [A.alloc(32 * 512, BF16).rearrange("p (c t) -> p c t", c=32) for _ in range(2)]
        xin = [A.alloc(512, F32) for _ in range(3)]
        load_w(wsl[0], ("wsl", 0), ("wsl", 0), w_out, 0, 32, 0, 512)
        it = 0
        for cg in range(8):
            if cg + 1 < 8:
                load_w(wsl[(cg + 1) % 2], ("wsl", (cg + 1) % 2), ("wsl", (cg + 1) % 2), w_out, 0, 32, (cg + 1) * 512, 512)
            W = wsl[cg % 2]
            for tt in range(9):
                k = it % 3
                b = it % 4
                it += 1
                S.dma("sp", xin[k], xw[OH0 + tt * 128:OH0 + (tt + 1) * 128, cg * 512:(cg + 1) * 512], ("xin", k), writes=[("xin", k)])
                for dc in range(32):
                    S.op("pe", lambda e, b=b, W=W, dc=dc, tt=tt: e.matmul(bank(b)[:, :512], mixT[:, dc, tt * 128:(tt + 1) * 128], W[:, dc, :], start=(dc == 0), stop=(dc == 31)),
                         reads=["mixT", ("wsl", cg % 2)], writes=[("ps", b)], signal=(dc == 31))
                S.op("dve", lambda e, b=b, k=k: e.tensor_tensor(out=xin[k], in0=bank(b)[:, :512], in1=xin[k], op=ALU.add),
                     reads=[("ps", b), ("xin", k)], writes=[("xin", k)])
                S.dma("sp", h1_s[tt * 128:(tt + 1) * 128, cg * 512:(cg + 1) * 512], xin[k], ("xin", k), reads=[("xin", k)], writes=[("h1", tt, cg)])

    s5()
    S.barrier()
    A.release(0)

    def ffn():
        mT = A.alloc(32 * OH, BF16).rearrange("p (c t) -> p c t", c=32)
        m0 = A.mark()
        norm_T(9, lambda tt: h1_s[tt * 128:(tt + 1) * 128, :], g_ffn_d, dst_res=mT, tag="s6")
        S.barrier()
        A.release(m0)
        wfc = A.alloc(192 * 3, F32).rearrange("p (c k) -> p c k", c=192)
        bfc = A.alloc(192, F32)
        S.dma("sp", wfc, w_ffconv_d[:, :, :], "fc", writes=["wfc"])
        S.dma("sp", bfc, b_ffconv_d[:, :], "fc", writes=["bfc"])
        NP, HP = 8, 12
        wg = [A.alloc(32 * 128, BF16).rearrange("p (c t) -> p c t", c=32) for _ in range(2)]
        wu = [A.alloc(32 * 128, BF16).rearrange("p (c t) -> p c t", c=32) for _ in range(2)]
        hidT = A.alloc(HP * OWN, BF16).rearrange("p (c t) -> p c t", c=HP)
        wd = [A.alloc(HP * 512, BF16).rearrange("p (c t) -> p c t", c=HP) for _ in range(2)]
        ug = [A.alloc(2 + OH, F32) for _ in range(2)]
        uu = [A.alloc(2 + OH, F32) for _ in range(2)]
        gc = A.alloc(OWN, F32)
        uc = A.alloc(OWN, F32)
        gl = A.alloc(OWN, F32)
        hin = [A.alloc(512, F32) for _ in range(4)]

        def load_up(hc):
            s = hc % 2
            load_w(wg[s], ("wg", s), ("wg", s), w_up, 0, 32, hc * 128, 128)
            load_w(wu[s], ("wu", s), ("wu", s), w_up, 0, 32, 12288 + hc * 128, 128)

        def load_dn(part, cg):
            s = (part * 8 + cg) % 2
            load_w(wd[s], ("wd", s), ("wd", s), w_down, part * HP, HP, cg * 512, 512)

        pbk = [0]
        hit = [0]
        load_up(0)
        for part in range(NP):
            for hcl in range(HP):
                hc = part * HP + hcl
                s = hc % 2
                if hc + 1 < 96:
                    load_up(hc + 1)
                for tb, (tb0, tbn) in enumerate(((96, 32), (128, 512), (640, 512))):
                    bg = (pbk[0] * 2) % 4
                    bu = bg + 1
                    pbk[0] += 1
                    sl = slice(tb0, tb0 + tbn)
                    for dc in range(32):
                        S.op("pe", lambda e, bg=bg, s=s, dc=dc, sl=sl, tbn=tbn: e.matmul(bank(bg)[:, :tbn], wg[s][:, dc, :], mT[:, dc, sl], start=(dc == 0), stop=(dc == 31)),
                             reads=[("wg", s)], writes=[("ps", bg)], signal=(dc == 31))
                    for dc in range(32):
                        S.op("pe", lambda e, bu=bu, s=s, dc=dc, sl=sl, tbn=tbn: e.matmul(bank(bu)[:, :tbn], wu[s][:, dc, :], mT[:, dc, sl], start=(dc == 0), stop=(dc == 31)),
                             reads=[("wu", s)], writes=[("ps", bu)], signal=(dc == 31))
                    S.op("act", lambda e, bg=bg, s=s, tb0=tb0, tbn=tbn: e.activation(out=ug[s][:, 2 + tb0:2 + tb0 + tbn], in_=bank(bg)[:, :tbn], func=AF.Copy),
                         reads=[("ps", bg)], writes=[("ug", s, tb)])
                    S.op("dve", lambda e, bu=bu, s=s, tb0=tb0, tbn=tbn: e.tensor_copy(out=uu[s][:, 2 + tb0:2 + tb0 + tbn], in_=bank(bu)[:, :tbn]),
                         reads=[("ps", bu)], writes=[("uu", s, tb)])
                ugk = [("ug", s, t) for t in range(3)]
                uuk = [("uu", s, t) for t in range(3)]
                fg, fu = hc, 96 + hc
                S.op("act", lambda e, s=s, fg=fg: e.activation(out=gc, in_=ug[s][:, 130:130 + OWN], func=AF.Identity, bias=bfc[:, fg:fg + 1], scale=wfc[:, fg, 2:3]),
                     reads=ugk + ["wfc", "bfc"], writes=["gc"])
                for k in range(2):
                    S.op("dve", lambda e, s=s, fg=fg, k=k: e.scalar_tensor_tensor(out=gc, in0=ug[s][:, 128 + k:128 + k + OWN], scalar=wfc[:, fg, k:k + 1], in1=gc, op0=ALU.mult, op1=ALU.add),
                         reads=ugk + ["gc", "wfc"], writes=["gc"])
                S.op("act", lambda e, s=s, fu=fu: e.activation(out=uc, in_=uu[s][:, 130:130 + OWN], func=AF.Identity, bias=bfc[:, fu:fu + 1], scale=wfc[:, fu, 2:3]),
                     reads=uuk + ["wfc", "bfc"], writes=["uc"])
                for k in range(2):
                    S.op("dve", lambda e, s=s, fu=fu, k=k: e.scalar_tensor_tensor(out=uc, in0=uu[s][:, 128 + k:128 + k + OWN], scalar=wfc[:, fu, k:k + 1], in1=uc, op0=ALU.mult, op1=ALU.add),
                         reads=uuk + ["uc", "wfc"], writes=["uc"])
                S.op("act", lambda e: e.activation(out=gl, in_=gc, func=AF.Gelu_apprx_tanh), reads=["gc"], writes=["gl"])
                S.op("dve", lambda e, hcl=hcl: e.tensor_tensor(out=hidT[:, hcl, :], in0=gl, in1=uc, op=ALU.mult),
                     reads=["gl", "uc"], writes=["hidT"])
            load_dn(part, 0)
            for cg in range(8):
                if cg + 1 < 8:
                    load_dn(part, cg + 1)
                sW = (part * 8 + cg) % 2
                for tt in range(8):
                    k = hit[0] % 4
                    hit[0] += 1
                    b = 4 + k
                    if part == 0:
                        src = h1_s[128 + tt * 128:128 + (tt + 1) * 128, cg * 512:(cg + 1) * 512]
                        rk = []
                    else:
                        src = hS_s[tt * 128:(tt + 1) * 128, cg * 512:(cg + 1) * 512]
                        rk = [("hS", tt, cg)]
                    S.dma("sp", hin[k], src, ("hin", k), reads=rk, writes=[("hin", k)])
                    for hcl in range(HP):
                        S.op("pe", lambda e, b=b, hcl=hcl, tt=tt, sW=sW: e.matmul(bank(b)[:, :512], hidT[:, hcl, tt * 128:(tt + 1) * 128], wd[sW][:, hcl, :], start=(hcl == 0), stop=(hcl == HP - 1)),
                             reads=["hidT", ("wd", sW)], writes=[("ps", b)], signal=(hcl == HP - 1))
                    S.op("dve", lambda e, b=b, k=k: e.tensor_tensor(out=hin[k], in0=bank(b)[:, :512], in1=hin[k], op=ALU.add),
                         reads=[("ps", b), ("hin", k)], writes=[("hin", k)])
                    S.dma("sp", hS_s[tt * 128:(tt + 1) * 128, cg * 512:(cg + 1) * 512], hin[k], ("hin", k), reads=[("hin", k)], writes=[("hS", tt, cg)])

    ffn()
    S.barrier()
    A.release(0)

    def ple():
        nT = A.alloc(32 * OWN, BF16).rearrange("p (c t) -> p c t", c=32)
        m0 = A.mark()
        norm_T(8, lambda tt: hS_s[tt * 128:(tt + 1) * 128, :], g_ple_d, dst_res=nT, tag="s9")
        S.barrier()
        A.release(m0)
        pT = A.alloc(2 * OWN, BF16).rearrange("p (c t) -> p c t", c=2)
        pin = [A.alloc(256, F32) for _ in range(2)]
        pbf = [A.alloc(256, BF16) for _ in range(2)]
        for tt in range(8):
            s = tt % 2
            S.dma("sp", pin[s], p_own[tt * 128:(tt + 1) * 128, :], ("pin", s), writes=[("pin", s)])
            S.op("dve", lambda e, s=s: e.tensor_copy(out=pbf[s], in_=pin[s]), reads=[("pin", s)], writes=[("pbf", s)])
            b = s
            pb = bank(b).bitcast(BF16)[:, 0:256].rearrange("p (c t) -> p c t", c=2)
            for c in range(2):
                S.op("pe", lambda e, pb=pb, c=c, s=s: e.transpose(out=pb[:, c, :], in_=pbf[s][:, c * 128:(c + 1) * 128], identity=identb),
                     reads=[("pbf", s), "identb"], writes=[("ps", b)], signal=(c == 1))
            S.op("dve", lambda e, pb=pb, tt=tt: e.tensor_copy(out=pT[:, :, tt * 128:(tt + 1) * 128], in_=pb), reads=[("ps", b)], writes=["pT"])
        wsl = [A.alloc(32 * 512, BF16).rearrange("p (c t) -> p c t", c=32) for _ in range(2)]
        wp = [A.alloc(2 * 512, BF16).rearrange("p (c t) -> p c t", c=2) for _ in range(2)]
        sg = [A.alloc(512, F32) for _ in range(2)]
        hin = [A.alloc(512, F32) for _ in range(3)]

        def loadw(cg):
            s = cg % 2
            load_w(wsl[s], ("wsl", s), ("wsl", s), w_ple_gate, 0, 32, cg * 512, 512)
            load_w(wp[s], ("wp", s), ("wp", s), w_ple, 0, 2, cg * 512, 512)

        loadw(0)
        it = 0
        for cg in range(8):
            if cg + 1 < 8:
                loadw(cg + 1)
            s = cg % 2
            for tt in range(8):
                k = it % 3
                u = it % 2
                bg = (it % 2) * 2
                bp = bg + 1
                it += 1
                S.dma("sp", hin[k], hS_s[tt * 128:(tt + 1) * 128, cg * 512:(cg + 1) * 512], ("hin", k), writes=[("hin", k)])
                for dc in range(32):
                    S.op("pe", lambda e, bg=bg, dc=dc, tt=tt, s=s: e.matmul(bank(bg)[:, :512], nT[:, dc, tt * 128:(tt + 1) * 128], wsl[s][:, dc, :], start=(dc == 0), stop=(dc == 31)),
                         reads=[("wsl", s)], writes=[("ps", bg)], signal=(dc == 31))
                for c in range(2):
                    S.op("pe", lambda e, bp=bp, c=c, tt=tt, s=s: e.matmul(bank(bp)[:, :512], pT[:, c, tt * 128:(tt + 1) * 128], wp[s][:, c, :], start=(c == 0), stop=(c == 1)),
                         reads=["pT", ("wp", s)], writes=[("ps", bp)], signal=(c == 1))
                S.op("act", lambda e, bg=bg, u=u: e.activation(out=sg[u], in_=bank(bg)[:, :512], func=AF.Sigmoid), reads=[("ps", bg)], writes=[("sg", u)])
                S.op("dve", lambda e, bp=bp, u=u: e.tensor_tensor(out=sg[u], in0=bank(bp)[:, :512], in1=sg[u], op=ALU.mult),
                     reads=[("ps", bp), ("sg", u)], writes=[("sg", u)])
                S.op("dve", lambda e, u=u, k=k: e.tensor_tensor(out=hin[k], in0=sg[u], in1=hin[k], op=ALU.add),
                     reads=[("sg", u), ("hin", k)], writes=[("hin", k)])
                S.dma("sp", h1_s[tt * 128:(tt + 1) * 128, cg * 512:(cg + 1) * 512], hin[k], ("hin", k), reads=[("hin", k)], writes=[("h3", tt, cg)])

    ple()
    S.barrier()
    A.release(0)

    def final():
        gf = A.alloc(D, F32)
        S.dma("sp", gf, g_final_d[:, :], "gf", writes=["gf"])
        xt = [A.alloc(D, F32) for _ in range(2)]
        junk = A.alloc(D, BF16)
        ot = [A.alloc(D, F32) for _ in range(2)]
        stat = A.alloc(8, F32)
        for tt in range(8):
            s = tt % 2
            S.dma("sp", xt[s], h1_s[tt * 128:(tt + 1) * 128, :], ("fx", s), writes=[("fx", s)])
            ms, sd, rs = stat[:, 3 * s:3 * s + 1], stat[:, 3 * s + 1:3 * s + 2], stat[:, 3 * s + 2:3 * s + 3]
            S.op("act", lambda e, s=s, ms=ms: e.activation(out=junk, in_=xt[s], func=AF.Square, scale=1.0 / 64.0, accum_out=ms),
                 reads=[("fx", s)], writes=["fjunk", ("fms", s)])
            S.op("act", lambda e, ms=ms, sd=sd: e.activation(out=sd, in_=ms, func=AF.Sqrt, bias=epscol, scale=1.0),
                 reads=[("fms", s), "epscol"], writes=[("fsd", s)])
            S.op("dve", lambda e, sd=sd, rs=rs: e.reciprocal(out=rs, in_=sd), reads=[("fsd", s)], writes=[("frs", s)])
            S.op("dve", lambda e, s=s, rs=rs: e.scalar_tensor_tensor(out=ot[s], in0=xt[s], scalar=rs, in1=gf, op0=ALU.mult, op1=ALU.mult),
                 reads=[("fx", s), ("frs", s), "gf"], writes=[("fo", s)])
            S.dma("sp", out_d[tt * 128:(tt + 1) * 128, :], ot[s], ("fo", s), reads=[("fo", s)], writes=[("out", tt)])

    final()
    S.barrier()

    keys = S.semkeys()
    sems = {}
    for i, k in enumerate(keys):
        sems[k] = stack.enter_context(nc.semaphore("s%d" % i))
    with nc.Block() as block:
        @block.tensor
        def _(e):
            S.emit("pe", e, sems)

        @block.scalar
        def _(e):
            S.emit("act", e, sems)

        @block.vector
        def _(e):
            S.emit("dve", e, sems)

        @block.gpsimd
        def _(e):
            S.emit("pool", e, sems)

        @block.sync
        def _(e):
            S.emit("sp", e, sems)
    stack.close()
    return nc


def _pc(v, n):
    v = np.asarray(v, np.float32).reshape(n // 128, 128)
    return np.ascontiguousarray(v.T)


def kernel(_debug=False, **inp):
    x = np.asarray(inp["x"], np.float32)
    p = np.asarray(inp["p"], np.float32)
    ident = np.eye(128, dtype=np.float32)
    jj, ss = np.meshgrid(np.arange(128), np.arange(128), indexing="ij")
    negtri = np.where(jj >= ss, -1.0, 0.0).astype(np.float32)
    negones = -np.ones((128, 128), np.float32)
    cmask = np.zeros((128, 3, NQ), np.float32)
    sp_ = np.arange(128)[:, None]
    tq = np.arange(NQ)[None, :]
    for mi in range(3):
        Dd = -128 * mi
        cmask[:, mi, :] = np.where(sp_ - tq < Dd, 0.0, NEG)
    common = {
        "ident": ident, "negtri": negtri, "negones": negones, "cmask": cmask,
        "g_mix": _pc(inp["g_mix"][0], 4096),
        "w_in": np.ascontiguousarray(inp["w_in"][0], np.float32),
        "w_rconv": np.ascontiguousarray(np.asarray(inp["w_rconv"][0], np.float32).reshape(4, 16, 128).transpose(2, 1, 0)),
        "b_rconv": _pc(inp["b_rconv"][0], 2048),
        "w_rg_a": np.ascontiguousarray(inp["w_rg_a"][0], np.float32),
        "b_rg_a": _pc(inp["b_rg_a"][0], 2048),
        "w_rg_x": np.ascontiguousarray(inp["w_rg_x"][0], np.float32),
        "b_rg_x": _pc(inp["b_rg_x"][0], 2048),
        "lam": _pc(inp["lam"][0], 2048),
        "g_att_out": _pc(inp["g_att_out"][0], 2048),
        "g_rec_out": _pc(inp["g_rec_out"][0], 2048),
        "w_out": np.ascontiguousarray(inp["w_out"][0], np.float32),
        "g_ffn": _pc(inp["g_ffn"][0], 4096),
        "w_up": np.ascontiguousarray(inp["w_up"][0], np.float32),
        "w_ffconv": np.ascontiguousarray(np.asarray(inp["w_ffconv"][0], np.float32).reshape(3, 192, 128).transpose(2, 1, 0)),
        "b_ffconv": _pc(inp["b_ffconv"][0], 24576),
        "w_down": np.ascontiguousarray(inp["w_down"][0], np.float32),
        "g_ple": _pc(inp["g_ple"][0], 4096),
        "w_ple": np.ascontiguousarray(inp["w_ple"][0], np.float32),
        "w_ple_gate": np.ascontiguousarray(inp["w_ple_gate"][0], np.float32),
        "g_final": np.ascontiguousarray(np.broadcast_to(np.asarray(inp["g_final"], np.float32)[None, :], (128, D))),
    }
    in_maps = []
    for c in range(8):
        b, j = c // 4, c % 4
        end = 1024 * (j + 1)
        P = SW - end
        xwin = np.zeros((SW, D), np.float32)
        xwin[P:] = x[b, :end]
        kb = np.zeros((128, 32), np.float32)
        kb[:, : P // 128] = NEG
        tv = np.ones((128, SW), np.float32)
        tv[:, :P] = 0.0
        m = dict(common)
        m.update({"xw": xwin, "p_own": np.ascontiguousarray(p[0, b, end - 1024:end]), "kbias": kb, "tokvalid": tv})
        in_maps.append(m)
    nc = build_program(debug=_debug)
    res = run_bass_kernel_spmd(nc, in_maps, core_ids=list(range(8)))
    if _debug:
        return res
    out = np.zeros((2, 4096, D), np.float32)
    for c in range(8):
        b, j = c // 4, c % 4
        out[b, 1024 * j:1024 * (j + 1)] = res.results[c]["out"]
    return out
```

```python
import numpy as np
from contextlib import ExitStack
import concourse.bass as bass
import concourse.mybir as mybir
from concourse.bass_utils import run_bass_kernel_spmd

F32, BF16 = mybir.dt.float32, mybir.dt.bfloat16
AF = mybir.ActivationFunctionType
ALU = mybir.AluOpType

D = 4096
SW = 4096
OH = 1152
OH0 = SW - OH
OWN = 1024
NQ = 384
EPS = 1e-6
NEG = -30000.0
ENG = ["pe", "act", "dve", "pool", "sp"]


class Sched:
    def __init__(self):
        self.ops = {e: [] for e in ENG}
        self.cnt = {e: 0 for e in ENG}
        self.dcnt = {}
        self.waited = {e: {} for e in ENG}
        self.buf = {}

    def _need(self, eng, tok, waits):
        kind, name, val = tok
        if kind == "e" and name == eng:
            return
        if kind == "d":
            val = 16 * self.dcnt[name]
        key = (kind, name)
        if self.waited[eng].get(key, 0) >= val:
            return
        self.waited[eng][key] = val
        waits.append((key, val))

    def _deps(self, eng, reads, writes):
        waits = []
        for k in reads:
            st = self.buf.get(k)
            if st and st["w"] is not None:
                self._need(eng, st["w"], waits)
        for k in writes:
            st = self.buf.get(k)
            if st:
                if st["w"] is not None:
                    self._need(eng, st["w"], waits)
                for t in st["r"].values():
                    self._need(eng, t, waits)
        return waits

    def _commit(self, tok, reads, writes):
        for k in reads:
            st = self.buf.setdefault(k, {"w": None, "r": {}})
            st["r"][(tok[0], tok[1])] = tok
        for k in writes:
            self.buf[k] = {"w": tok, "r": {}}

    def op(self, eng, fn, reads=(), writes=(), signal=True):
        waits = self._deps(eng, reads, writes)
        if signal:
            self.cnt[eng] += 1
            tok = ("e", eng, self.cnt[eng])
        else:
            tok = ("e", eng, self.cnt[eng] + 1)
        self._commit(tok, reads, writes)
        self.ops[eng].append((waits, fn, ("e", eng) if signal else None))

    def dma(self, q, out, in_, semkey, reads=(), writes=()):
        waits = self._deps(q, reads, writes)
        self.dcnt[semkey] = self.dcnt.get(semkey, 0) + 1
        tok = ("d", semkey, 16 * self.dcnt[semkey])
        self._commit(tok, reads, writes)
        self.ops[q].append((waits, (lambda e, o=out, i=in_: e.dma_start(out=o, in_=i)), ("d", semkey)))

    def barrier(self):
        for e in ENG:
            waits = []
            for o in ENG:
                if o != e and self.cnt[o] > 0:
                    self._need(e, ("e", o, self.cnt[o]), waits)
            for k in self.dcnt:
                self._need(e, ("d", k, 0), waits)
            self.ops[e].append((waits, None, None))
        self.buf = {}

    def semkeys(self):
        return [("e", e) for e in ENG] + [("d", k) for k in self.dcnt]

    def emit(self, name, e, sems):
        for waits, fn, inc in self.ops[name]:
            for key, val in waits:
                e.wait_ge(sems[key], val)
            if fn is None:
                continue
            ins = fn(e)
            if inc is not None:
                ins.then_inc(sems[inc], 1 if inc[0] == "e" else 16)


class Arena:
    def __init__(self, ap, nbytes):
        self.ap, self.n, self.off = ap, nbytes, 0

    def alloc(self, nelem, dt):
        sz = 4 if dt == F32 else 2
        nb = (nelem * sz + 63) // 64 * 64
        assert self.off + nb <= self.n, ("arena overflow", self.off, nb, self.n)
        a = self.ap[:, self.off // 4:(self.off + nb) // 4]
        self.off += nb
        if dt != F32:
            a = a.bitcast(dt)
        return a[:, :nelem]

    def mark(self):
        return self.off

    def release(self, m):
        self.off = m


def build_program(debug=False):
    nc = bass.Bass("TRN2", target_bir_lowering=False)
    S = Sched()

    def din(name, shape):
        return nc.dram_tensor(name, list(shape), F32, kind="ExternalInput").ap()

    xw = din("xw", [SW, D])
    p_own = din("p_own", [OWN, 256])
    kbias_d = din("kbias", [128, 32])
    tokvalid_d = din("tokvalid", [128, SW])
    ident_d = din("ident", [128, 128])
    negtri_d = din("negtri", [128, 128])
    negones_d = din("negones", [128, 128])
    cmask_d = din("cmask", [128, 3, NQ])
    g_mix_d = din("g_mix", [128, 32])
    w_in = din("w_in", [D, 10240])
    w_rconv_d = din("w_rconv", [128, 16, 4])
    b_rconv_d = din("b_rconv", [128, 16])
    w_rg_a = din("w_rg_a", [16, 128, 128])
    b_rg_a_d = din("b_rg_a", [128, 16])
    w_rg_x = din("w_rg_x", [16, 128, 128])
    b_rg_x_d = din("b_rg_x", [128, 16])
    lam_d = din("lam", [128, 16])
    g_att_d = din("g_att_out", [128, 16])
    g_rec_d = din("g_rec_out", [128, 16])
    w_out = din("w_out", [D, D])
    g_ffn_d = din("g_ffn", [128, 32])
    w_up = din("w_up", [D, 24576])
    w_ffconv_d = din("w_ffconv", [128, 192, 3])
    b_ffconv_d = din("b_ffconv", [128, 192])
    w_down = din("w_down", [12288, D])
    g_ple_d = din("g_ple", [128, 32])
    w_ple = din("w_ple", [256, D])
    w_ple_gate = din("w_ple_gate", [D, D])
    g_final_d = din("g_final", [128, D])
    out_d = nc.dram_tensor("out", [OWN, D], F32, kind="ExternalOutput").ap()

    def dscr(name, shape, dt):
        if debug:
            return nc.dram_tensor(name, list(shape), dt, kind="ExternalOutput").ap()
        return nc.dram_tensor(name, list(shape), dt).ap()

    aT_s = dscr("aT_s", [32, 128, SW], BF16)
    kT_s = dscr("kT_s", [16, 128, SW], BF16)
    v_s = dscr("v_s", [SW, 2048], BF16)
    qT_s = dscr("qT_s", [16, 128, OH], BF16)
    xrT_s = dscr("xrT_s", [16, 128, SW], F32)
    yrT_s = dscr("yrT_s", [16, 128, OH], F32)
    attT_s = dscr("attT_s", [16, 128, OH], F32)
    recT_s = dscr("recT_s", [16, 128, OH], F32)
    mixT_s = dscr("mixT_s", [32, 128, OH], BF16)
    h1_s = dscr("h1_s", [OH, D], F32)
    hS_s = dscr("hS_s", [OWN, D], F32)

    stack = ExitStack()
    ARENA_BYTES = 204 * 1024
    arena_t = stack.enter_context(nc.sbuf_tensor("arena", [128, ARENA_BYTES // 4], F32))
    cst_t = stack.enter_context(nc.sbuf_tensor("cst", [128, 512], F32))
    ps_t = stack.enter_context(nc.psum_tensor("ps", [128, 8, 512], F32))
    A = Arena(arena_t[:, :], ARENA_BYTES)
    C = Arena(cst_t[:, :], 512 * 4)

    def bank(b):
        return ps_t[:, b, :]

    identb = C.alloc(128, BF16)
    negtri = C.alloc(128, BF16)
    negones = C.alloc(128, BF16)
    onesb = C.alloc(128, BF16)
    onecol = C.alloc(1, F32)
    epscol = C.alloc(1, F32)
    kbias = C.alloc(32, F32)
    S.dma("pool", identb, ident_d[:, :], "c0", writes=["identb"])
    S.dma("pool", negtri, negtri_d[:, :], "c0", writes=["negtri"])
    S.dma("pool", negones, negones_d[:, :], "c0", writes=["negones"])
    S.dma("sp", kbias, kbias_d[:, :], "c1", writes=["kbias"])
    S.op("dve", lambda e: e.memset(onesb, 1.0), writes=["onesb"])
    S.op("dve", lambda e: e.memset(onecol, 1.0), writes=["onecol"])
    S.op("dve", lambda e: e.memset(epscol, EPS), writes=["epscol"])
    S.barrier()

    evac_flip = [0]

    def evac_copy(out, in_, reads, writes, scale=None):
        evac_flip[0] ^= 1
        if evac_flip[0] or scale is not None:
            if scale is None:
                S.op("act", lambda e: e.activation(out=out, in_=in_, func=AF.Copy), reads=reads, writes=writes)
            else:
                S.op("act", lambda e: e.activation(out=out, in_=in_, func=AF.Copy, scale=scale), reads=reads, writes=writes)
        else:
            S.op("dve", lambda e: e.tensor_copy(out=out, in_=in_), reads=reads, writes=writes)

    def load_w(slot_ap, slot_key, semkey, w_ap, kc0, nkc, c0, ncols):
        wv = w_ap.rearrange("(c p) f -> p c f", p=128)
        step = 8
        for k in range(0, nkc, step):
            n = min(step, nkc - k)
            S.dma("pool", slot_ap[:, k:k + n, :], wv[:, kc0 + k:kc0 + k + n, c0:c0 + ncols], semkey, writes=[slot_key])

    def norm_T(ntiles, src_fn, g_dram, dst_res=None, dst_scr=None, tag="n"):
        gsb = A.alloc(32, F32)
        S.dma("sp", gsb, g_dram[:, :], tag + "g", writes=[tag + "g"])
        xt = [A.alloc(D, F32) for _ in range(2)]
        junk = A.alloc(D, BF16)
        xs = [A.alloc(D, BF16) for _ in range(2)]
        stat = A.alloc(8, F32)
        aTt = None
        if dst_res is None:
            aTt = [A.alloc(32 * 512, BF16).rearrange("p (c t) -> p c t", c=32) for _ in range(2)]
        for tt in range(ntiles):
            s = tt % 2
            S.dma("sp", xt[s], src_fn(tt), (tag + "x", s), writes=[(tag + "xt", s)])
            ms, sd, rs = stat[:, 3 * s:3 * s + 1], stat[:, 3 * s + 1:3 * s + 2], stat[:, 3 * s + 2:3 * s + 3]
            S.op("act", lambda e, s=s, ms=ms: e.activation(out=junk, in_=xt[s], func=AF.Square, scale=1.0 / 64.0, accum_out=ms),
                 reads=[(tag + "xt", s)], writes=[tag + "junk", (tag + "ms", s)])
            S.op("act", lambda e, ms=ms, sd=sd: e.activation(out=sd, in_=ms, func=AF.Sqrt, bias=epscol, scale=1.0),
                 reads=[(tag + "ms", s), "epscol"], writes=[(tag + "sd", s)])
            S.op("dve", lambda e, sd=sd, rs=rs: e.reciprocal(out=rs, in_=sd), reads=[(tag + "sd", s)], writes=[(tag + "rs", s)])
            S.op("act", lambda e, s=s, rs=rs: e.activation(out=xs[s], in_=xt[s], func=AF.Copy, scale=rs),
                 reads=[(tag + "xt", s), (tag + "rs", s)], writes=[(tag + "xs", s)])
            if dst_res is not None:
                dst = dst_res[:, :, tt * 128:(tt + 1) * 128]
                dkey = (tag + "res", tt)
            else:
                gs = (tt // 4) % 2
                dst = aTt[gs][:, :, (tt % 4) * 128:(tt % 4 + 1) * 128]
                dkey = (tag + "aTt", gs)
            for g4 in range(4):
                b = (tt * 4 + g4) % 2
                pb = bank(b).bitcast(BF16)[:, 0:1024].rearrange("p (c t) -> p c t", c=8)
                for i in range(8):
                    dc = g4 * 8 + i
                    S.op("pe", lambda e, pb=pb, i=i, dc=dc, s=s: e.transpose(out=pb[:, i, :], in_=xs[s][:, dc * 128:(dc + 1) * 128], identity=identb),
                         reads=[(tag + "xs", s), "identb"], writes=[("ps", b)], signal=(i == 7))
                gb = gsb[:, g4 * 8:g4 * 8 + 8].unsqueeze(2).broadcast_to([128, 8, 128])
                S.op("dve", lambda e, pb=pb, dst=dst, g4=g4, gb=gb: e.tensor_tensor(out=dst[:, g4 * 8:g4 * 8 + 8, :], in0=pb, in1=gb, op=ALU.mult),
                     reads=[("ps", b), tag + "g"], writes=[dkey])
            if dst_res is None and tt % 4 == 3:
                t0 = (tt // 4) * 512
                for k8 in range(0, 32, 8):
                    S.dma("sp", dst_scr[k8:k8 + 8, :, t0:t0 + 512].rearrange("c p t -> p c t"), aTt[gs][:, k8:k8 + 8, :], (tag + "st", gs),
                          reads=[dkey], writes=[(tag + "scr", tt, k8)])

    norm_T(32, lambda tt: xw[tt * 128:(tt + 1) * 128, :], g_mix_d, dst_scr=aT_s, tag="s0")
    S.barrier()
    A.release(0)

    def s1():
        at = [A.alloc(32 * 512, BF16).rearrange("p (c t) -> p c t", c=32) for _ in range(2)]
        wsl = [A.alloc(32 * 512, BF16).rearrange("p (c t) -> p c t", c=32) for _ in range(2)]
        stg_f = [A.alloc(512, F32) for _ in range(4)]
        stg_b = [A.alloc(512, BF16) for _ in range(4)]
        allblk = [(i * 512, 512) for i in range(8)]
        ownblk = [(OH0 + i * NQ, NQ) for i in range(3)]
        groups = []
        for g in range(4):
            groups.append(("k", 2048 + g * 512, allblk, "fm"))
        for g in range(4):
            groups.append(("v", 4096 + g * 512, allblk, "tm"))
        for g in range(4):
            groups.append(("xr", 6144 + g * 512, allblk, "fm"))
        for g in range(4):
            groups.append(("q", g * 512, ownblk, "fm"))
        for g in range(4):
            groups.append(("yr", 8192 + g * 512, ownblk, "fm"))
        work = [(gi, bi) for gi, gr in enumerate(groups) for bi in range(len(gr[2]))]
        nat = [0]
        nst = [0]
        pbk = [0]

        def load_at(idx):
            gi, bi = work[idx]
            t0, nt = groups[gi][2][bi]
            s = idx % 2
            for k in range(0, 32, 8):
                S.dma("sp", at[s][:, k:k + 8, :nt], aT_s[k:k + 8, :, t0:t0 + nt].rearrange("c p t -> p c t"), ("at", s), writes=[("at", s)])

        def load_wg(gi):
            load_w(wsl[gi % 2], ("wsl", gi % 2), ("wsl", gi % 2), w_in, 0, 32, groups[gi][1], 512)

        load_wg(0)
        load_at(0)
        for idx, (gi, bi) in enumerate(work):
            name, c0, blks, lay = groups[gi]
            t0, nt = blks[bi]
            s = idx % 2
            if bi == 0 and gi + 1 < len(groups):
                load_wg(gi + 1)
            if idx + 1 < len(work):
                load_at(idx + 1)
            W = wsl[gi % 2]
            g = (c0 % 2048) // 512
            for sub in range(4):
                b = pbk[0] % 4
                pbk[0] += 1
                if lay == "fm":
                    for dc in range(32):
                        S.op("pe", lambda e, b=b, W=W, sub=sub, dc=dc, s=s, nt=nt: e.matmul(bank(b)[:, :nt], W[:, dc, sub * 128:(sub + 1) * 128], at[s][:, dc, :nt], start=(dc == 0), stop=(dc == 31)),
                             reads=[("wsl", gi % 2), ("at", s)], writes=[("ps", b)], signal=(dc == 31))
                    fc = g * 4 + sub
                    k = nst[0] % 4
                    nst[0] += 1
                    if name in ("k", "q"):
                        stg = stg_b[k]
                        sc = float(1.0 / np.sqrt(128.0)) if name == "q" else None
                        evac_copy(stg[:, :nt], bank(b)[:, :nt], [("ps", b)], [("stg", k)], scale=sc)
                        if name == "k":
                            dst = kT_s[fc, :, t0:t0 + nt]
                        else:
                            dst = qT_s[fc, :, t0 - OH0:t0 - OH0 + nt]
                    else:
                        stg = stg_f[k]
                        evac_copy(stg[:, :nt], bank(b)[:, :nt], [("ps", b)], [("stg", k)])
                        if name == "xr":
                            dst = xrT_s[fc, :, t0:t0 + nt]
                        else:
                            dst = yrT_s[fc, :, t0 - OH0:t0 - OH0 + nt]
                    S.dma("sp", dst, stg[:, :nt], ("stg", k), reads=[("stg", k)], writes=[("s1o", name, fc, t0)])
                else:
                    for dc in range(32):
                        S.op("pe", lambda e, b=b, W=W, sub=sub, dc=dc, s=s: e.matmul(bank(b)[:, :512], at[s][:, dc, sub * 128:(sub + 1) * 128], W[:, dc, :], start=(dc == 0), stop=(dc == 31)),
                             reads=[("wsl", gi % 2), ("at", s)], writes=[("ps", b)], signal=(dc == 31))
                    k = nst[0] % 4
                    nst[0] += 1
                    evac_copy(stg_b[k], bank(b)[:, :512], [("ps", b)], [("stg", k)])
                    S.dma("sp", v_s[t0 + sub * 128:t0 + (sub + 1) * 128, g * 512:(g + 1) * 512], stg_b[k], ("stg", k),
                          reads=[("stg", k)], writes=[("s1o", "v", g, t0, sub)])

    s1()
    S.barrier()
    A.release(0)

    def s2():
        cm = A.alloc(3 * NQ, BF16).rearrange("p (m t) -> p m t", m=3)
        S.dma("pool", cm, cmask_d[:, :, :], "cm", writes=["cm"])
        kT = [A.alloc(SW, BF16) for _ in range(2)]
        vh = [A.alloc(32 * 128, BF16).rearrange("p (k d) -> p k d", k=32) for _ in range(2)]
        qT = [A.alloc(OH, BF16) for _ in range(2)]
        eb = [A.alloc(NQ, F32) for _ in range(2)]
        Lh = [A.alloc(NQ, BF16) for _ in range(3)]
        ab = [A.alloc(NQ, BF16) for _ in range(3)]
        Rb = [A.alloc(NQ, BF16) for _ in range(3)]
        ost = [A.alloc(NQ, F32) for _ in range(2)]

        def load_head(h):
            s = h % 2
            S.dma("sp", kT[s], kT_s[h, :, :], ("kT", s), writes=[("kT", s)])
            S.dma("sp", vh[s], v_s[:, h * 128:(h + 1) * 128].rearrange("(k p) d -> p k d", p=128), ("vh", s), writes=[("vh", s)])
            S.dma("sp", qT[s], qT_s[h, :, :], ("qT", s), writes=[("qT", s)])

        pairs = []
        for h in range(16):
            for qb in range(3):
                t0 = OH0 + qb * NQ
                kmax = (t0 + NQ - 2) // 128
                for kb in range(kmax, -1, -1):
                    Dd = t0 - 128 * kb
                    mi = (-Dd // 128) if Dd in (0, -128, -256) else None
                    pairs.append(dict(h=h, qb=qb, kb=kb, mi=mi, first=(kb == kmax), last=(kb == 0)))
        n = len(pairs)
        gidx = [0] * n
        c = 0
        for i, p in enumerate(pairs):
            if p["first"]:
                c = 0
            gidx[i] = c
            c += 1

        def zmm(i, b):
            p = pairs[i]
            s = p["h"] % 2
            kb, qo = p["kb"], p["qb"] * NQ
            return s, kb, qo

        def stageA(i):
            p = pairs[i]
            if i == 0:
                load_head(0)
            if p["qb"] == 0 and gidx[i] == 4 and p["h"] + 1 < 16:
                load_head(p["h"] + 1)
            s, kb, qo = zmm(i, 0)
            b = i % 2
            part = p["mi"] is not None
            S.op("pe", lambda e: e.matmul(bank(b)[:, :NQ], kT[s][:, kb * 128:(kb + 1) * 128], qT[s][:, qo:qo + NQ], start=True, stop=not part),
                 reads=[("kT", s), ("qT", s)], writes=[("ps", b)], signal=not part)
            if part:
                mi = p["mi"]
                S.op("pe", lambda e: e.matmul(bank(b)[:, :NQ], identb, cm[:, mi, :], start=False, stop=True),
                     reads=["identb", "cm"], writes=[("ps", b)], signal=True)
            S.op("act", lambda e: e.activation(out=eb[b], in_=bank(b)[:, :NQ], func=AF.Exp, bias=kbias[:, kb:kb + 1], scale=1.0),
                 reads=[("ps", b), "kbias"], writes=[("eb", b)])
            l = i % 3
            S.op("act", lambda e: e.activation(out=Lh[l], in_=eb[b], func=AF.Ln, bias=onecol, scale=1.0),
                 reads=[("eb", b), "onecol"], writes=[("Lh", l)])
            if not p["last"]:
                rn = (i + 1) % 3
                ro = i % 3
                if p["first"]:
                    S.op("dve", lambda e: e.tensor_copy(out=Rb[rn], in_=Lh[l]), reads=[("Lh", l)], writes=[("Rb", rn)])
                else:
                    S.op("dve", lambda e: e.tensor_tensor(out=Rb[rn], in0=Rb[ro], in1=Lh[l], op=ALU.add),
                         reads=[("Lh", l), ("Rb", ro)], writes=[("Rb", rn)])

        def stageB(i):
            p = pairs[i]
            s, kb, qo = zmm(i, 0)
            b = 2 + i % 2
            l = i % 3
            gi = gidx[i]
            part = p["mi"] is not None
            S.op("pe", lambda e: e.matmul(bank(b)[:, :NQ], kT[s][:, kb * 128:(kb + 1) * 128], qT[s][:, qo:qo + NQ], start=True, stop=False),
                 reads=[("kT", s), ("qT", s)], writes=[("ps", b)], signal=False)
            if part:
                mi = p["mi"]
                S.op("pe", lambda e: e.matmul(bank(b)[:, :NQ], identb, cm[:, mi, :], start=False, stop=False),
                     reads=["identb", "cm"], writes=[("ps", b)], signal=False)
            S.op("pe", lambda e: e.matmul(bank(b)[:, :NQ], negtri, Lh[l], start=False, stop=p["first"]),
                 reads=["negtri", ("Lh", l)], writes=[("ps", b)], signal=p["first"])
            if not p["first"]:
                S.op("pe", lambda e: e.matmul(bank(b)[:, :NQ], negones, Rb[i % 3], start=False, stop=True),
                     reads=["negones", ("Rb", i % 3)], writes=[("ps", b)], signal=True)
            S.op("act", lambda e: e.activation(out=ab[l], in_=bank(b)[:, :NQ], func=AF.Exp, bias=kbias[:, kb:kb + 1], scale=1.0),
                 reads=[("ps", b), "kbias"], writes=[("ab", l)])

        grp = [0]

        def stageO(i):
            p = pairs[i]
            s, kb, qo = zmm(i, 0)
            l = i % 3
            if p["first"]:
                grp[0] += 1
            b = 4 + grp[0] % 2
            S.op("pe", lambda e: e.matmul(bank(b)[:, :NQ], vh[s][:, kb, :], ab[l], start=p["first"], stop=p["last"]),
                 reads=[("vh", s), ("ab", l)], writes=[("ps", b)], signal=p["last"])
            if p["last"]:
                k = grp[0] % 2
                evac_copy(ost[k], bank(b)[:, :NQ], [("ps", b)], [("ost", k)])
                S.dma("sp", attT_s[p["h"], :, qo:qo + NQ], ost[k], ("ost", k), reads=[("ost", k)], writes=[("att", p["h"], qo)])

        for st in range(n + 2):
            if st < n:
                stageA(st)
            if 0 <= st - 1 < n:
                stageB(st - 1)
            if 0 <= st - 2 < n:
                stageO(st - 2)

    s2()
    S.barrier()
    A.release(0)

    def s3():
        SEG = 1024
        tv = A.alloc(SW, F32)
        S.dma("sp", tv, tokvalid_d[:, :], "tv", writes=["tv"])
        wrc = A.alloc(64, F32).rearrange("p (c k) -> p c k", c=16)
        brc = A.alloc(16, F32)
        bga = A.alloc(16, F32)
        bgx = A.alloc(16, F32)
        lam = A.alloc(16, F32)
        c8 = A.alloc(16, F32)
        c16 = A.alloc(16, F32)
        S.dma("sp", wrc, w_rconv_d[:, :, :], "s3c", writes=["s3c"])
        S.dma("sp", brc, b_rconv_d[:, :], "s3c", writes=["s3c1"])
        S.dma("sp", bga, b_rg_a_d[:, :], "s3c", writes=["s3c2"])
        S.dma("sp", bgx, b_rg_x_d[:, :], "s3c", writes=["s3c3"])
        S.dma("sp", lam, lam_d[:, :], "s3c", writes=["s3c4"])
        S.op("act", lambda e: e.activation(out=c8, in_=lam, func=AF.Exp, scale=-1.0), reads=["s3c4"], writes=["c8"])
        S.op("act", lambda e: e.activation(out=c8, in_=c8, func=AF.Ln, bias=onecol, scale=1.0), reads=["c8", "onecol"], writes=["c8"])
        S.op("dve", lambda e: e.tensor_scalar_mul(out=c16, in0=c8, scalar1=-16.0), reads=["c8"], writes=["c16"])
        S.op("dve", lambda e: e.tensor_scalar_mul(out=c8, in0=c8, scalar1=-8.0), reads=["c8", "c16"], writes=["c8"])
        xr = [A.alloc(3 + SW, F32) for _ in range(2)]
        for s in range(2):
            S.op("dve", lambda e, s=s: e.memset(xr[s][:, 0:3], 0.0), writes=[("xr", s)])
        yr = [A.alloc(OH, F32) for _ in range(2)]
        Wa = [A.alloc(128, BF16) for _ in range(2)]
        Wx = [A.alloc(128, BF16) for _ in range(2)]
        hb = [A.alloc(SW, F32) for _ in range(2)]
        xc = [A.alloc(SEG, F32) for _ in range(2)]
        xcb = [A.alloc(SEG, BF16) for _ in range(2)]
        rr = [A.alloc(SEG, F32) for _ in range(2)]
        ii = [A.alloc(SEG, F32) for _ in range(2)]
        aa = [A.alloc(SEG, F32) for _ in range(2)]
        mm = [A.alloc(SEG, F32) for _ in range(2)]
        gy = [A.alloc(OH, F32) for _ in range(2)]
        seg_i = [0]

        def loadc(c):
            s = c % 2
            S.dma("sp", xr[s][:, 3:], xrT_s[c, :, :], ("xr", s), writes=[("xr", s)])
            S.dma("sp", yr[s], yrT_s[c, :, :], ("yr", s), writes=[("yr", s)])
            S.dma("pool", Wa[s], w_rg_a[c, :, :], ("Wg", s), writes=[("Wa", s)])
            S.dma("pool", Wx[s], w_rg_x[c, :, :], ("Wg", s), writes=[("Wx", s)])

        def head(c, sg, u):
            s = c % 2
            s0 = sg * SEG
            X = xr[s]
            S.op("act", lambda e: e.activation(out=xc[u], in_=X[:, 3 + s0:3 + s0 + SEG], func=AF.Identity, bias=brc[:, c:c + 1], scale=wrc[:, c, 3:4]),
                 reads=[("xr", s), "s3c", "s3c1"], writes=[("xc", u)])
            for k in range(3):
                S.op("dve", lambda e, k=k: e.scalar_tensor_tensor(out=xc[u], in0=X[:, k + s0:k + s0 + SEG], scalar=wrc[:, c, k:k + 1], in1=xc[u], op0=ALU.mult, op1=ALU.add),
                     reads=[("xr", s), ("xc", u), "s3c"], writes=[("xc", u)])
            S.op("dve", lambda e: e.tensor_copy(out=xcb[u], in_=xc[u]), reads=[("xc", u)], writes=[("xcb", u)])
            for half in range(2):
                ba, bx = 6, 7
                sl = slice(half * 512, (half + 1) * 512)
                S.op("pe", lambda e, sl=sl: e.matmul(bank(ba)[:, :512], Wa[s], xcb[u][:, sl], start=True, stop=True),
                     reads=[("Wa", s), ("xcb", u)], writes=[("ps", ba)])
                S.op("pe", lambda e, sl=sl: e.matmul(bank(bx)[:, :512], Wx[s], xcb[u][:, sl], start=True, stop=True),
                     reads=[("Wx", s), ("xcb", u)], writes=[("ps", bx)])
                S.op("act", lambda e, sl=sl: e.activation(out=rr[u][:, sl], in_=bank(ba)[:, :512], func=AF.Sigmoid, bias=bga[:, c:c + 1], scale=1.0),
                     reads=[("ps", ba), "s3c2"], writes=[("rr", u, half)])
                S.op("act", lambda e, sl=sl: e.activation(out=ii[u][:, sl], in_=bank(bx)[:, :512], func=AF.Sigmoid, bias=bgx[:, c:c + 1], scale=1.0),
                     reads=[("ps", bx), "s3c3"], writes=[("ii", u, half)])

        def tail(c, sg, u):
            s = c % 2
            s0 = sg * SEG
            S.op("act", lambda e: e.activation(out=aa[u], in_=rr[u], func=AF.Exp, scale=c8[:, c:c + 1]),
                 reads=[("rr", u, 0), ("rr", u, 1), "c8"], writes=[("aa", u)])
            S.op("act", lambda e: e.activation(out=mm[u], in_=rr[u], func=AF.Exp, scale=c16[:, c:c + 1]),
                 reads=[("rr", u, 0), ("rr", u, 1), "c16"], writes=[("mm", u)])
            S.op("act", lambda e: e.activation(out=mm[u], in_=mm[u], func=AF.Sqrt, bias=onecol, scale=-1.0),
                 reads=[("mm", u), "onecol"], writes=[("mm", u)])
            S.op("dve", lambda e: e.tensor_tensor(out=ii[u], in0=ii[u], in1=xc[u], op=ALU.mult),
                 reads=[("ii", u, 0), ("ii", u, 1), ("xc", u)], writes=[("ii", u, 0), ("ii", u, 1)])
            S.op("pool", lambda e: e.tensor_tensor(out=ii[u], in0=ii[u], in1=tv[:, s0:s0 + SEG], op=ALU.mult),
                 reads=[("ii", u, 0), ("ii", u, 1), "tv"], writes=[("ii", u, 0), ("ii", u, 1)])
            S.op("dve", lambda e: e.tensor_tensor(out=mm[u], in0=mm[u], in1=ii[u], op=ALU.mult),
                 reads=[("mm", u), ("ii", u, 0), ("ii", u, 1)], writes=[("mm", u)])
            H = hb[s]
            if sg == 0:
                S.op("dve", lambda e: e.tensor_tensor_scan(out=H[:, 0:SEG], data0=aa[u], data1=mm[u], initial=0.0, op0=ALU.mult, op1=ALU.add),
                     reads=[("aa", u), ("mm", u)], writes=[("hb", s)])
            else:
                S.op("dve", lambda e: e.tensor_tensor_scan(out=H[:, s0:s0 + SEG], data0=aa[u], data1=mm[u], initial=H[:, s0 - 1:s0], op0=ALU.mult, op1=ALU.add),
                     reads=[("aa", u), ("mm", u), ("hb", s)], writes=[("hb", s)])
            if sg == 3:
                S.op("act", lambda e: e.activation(out=gy[s], in_=yr[s], func=AF.Gelu_apprx_tanh), reads=[("yr", s)], writes=[("gy", s)])
                S.op("dve", lambda e: e.tensor_tensor(out=gy[s], in0=gy[s], in1=hb[s][:, OH0:SW], op=ALU.mult),
                     reads=[("gy", s), ("hb", s)], writes=[("gy", s)])
                S.dma("sp", recT_s[c, :, :], gy[s], ("gy", s), reads=[("gy", s)], writes=[("rec", c)])

        units = [(c, sg) for c in range(16) for sg in range(4)]
        loadc(0)
        for n_, (c, sg) in enumerate(units):
            head(c, sg, n_ % 2)
            if n_ >= 1:
                pc, psg = units[n_ - 1]
                tail(pc, psg, (n_ - 1) % 2)
            if sg == 0 and c + 1 < 16:
                loadc(c + 1)
        pc, psg = units[-1]
        tail(pc, psg, (len(units) - 1) % 2)

    s3()
    S.barrier()
    A.release(0)

    def s4():
        ga = A.alloc(16, F32)
        gr = A.alloc(16, F32)
        S.dma("sp", ga, g_att_d[:, :], "s4g", writes=["ga"])
        S.dma("sp", gr, g_rec_d[:, :], "s4g", writes=["gr"])
        blk = [A.alloc(16 * NQ, F32).rearrange("p (c t) -> p c t", c=16) for _ in range(2)]
        sq = [A.alloc(16 * NQ, BF16).rearrange("p (c t) -> p c t", c=16) for _ in range(2)]
        mx = [A.alloc(16 * NQ, BF16).rearrange("p (c t) -> p c t", c=16) for _ in range(2)]
        rs = [A.alloc(NQ, F32) for _ in range(2)]
        it = 0
        for grp_i, (src, gv, gk) in enumerate(((attT_s, ga, "ga"), (recT_s, gr, "gr"))):
            for tb in range(3):
                s = it % 2
                it += 1
                t0 = tb * NQ
                S.dma("sp", blk[s], src[:, :, t0:t0 + NQ].rearrange("c p t -> p c t"), ("blk", s), writes=[("blk", s)])
                S.op("act", lambda e, s=s: e.activation(out=sq[s], in_=blk[s], func=AF.Square), reads=[("blk", s)], writes=[("sq", s)])
                b = s
                for c in range(16):
                    S.op("pe", lambda e, s=s, c=c, b=b: e.matmul(bank(b)[:, :NQ], onesb, sq[s][:, c, :], start=(c == 0), stop=(c == 15)),
                         reads=["onesb", ("sq", s)], writes=[("ps", b)], signal=(c == 15))
                S.op("act", lambda e, s=s, b=b: e.activation(out=rs[s], in_=bank(b)[:, :NQ], func=AF.Sqrt, bias=epscol, scale=1.0 / 2048.0),
                     reads=[("ps", b), "epscol"], writes=[("rs", s)])
                S.op("dve", lambda e, s=s: e.reciprocal(out=rs[s], in_=rs[s]), reads=[("rs", s)], writes=[("rs", s)])
                for c in range(16):
                    S.op("dve", lambda e, s=s, c=c, gv=gv: e.scalar_tensor_tensor(out=mx[s][:, c, :], in0=blk[s][:, c, :], scalar=gv[:, c:c + 1], in1=rs[s], op0=ALU.mult, op1=ALU.mult),
                         reads=[("blk", s), ("rs", s), gk], writes=[("mx", s)])
                S.dma("sp", mixT_s[grp_i * 16:(grp_i + 1) * 16, :, t0:t0 + NQ].rearrange("c p t -> p c t"), mx[s], ("mx", s),
                      reads=[("mx", s)], writes=[("mix", grp_i, tb)])

    s4()
    S.barrier()
    A.release(0)

    def s5():
        mixT = A.alloc(32 * OH, BF16).rearrange("p (c t) -> p c t", c=32)
        for k in range(0, 32, 8):
            S.dma("sp", mixT[:, k:k + 8, :], mixT_s[k:k + 8, :, :].rearrange("c p t -> p c t"), "mixT", writes=["mixT"])
        wsl = [A.alloc(32 * 512, BF16).rearrange("p (c t) -> p c t", c=32) for _ in range(2)]
        xin = [A.alloc(512, F32) for _ in range(3)]
        load_w(wsl[0], ("wsl", 0), ("wsl", 0), w_out, 0, 32, 0, 512)
        it = 0
        for cg in range(8):
            if cg + 1 < 8:
                load_w(wsl[(cg + 1) % 2], ("wsl", (cg + 1) % 2), ("wsl", (cg + 1) % 2), w_out, 0, 32, (cg + 1) * 512, 512)
            W = wsl[cg % 2]
            for tt in range(9):
                k = it % 3
                b = it % 4
                it += 1
                S.dma("sp", xin[k], xw[OH0 + tt * 128:OH0 + (tt + 1) * 128, cg * 512:(cg + 1) * 512], ("xin", k), writes=[("xin", k)])
                for dc in range(32):
                    S.op("pe", lambda e, b=b, W=W, dc=dc, tt=tt: e.matmul(bank(b)[:, :512], mixT[:, dc, tt * 128:(tt + 1) * 128], W[:, dc, :], start=(dc == 0), stop=(dc == 31)),
                         reads=["mixT", ("wsl", cg % 2)], writes=[("ps", b)], signal=(dc == 31))
                S.op("dve", lambda e, b=b, k=k: e.tensor_tensor(out=xin[k], in0=bank(b)[:, :512], in1=xin[k], op=ALU.add),
                     reads=[("ps", b), ("xin", k)], writes=[("xin", k)])
                S.dma("sp", h1_s[tt * 128:(tt + 1) * 128, cg * 512:(cg + 1) * 512], xin[k], ("xin", k), reads=[("xin", k)], writes=[("h1", tt, cg)])

    s5()
    S.barrier()
    A.release(0)

    def ffn():
        mT = A.alloc(32 * OH, BF16).rearrange("p (c t) -> p c t", c=32)
        m0 = A.mark()
        norm_T(9, lambda tt: h1_s[tt * 128:(tt + 1) * 128, :], g_ffn_d, dst_res=mT, tag="s6")
        S.barrier()
        A.release(m0)
        wfc = A.alloc(192 * 3, F32).rearrange("p (c k) -> p c k", c=192)
        bfc = A.alloc(192, F32)
        S.dma("sp", wfc, w_ffconv_d[:, :, :], "fc", writes=["wfc"])
        S.dma("sp", bfc, b_ffconv_d[:, :], "fc", writes=["bfc"])
        NP, HP = 8, 12
        wg = [A.alloc(32 * 128, BF16).rearrange("p (c t) -> p c t", c=32) for _ in range(2)]
        wu = [A.alloc(32 * 128, BF16).rearrange("p (c t) -> p c t", c=32) for _ in range(2)]
        hidT = A.alloc(HP * OWN, BF16).rearrange("p (c t) -> p c t", c=HP)
        wd = [A.alloc(HP * 512, BF16).rearrange("p (c t) -> p c t", c=HP) for _ in range(2)]
        ug = [A.alloc(2 + OH, F32) for _ in range(2)]
        uu = [A.alloc(2 + OH, F32) for _ in range(2)]
        gc = A.alloc(OWN, F32)
        uc = A.alloc(OWN, F32)
        gl = A.alloc(OWN, F32)
        hin = [A.alloc(512, F32) for _ in range(8)]

        def load_up(hc):
            s = hc % 2
            load_w(wg[s], ("wg", s), ("wg", s), w_up, 0, 32, hc * 128, 128)
            load_w(wu[s], ("wu", s), ("wu", s), w_up, 0, 32, 12288 + hc * 128, 128)

        def load_dn(part, cg):
            s = (part * 8 + cg) % 2
            load_w(wd[s], ("wd", s), ("wd", s), w_down, part * HP, HP, cg * 512, 512)

        pbk = [0]
        hit = [0]
        load_up(0)
        for part in range(NP):
            for hcl in range(HP):
                hc = part * HP + hcl
                s = hc % 2
                if hc + 1 < 96:
                    load_up(hc + 1)
                for tb, (tb0, tbn) in enumerate(((96, 32), (128, 512), (640, 512))):
                    bg = (pbk[0] * 2) % 4
                    bu = bg + 1
                    pbk[0] += 1
                    sl = slice(tb0, tb0 + tbn)
                    for dc in range(32):
                        S.op("pe", lambda e, bg=bg, s=s, dc=dc, sl=sl, tbn=tbn: e.matmul(bank(bg)[:, :tbn], wg[s][:, dc, :], mT[:, dc, sl], start=(dc == 0), stop=(dc == 31)),
                             reads=[("wg", s)], writes=[("ps", bg)], signal=(dc == 31))
                    for dc in range(32):
                        S.op("pe", lambda e, bu=bu, s=s, dc=dc, sl=sl, tbn=tbn: e.matmul(bank(bu)[:, :tbn], wu[s][:, dc, :], mT[:, dc, sl], start=(dc == 0), stop=(dc == 31)),
                             reads=[("wu", s)], writes=[("ps", bu)], signal=(dc == 31))
                    S.op("act", lambda e, bg=bg, s=s, tb0=tb0, tbn=tbn: e.activation(out=ug[s][:, 2 + tb0:2 + tb0 + tbn], in_=bank(bg)[:, :tbn], func=AF.Copy),
                         reads=[("ps", bg)], writes=[("ug", s, tb)])
                    S.op("dve", lambda e, bu=bu, s=s, tb0=tb0, tbn=tbn: e.tensor_copy(out=uu[s][:, 2 + tb0:2 + tb0 + tbn], in_=bank(bu)[:, :tbn]),
                         reads=[("ps", bu)], writes=[("uu", s, tb)])
                ugk = [("ug", s, t) for t in range(3)]
                uuk = [("uu", s, t) for t in range(3)]
                fg, fu = hc, 96 + hc
                S.op("act", lambda e, s=s, fg=fg: e.activation(out=gc, in_=ug[s][:, 130:130 + OWN], func=AF.Identity, bias=bfc[:, fg:fg + 1], scale=wfc[:, fg, 2:3]),
                     reads=ugk + ["wfc", "bfc"], writes=["gc"])
                for k in range(2):
                    S.op("dve", lambda e, s=s, fg=fg, k=k: e.scalar_tensor_tensor(out=gc, in0=ug[s][:, 128 + k:128 + k + OWN], scalar=wfc[:, fg, k:k + 1], in1=gc, op0=ALU.mult, op1=ALU.add),
                         reads=ugk + ["gc", "wfc"], writes=["gc"])
                S.op("act", lambda e, s=s, fu=fu: e.activation(out=uc, in_=uu[s][:, 130:130 + OWN], func=AF.Identity, bias=bfc[:, fu:fu + 1], scale=wfc[:, fu, 2:3]),
                     reads=uuk + ["wfc", "bfc"], writes=["uc"])
                for k in range(2):
                    S.op("dve", lambda e, s=s, fu=fu, k=k: e.scalar_tensor_tensor(out=uc, in0=uu[s][:, 128 + k:128 + k + OWN], scalar=wfc[:, fu, k:k + 1], in1=uc, op0=ALU.mult, op1=ALU.add),
                         reads=uuk + ["uc", "wfc"], writes=["uc"])
                S.op("act", lambda e: e.activation(out=gl, in_=gc, func=AF.Gelu_apprx_tanh), reads=["gc"], writes=["gl"])
                S.op("dve", lambda e, hcl=hcl: e.tensor_tensor(out=hidT[:, hcl, :], in0=gl, in1=uc, op=ALU.mult),
                     reads=["gl", "uc"], writes=["hidT"])
            load_dn(part, 0)

            def load_hin(g):
                cg_, tt_ = g // 8, g % 8
                k_ = (hit[0] + (g - gcur[0])) % 8
                if part == 0:
                    src = h1_s[128 + tt_ * 128:128 + (tt_ + 1) * 128, cg_ * 512:(cg_ + 1) * 512]
                    rk = []
                else:
                    src = hS_s[tt_ * 128:(tt_ + 1) * 128, cg_ * 512:(cg_ + 1) * 512]
                    rk = [("hS", tt_, cg_)]
                S.dma("sp", hin[k_], src, ("hin", k_), reads=rk, writes=[("hin", k_)])

            LA = 4
            gcur = [0]
            for g in range(LA):
                load_hin(g)
            for cg in range(8):
                if cg + 1 < 8:
                    load_dn(part, cg + 1)
                sW = (part * 8 + cg) % 2
                for tt in range(8):
                    gcur[0] = cg * 8 + tt
                    if gcur[0] + LA < 64:
                        load_hin(gcur[0] + LA)
                    k = hit[0] % 8
                    hit[0] += 1
                    b = 4 + k % 4
                    for hcl in range(HP):
                        S.op("pe", lambda e, b=b, hcl=hcl, tt=tt, sW=sW: e.matmul(bank(b)[:, :512], hidT[:, hcl, tt * 128:(tt + 1) * 128], wd[sW][:, hcl, :], start=(hcl == 0), stop=(hcl == HP - 1)),
                             reads=["hidT", ("wd", sW)], writes=[("ps", b)], signal=(hcl == HP - 1))
                    S.op("dve", lambda e, b=b, k=k: e.tensor_tensor(out=hin[k], in0=bank(b)[:, :512], in1=hin[k], op=ALU.add),
                         reads=[("ps", b), ("hin", k)], writes=[("hin", k)])
                    S.dma("sp", hS_s[tt * 128:(tt + 1) * 128, cg * 512:(cg + 1) * 512], hin[k], ("hin", k), reads=[("hin", k)], writes=[("hS", tt, cg)])

    ffn()
    S.barrier()
    A.release(0)

    def ple():
        nT = A.alloc(32 * OWN, BF16).rearrange("p (c t) -> p c t", c=32)
        m0 = A.mark()
        norm_T(8, lambda tt: hS_s[tt * 128:(tt + 1) * 128, :], g_ple_d, dst_res=nT, tag="s9")
        S.barrier()
        A.release(m0)
        pT = A.alloc(2 * OWN, BF16).rearrange("p (c t) -> p c t", c=2)
        pin = [A.alloc(256, F32) for _ in range(2)]
        pbf = [A.alloc(256, BF16) for _ in range(2)]
        for tt in range(8):
            s = tt % 2
            S.dma("sp", pin[s], p_own[tt * 128:(tt + 1) * 128, :], ("pin", s), writes=[("pin", s)])
            S.op("dve", lambda e, s=s: e.tensor_copy(out=pbf[s], in_=pin[s]), reads=[("pin", s)], writes=[("pbf", s)])
            b = s
            pb = bank(b).bitcast(BF16)[:, 0:256].rearrange("p (c t) -> p c t", c=2)
            for c in range(2):
                S.op("pe", lambda e, pb=pb, c=c, s=s: e.transpose(out=pb[:, c, :], in_=pbf[s][:, c * 128:(c + 1) * 128], identity=identb),
                     reads=[("pbf", s), "identb"], writes=[("ps", b)], signal=(c == 1))
            S.op("dve", lambda e, pb=pb, tt=tt: e.tensor_copy(out=pT[:, :, tt * 128:(tt + 1) * 128], in_=pb), reads=[("ps", b)], writes=["pT"])
        wsl = [A.alloc(32 * 512, BF16).rearrange("p (c t) -> p c t", c=32) for _ in range(2)]
        wp = [A.alloc(2 * 512, BF16).rearrange("p (c t) -> p c t", c=2) for _ in range(2)]
        sg = [A.alloc(512, F32) for _ in range(2)]
        hin = [A.alloc(512, F32) for _ in range(3)]

        def loadw(cg):
            s = cg % 2
            load_w(wsl[s], ("wsl", s), ("wsl", s), w_ple_gate, 0, 32, cg * 512, 512)
            load_w(wp[s], ("wp", s), ("wp", s), w_ple, 0, 2, cg * 512, 512)

        loadw(0)
        it = 0
        for cg in range(8):
            if cg + 1 < 8:
                loadw(cg + 1)
            s = cg % 2
            for tt in range(8):
                k = it % 3
                u = it % 2
                bg = (it % 2) * 2
                bp = bg + 1
                it += 1
                S.dma("sp", hin[k], hS_s[tt * 128:(tt + 1) * 128, cg * 512:(cg + 1) * 512], ("hin", k), writes=[("hin", k)])
                for dc in range(32):
                    S.op("pe", lambda e, bg=bg, dc=dc, tt=tt, s=s: e.matmul(bank(bg)[:, :512], nT[:, dc, tt * 128:(tt + 1) * 128], wsl[s][:, dc, :], start=(dc == 0), stop=(dc == 31)),
                         reads=[("wsl", s)], writes=[("ps", bg)], signal=(dc == 31))
                for c in range(2):
                    S.op("pe", lambda e, bp=bp, c=c, tt=tt, s=s: e.matmul(bank(bp)[:, :512], pT[:, c, tt * 128:(tt + 1) * 128], wp[s][:, c, :], start=(c == 0), stop=(c == 1)),
                         reads=["pT", ("wp", s)], writes=[("ps", bp)], signal=(c == 1))
                S.op("act", lambda e, bg=bg, u=u: e.activation(out=sg[u], in_=bank(bg)[:, :512], func=AF.Sigmoid), reads=[("ps", bg)], writes=[("sg", u)])
                S.op("dve", lambda e, bp=bp, u=u: e.tensor_tensor(out=sg[u], in0=bank(bp)[:, :512], in1=sg[u], op=ALU.mult),
                     reads=[("ps", bp), ("sg", u)], writes=[("sg", u)])
                S.op("dve", lambda e, u=u, k=k: e.tensor_tensor(out=hin[k], in0=sg[u], in1=hin[k], op=ALU.add),
                     reads=[("sg", u), ("hin", k)], writes=[("hin", k)])
                S.dma("sp", h1_s[tt * 128:(tt + 1) * 128, cg * 512:(cg + 1) * 512], hin[k], ("hin", k), reads=[("hin", k)], writes=[("h3", tt, cg)])

    ple()
    S.barrier()
    A.release(0)

    def final():
        gf = A.alloc(D, F32)
        S.dma("sp", gf, g_final_d[:, :], "gf", writes=["gf"])
        xt = [A.alloc(D, F32) for _ in range(2)]
        junk = A.alloc(D, BF16)
        ot = [A.alloc(D, F32) for _ in range(2)]
        stat = A.alloc(8, F32)
        for tt in range(8):
            s = tt % 2
            S.dma("sp", xt[s], h1_s[tt * 128:(tt + 1) * 128, :], ("fx", s), writes=[("fx", s)])
            ms, sd, rs = stat[:, 3 * s:3 * s + 1], stat[:, 3 * s + 1:3 * s + 2], stat[:, 3 * s + 2:3 * s + 3]
            S.op("act", lambda e, s=s, ms=ms: e.activation(out=junk, in_=xt[s], func=AF.Square, scale=1.0 / 64.0, accum_out=ms),
                 reads=[("fx", s)], writes=["fjunk", ("fms", s)])
            S.op("act", lambda e, ms=ms, sd=sd: e.activation(out=sd, in_=ms, func=AF.Sqrt, bias=epscol, scale=1.0),
                 reads=[("fms", s), "epscol"], writes=[("fsd", s)])
            S.op("dve", lambda e, sd=sd, rs=rs: e.reciprocal(out=rs, in_=sd), reads=[("fsd", s)], writes=[("frs", s)])
            S.op("dve", lambda e, s=s, rs=rs: e.scalar_tensor_tensor(out=ot[s], in0=xt[s], scalar=rs, in1=gf, op0=ALU.mult, op1=ALU.mult),
                 reads=[("fx", s), ("frs", s), "gf"], writes=[("fo", s)])
            S.dma("sp", out_d[tt * 128:(tt + 1) * 128, :], ot[s], ("fo", s), reads=[("fo", s)], writes=[("out", tt)])

    final()
    S.barrier()

    keys = S.semkeys()
    sems = {}
    for i, k in enumerate(keys):
        sems[k] = stack.enter_context(nc.semaphore("s%d" % i))
    with nc.Block() as block:
        @block.tensor
        def _(e):
            S.emit("pe", e, sems)

        @block.scalar
        def _(e):
            S.emit("act", e, sems)

        @block.vector
        def _(e):
            S.emit("dve", e, sems)

        @block.gpsimd
        def _(e):
            S.emit("pool", e, sems)

        @block.sync
        def _(e):
            S.emit("sp", e, sems)
    stack.close()
    return nc


def _pc(v, n):
    v = np.asarray(v, np.float32).reshape(n // 128, 128)
    return np.ascontiguousarray(v.T)


def kernel(_debug=False, **inp):
    x = np.asarray(inp["x"], np.float32)
    p = np.asarray(inp["p"], np.float32)
    ident = np.eye(128, dtype=np.float32)
    jj, ss = np.meshgrid(np.arange(128), np.arange(128), indexing="ij")
    negtri = np.where(jj >= ss, -1.0, 0.0).astype(np.float32)
    negones = -np.ones((128, 128), np.float32)
    cmask = np.zeros((128, 3, NQ), np.float32)
    sp_ = np.arange(128)[:, None]
    tq = np.arange(NQ)[None, :]
    for mi in range(3):
        Dd = -128 * mi
        cmask[:, mi, :] = np.where(sp_ - tq < Dd, 0.0, NEG)
    common = {
        "ident": ident, "negtri": negtri, "negones": negones, "cmask": cmask,
        "g_mix": _pc(inp["g_mix"][0], 4096),
        "w_in": np.ascontiguousarray(inp["w_in"][0], np.float32),
        "w_rconv": np.ascontiguousarray(np.asarray(inp["w_rconv"][0], np.float32).reshape(4, 16, 128).transpose(2, 1, 0)),
        "b_rconv": _pc(inp["b_rconv"][0], 2048),
        "w_rg_a": np.ascontiguousarray(inp["w_rg_a"][0], np.float32),
        "b_rg_a": _pc(inp["b_rg_a"][0], 2048),
        "w_rg_x": np.ascontiguousarray(inp["w_rg_x"][0], np.float32),
        "b_rg_x": _pc(inp["b_rg_x"][0], 2048),
        "lam": _pc(inp["lam"][0], 2048),
        "g_att_out": _pc(inp["g_att_out"][0], 2048),
        "g_rec_out": _pc(inp["g_rec_out"][0], 2048),
        "w_out": np.ascontiguousarray(inp["w_out"][0], np.float32),
        "g_ffn": _pc(inp["g_ffn"][0], 4096),
        "w_up": np.ascontiguousarray(inp["w_up"][0], np.float32),
        "w_ffconv": np.ascontiguousarray(np.asarray(inp["w_ffconv"][0], np.float32).reshape(3, 192, 128).transpose(2, 1, 0)),
        "b_ffconv": _pc(inp["b_ffconv"][0], 24576),
        "w_down": np.ascontiguousarray(inp["w_down"][0], np.float32),
        "g_ple": _pc(inp["g_ple"][0], 4096),
        "w_ple": np.ascontiguousarray(inp["w_ple"][0], np.float32),
        "w_ple_gate": np.ascontiguousarray(inp["w_ple_gate"][0], np.float32),
        "g_final": np.ascontiguousarray(np.broadcast_to(np.asarray(inp["g_final"], np.float32)[None, :], (128, D))),
    }
    in_maps = []
    for c in range(8):
        b, j = c // 4, c % 4
        end = 1024 * (j + 1)
        P = SW - end
        xwin = np.zeros((SW, D), np.float32)
        xwin[P:] = x[b, :end]
        kb = np.zeros((128, 32), np.float32)
        kb[:, : P // 128] = NEG
        tv = np.ones((128, SW), np.float32)
        tv[:, :P] = 0.0
        m = dict(common)
        m.update({"xw": xwin, "p_own": np.ascontiguousarray(p[0, b, end - 1024:end]), "kbias": kb, "tokvalid": tv})
        in_maps.append(m)
    nc = build_program(debug=_debug)
    res = run_bass_kernel_spmd(nc, in_maps, core_ids=list(range(8)))
    if _debug:
        return res
    out = np.zeros((2, 4096, D), np.float32)
    for c in range(8):
        b, j = c // 4, c % 4
        out[b, 1024 * j:1024 * (j + 1)] = res.results[c]["out"]
    return out
```

```python
import numpy as np
from contextlib import ExitStack
import concourse.bass as bass
import concourse.mybir as mybir
from concourse.bass_utils import run_bass_kernel_spmd

F32, BF16 = mybir.dt.float32, mybir.dt.bfloat16
AF = mybir.ActivationFunctionType
ALU = mybir.AluOpType

D = 4096
SW = 4096
OH = 1152
OH0 = SW - OH
OWN = 1024
NQ = 384
EPS = 1e-6
NEG = -30000.0
ENG = ["pe", "act", "dve", "pool", "sp"]


class Sched:
    def __init__(self):
        self.ops = {e: [] for e in ENG}
        self.cnt = {e: 0 for e in ENG}
        self.dcnt = {}
        self.waited = {e: {} for e in ENG}
        self.buf = {}

    def _need(self, eng, tok, waits):
        kind, name, val = tok
        if kind == "e" and name == eng:
            return
        if kind == "d":
            val = 16 * self.dcnt[name]
        key = (kind, name)
        if self.waited[eng].get(key, 0) >= val:
            return
        self.waited[eng][key] = val
        waits.append((key, val))

    def _deps(self, eng, reads, writes):
        waits = []
        for k in reads:
            st = self.buf.get(k)
            if st and st["w"] is not None:
                self._need(eng, st["w"], waits)
        for k in writes:
            st = self.buf.get(k)
            if st:
                if st["w"] is not None:
                    self._need(eng, st["w"], waits)
                for t in st["r"].values():
                    self._need(eng, t, waits)
        return waits

    def _commit(self, tok, reads, writes):
        for k in reads:
            st = self.buf.setdefault(k, {"w": None, "r": {}})
            st["r"][(tok[0], tok[1])] = tok
        for k in writes:
            self.buf[k] = {"w": tok, "r": {}}

    def op(self, eng, fn, reads=(), writes=(), signal=True):
        waits = self._deps(eng, reads, writes)
        if signal:
            self.cnt[eng] += 1
            tok = ("e", eng, self.cnt[eng])
        else:
            tok = ("e", eng, self.cnt[eng] + 1)
        self._commit(tok, reads, writes)
        self.ops[eng].append((waits, fn, ("e", eng) if signal else None))

    def dma(self, q, out, in_, semkey, reads=(), writes=()):
        waits = self._deps(q, reads, writes)
        self.dcnt[semkey] = self.dcnt.get(semkey, 0) + 1
        tok = ("d", semkey, 16 * self.dcnt[semkey])
        self._commit(tok, reads, writes)
        self.ops[q].append((waits, (lambda e, o=out, i=in_: e.dma_start(out=o, in_=i)), ("d", semkey)))

    def barrier(self):
        for e in ENG:
            waits = []
            for o in ENG:
                if o != e and self.cnt[o] > 0:
                    self._need(e, ("e", o, self.cnt[o]), waits)
            for k in self.dcnt:
                self._need(e, ("d", k, 0), waits)
            self.ops[e].append((waits, None, None))
        self.buf = {}

    def semkeys(self):
        return [("e", e) for e in ENG] + [("d", k) for k in self.dcnt]

    def emit(self, name, e, sems):
        for waits, fn, inc in self.ops[name]:
            for key, val in waits:
                e.wait_ge(sems[key], val)
            if fn is None:
                continue
            ins = fn(e)
            if inc is not None:
                ins.then_inc(sems[inc], 1 if inc[0] == "e" else 16)


class Arena:
    def __init__(self, ap, nbytes):
        self.ap, self.n, self.off = ap, nbytes, 0

    def alloc(self, nelem, dt):
        sz = 4 if dt == F32 else 2
        nb = (nelem * sz + 63) // 64 * 64
        assert self.off + nb <= self.n, ("arena overflow", self.off, nb, self.n)
        a = self.ap[:, self.off // 4:(self.off + nb) // 4]
        self.off += nb
        if dt != F32:
            a = a.bitcast(dt)
        return a[:, :nelem]

    def mark(self):
        return self.off

    def release(self, m):
        self.off = m


def build_program(debug=False):
    nc = bass.Bass("TRN2", target_bir_lowering=False)
    S = Sched()

    def din(name, shape):
        return nc.dram_tensor(name, list(shape), F32, kind="ExternalInput").ap()

    xw = din("xw", [SW, D])
    p_own = din("p_own", [OWN, 256])
    kbias_d = din("kbias", [128, 32])
    tokvalid_d = din("tokvalid", [128, SW])
    ident_d = din("ident", [128, 128])
    negtri_d = din("negtri", [128, 128])
    negones_d = din("negones", [128, 128])
    cmask_d = din("cmask", [128, 3, NQ])
    g_mix_d = din("g_mix", [128, 32])
    w_in = din("w_in", [D, 10240])
    w_rconv_d = din("w_rconv", [128, 16, 4])
    b_rconv_d = din("b_rconv", [128, 16])
    w_rg_a = din("w_rg_a", [16, 128, 128])
    b_rg_a_d = din("b_rg_a", [128, 16])
    w_rg_x = din("w_rg_x", [16, 128, 128])
    b_rg_x_d = din("b_rg_x", [128, 16])
    lam_d = din("lam", [128, 16])
    g_att_d = din("g_att_out", [128, 16])
    g_rec_d = din("g_rec_out", [128, 16])
    w_out = din("w_out", [D, D])
    g_ffn_d = din("g_ffn", [128, 32])
    w_up = din("w_up", [D, 24576])
    w_ffconv_d = din("w_ffconv", [128, 192, 3])
    b_ffconv_d = din("b_ffconv", [128, 192])
    w_down = din("w_down", [12288, D])
    g_ple_d = din("g_ple", [128, 32])
    w_ple = din("w_ple", [256, D])
    w_ple_gate = din("w_ple_gate", [D, D])
    g_final_d = din("g_final", [128, D])
    out_d = nc.dram_tensor("out", [OWN, D], F32, kind="ExternalOutput").ap()

    def dscr(name, shape, dt):
        if debug:
            return nc.dram_tensor(name, list(shape), dt, kind="ExternalOutput").ap()
        return nc.dram_tensor(name, list(shape), dt).ap()

    aT_s = dscr("aT_s", [32, 128, SW], BF16)
    kT_s = dscr("kT_s", [16, 128, SW], BF16)
    v_s = dscr("v_s", [SW, 2048], BF16)
    qT_s = dscr("qT_s", [16, 128, OH], BF16)
    xrT_s = dscr("xrT_s", [16, 128, SW], F32)
    yrT_s = dscr("yrT_s", [16, 128, OH], F32)
    attT_s = dscr("attT_s", [16, 128, OH], F32)
    recT_s = dscr("recT_s", [16, 128, OH], F32)
    mixT_s = dscr("mixT_s", [32, 128, OH], BF16)
    h1_s = dscr("h1_s", [OH, D], F32)
    hS_s = dscr("hS_s", [OWN, D], F32)

    stack = ExitStack()
    ARENA_BYTES = 204 * 1024
    arena_t = stack.enter_context(nc.sbuf_tensor("arena", [128, ARENA_BYTES // 4], F32))
    cst_t = stack.enter_context(nc.sbuf_tensor("cst", [128, 512], F32))
    ps_t = stack.enter_context(nc.psum_tensor("ps", [128, 8, 512], F32))
    A = Arena(arena_t[:, :], ARENA_BYTES)
    C = Arena(cst_t[:, :], 512 * 4)

    def bank(b):
        return ps_t[:, b, :]

    identb = C.alloc(128, BF16)
    negtri = C.alloc(128, BF16)
    negones = C.alloc(128, BF16)
    onesb = C.alloc(128, BF16)
    onecol = C.alloc(1, F32)
    epscol = C.alloc(1, F32)
    kbias = C.alloc(32, F32)
    S.dma("pool", identb, ident_d[:, :], "c0", writes=["identb"])
    S.dma("pool", negtri, negtri_d[:, :], "c0", writes=["negtri"])
    S.dma("pool", negones, negones_d[:, :], "c0", writes=["negones"])
    S.dma("sp", kbias, kbias_d[:, :], "c1", writes=["kbias"])
    S.op("dve", lambda e: e.memset(onesb, 1.0), writes=["onesb"])
    S.op("dve", lambda e: e.memset(onecol, 1.0), writes=["onecol"])
    S.op("dve", lambda e: e.memset(epscol, EPS), writes=["epscol"])
    S.barrier()

    evac_flip = [0]

    def evac_copy(out, in_, reads, writes, scale=None):
        evac_flip[0] ^= 1
        if evac_flip[0] or scale is not None:
            if scale is None:
                S.op("act", lambda e: e.activation(out=out, in_=in_, func=AF.Copy), reads=reads, writes=writes)
            else:
                S.op("act", lambda e: e.activation(out=out, in_=in_, func=AF.Copy, scale=scale), reads=reads, writes=writes)
        else:
            S.op("dve", lambda e: e.tensor_copy(out=out, in_=in_), reads=reads, writes=writes)

    def load_w(slot_ap, slot_key, semkey, w_ap, kc0, nkc, c0, ncols):
        wv = w_ap.rearrange("(c p) f -> p c f", p=128)
        step = 8
        for k in range(0, nkc, step):
            n = min(step, nkc - k)
            S.dma("pool", slot_ap[:, k:k + n, :], wv[:, kc0 + k:kc0 + k + n, c0:c0 + ncols], semkey, writes=[slot_key])

    def norm_T(ntiles, src_fn, g_dram, dst_res=None, dst_scr=None, tag="n"):
        gsb = A.alloc(32, F32)
        S.dma("sp", gsb, g_dram[:, :], tag + "g", writes=[tag + "g"])
        xt = [A.alloc(D, F32) for _ in range(2)]
        junk = A.alloc(D, BF16)
        xs = [A.alloc(D, BF16) for _ in range(2)]
        stat = A.alloc(8, F32)
        aTt = None
        if dst_res is None:
            aTt = [A.alloc(32 * 512, BF16).rearrange("p (c t) -> p c t", c=32) for _ in range(2)]
        for tt in range(ntiles):
            s = tt % 2
            S.dma("sp", xt[s], src_fn(tt), (tag + "x", s), writes=[(tag + "xt", s)])
            ms, sd, rs = stat[:, 3 * s:3 * s + 1], stat[:, 3 * s + 1:3 * s + 2], stat[:, 3 * s + 2:3 * s + 3]
            S.op("act", lambda e, s=s, ms=ms: e.activation(out=junk, in_=xt[s], func=AF.Square, scale=1.0 / 64.0, accum_out=ms),
                 reads=[(tag + "xt", s)], writes=[tag + "junk", (tag + "ms", s)])
            S.op("act", lambda e, ms=ms, sd=sd: e.activation(out=sd, in_=ms, func=AF.Sqrt, bias=epscol, scale=1.0),
                 reads=[(tag + "ms", s), "epscol"], writes=[(tag + "sd", s)])
            S.op("dve", lambda e, sd=sd, rs=rs: e.reciprocal(out=rs, in_=sd), reads=[(tag + "sd", s)], writes=[(tag + "rs", s)])
            S.op("act", lambda e, s=s, rs=rs: e.activation(out=xs[s], in_=xt[s], func=AF.Copy, scale=rs),
                 reads=[(tag + "xt", s), (tag + "rs", s)], writes=[(tag + "xs", s)])
            if dst_res is not None:
                dst = dst_res[:, :, tt * 128:(tt + 1) * 128]
                dkey = (tag + "res", tt)
            else:
                gs = (tt // 4) % 2
                dst = aTt[gs][:, :, (tt % 4) * 128:(tt % 4 + 1) * 128]
                dkey = (tag + "aTt", gs)
            for g4 in range(4):
                b = (tt * 4 + g4) % 2
                pb = bank(b).bitcast(BF16)[:, 0:1024].rearrange("p (c t) -> p c t", c=8)
                for i in range(8):
                    dc = g4 * 8 + i
                    S.op("pe", lambda e, pb=pb, i=i, dc=dc, s=s: e.transpose(out=pb[:, i, :], in_=xs[s][:, dc * 128:(dc + 1) * 128], identity=identb),
                         reads=[(tag + "xs", s), "identb"], writes=[("ps", b)], signal=(i == 7))
                gb = gsb[:, g4 * 8:g4 * 8 + 8].unsqueeze(2).broadcast_to([128, 8, 128])
                S.op("dve", lambda e, pb=pb, dst=dst, g4=g4, gb=gb: e.tensor_tensor(out=dst[:, g4 * 8:g4 * 8 + 8, :], in0=pb, in1=gb, op=ALU.mult),
                     reads=[("ps", b), tag + "g"], writes=[dkey])
            if dst_res is None and tt % 4 == 3:
                t0 = (tt // 4) * 512
                for k8 in range(0, 32, 8):
                    S.dma("sp", dst_scr[k8:k8 + 8, :, t0:t0 + 512].rearrange("c p t -> p c t"), aTt[gs][:, k8:k8 + 8, :], (tag + "st", gs),
                          reads=[dkey], writes=[(tag + "scr", tt, k8)])

    norm_T(32, lambda tt: xw[tt * 128:(tt + 1) * 128, :], g_mix_d, dst_scr=aT_s, tag="s0")
    S.barrier()
    A.release(0)

    def s1():
        at = [A.alloc(32 * 512, BF16).rearrange("p (c t) -> p c t", c=32) for _ in range(2)]
        wsl = [A.alloc(32 * 512, BF16).rearrange("p (c t) -> p c t", c=32) for _ in range(2)]
        stg_f = [A.alloc(512, F32) for _ in range(4)]
        stg_b = [A.alloc(512, BF16) for _ in range(4)]
        allblk = [(i * 512, 512) for i in range(8)]
        ownblk = [(OH0 + i * NQ, NQ) for i in range(3)]
        groups = []
        for g in range(4):
            groups.append(("k", 2048 + g * 512, allblk, "fm"))
        for g in range(4):
            groups.append(("v", 4096 + g * 512, allblk, "tm"))
        for g in range(4):
            groups.append(("xr", 6144 + g * 512, allblk, "fm"))
        for g in range(4):
            groups.append(("q", g * 512, ownblk, "fm"))
        for g in range(4):
            groups.append(("yr", 8192 + g * 512, ownblk, "fm"))
        work = [(gi, bi) for gi, gr in enumerate(groups) for bi in range(len(gr[2]))]
        nat = [0]
        nst = [0]
        pbk = [0]

        def load_at(idx):
            gi, bi = work[idx]
            t0, nt = groups[gi][2][bi]
            s = idx % 2
            for k in range(0, 32, 8):
                S.dma("sp", at[s][:, k:k + 8, :nt], aT_s[k:k + 8, :, t0:t0 + nt].rearrange("c p t -> p c t"), ("at", s), writes=[("at", s)])

        def load_wg(gi):
            load_w(wsl[gi % 2], ("wsl", gi % 2), ("wsl", gi % 2), w_in, 0, 32, groups[gi][1], 512)

        load_wg(0)
        load_at(0)
        for idx, (gi, bi) in enumerate(work):
            name, c0, blks, lay = groups[gi]
            t0, nt = blks[bi]
            s = idx % 2
            if bi == 0 and gi + 1 < len(groups):
                load_wg(gi + 1)
            if idx + 1 < len(work):
                load_at(idx + 1)
            W = wsl[gi % 2]
            g = (c0 % 2048) // 512
            for sub in range(4):
                b = pbk[0] % 4
                pbk[0] += 1
                if lay == "fm":
                    for dc in range(32):
                        S.op("pe", lambda e, b=b, W=W, sub=sub, dc=dc, s=s, nt=nt: e.matmul(bank(b)[:, :nt], W[:, dc, sub * 128:(sub + 1) * 128], at[s][:, dc, :nt], start=(dc == 0), stop=(dc == 31)),
                             reads=[("wsl", gi % 2), ("at", s)], writes=[("ps", b)], signal=(dc == 31))
                    fc = g * 4 + sub
                    k = nst[0] % 4
                    nst[0] += 1
                    if name in ("k", "q"):
                        stg = stg_b[k]
                        sc = float(1.0 / np.sqrt(128.0)) if name == "q" else None
                        evac_copy(stg[:, :nt], bank(b)[:, :nt], [("ps", b)], [("stg", k)], scale=sc)
                        if name == "k":
                            dst = kT_s[fc, :, t0:t0 + nt]
                        else:
                            dst = qT_s[fc, :, t0 - OH0:t0 - OH0 + nt]
                    else:
                        stg = stg_f[k]
                        evac_copy(stg[:, :nt], bank(b)[:, :nt], [("ps", b)], [("stg", k)])
                        if name == "xr":
                            dst = xrT_s[fc, :, t0:t0 + nt]
                        else:
                            dst = yrT_s[fc, :, t0 - OH0:t0 - OH0 + nt]
                    S.dma("sp", dst, stg[:, :nt], ("stg", k), reads=[("stg", k)], writes=[("s1o", name, fc, t0)])
                else:
                    for dc in range(32):
                        S.op("pe", lambda e, b=b, W=W, sub=sub, dc=dc, s=s: e.matmul(bank(b)[:, :512], at[s][:, dc, sub * 128:(sub + 1) * 128], W[:, dc, :], start=(dc == 0), stop=(dc == 31)),
                             reads=[("wsl", gi % 2), ("at", s)], writes=[("ps", b)], signal=(dc == 31))
                    k = nst[0] % 4
                    nst[0] += 1
                    evac_copy(stg_b[k], bank(b)[:, :512], [("ps", b)], [("stg", k)])
                    S.dma("sp", v_s[t0 + sub * 128:t0 + (sub + 1) * 128, g * 512:(g + 1) * 512], stg_b[k], ("stg", k),
                          reads=[("stg", k)], writes=[("s1o", "v", g, t0, sub)])

    s1()
    S.barrier()
    A.release(0)

    def s2():
        cm = A.alloc(3 * NQ, BF16).rearrange("p (m t) -> p m t", m=3)
        S.dma("pool", cm, cmask_d[:, :, :], "cm", writes=["cm"])
        kT = [A.alloc(SW, BF16) for _ in range(2)]
        vh = [A.alloc(32 * 128, BF16).rearrange("p (k d) -> p k d", k=32) for _ in range(2)]
        qT = [A.alloc(OH, BF16) for _ in range(2)]
        eb = [A.alloc(NQ, F32) for _ in range(2)]
        Lh = [A.alloc(NQ, BF16) for _ in range(3)]
        ab = [A.alloc(NQ, BF16) for _ in range(3)]
        Rb = [A.alloc(NQ, BF16) for _ in range(3)]
        ost = [A.alloc(NQ, F32) for _ in range(2)]

        def load_head(h):
            s = h % 2
            S.dma("sp", kT[s], kT_s[h, :, :], ("kT", s), writes=[("kT", s)])
            S.dma("sp", vh[s], v_s[:, h * 128:(h + 1) * 128].rearrange("(k p) d -> p k d", p=128), ("vh", s), writes=[("vh", s)])
            S.dma("sp", qT[s], qT_s[h, :, :], ("qT", s), writes=[("qT", s)])

        pairs = []
        for h in range(16):
            for qb in range(3):
                t0 = OH0 + qb * NQ
                kmax = (t0 + NQ - 2) // 128
                for kb in range(kmax, -1, -1):
                    Dd = t0 - 128 * kb
                    mi = (-Dd // 128) if Dd in (0, -128, -256) else None
                    pairs.append(dict(h=h, qb=qb, kb=kb, mi=mi, first=(kb == kmax), last=(kb == 0)))
        n = len(pairs)
        gidx = [0] * n
        c = 0
        for i, p in enumerate(pairs):
            if p["first"]:
                c = 0
            gidx[i] = c
            c += 1

        def zmm(i, b):
            p = pairs[i]
            s = p["h"] % 2
            kb, qo = p["kb"], p["qb"] * NQ
            return s, kb, qo

        def stageA(i):
            p = pairs[i]
            if i == 0:
                load_head(0)
            if p["qb"] == 0 and gidx[i] == 4 and p["h"] + 1 < 16:
                load_head(p["h"] + 1)
            s, kb, qo = zmm(i, 0)
            b = i % 2
            part = p["mi"] is not None
            S.op("pe", lambda e: e.matmul(bank(b)[:, :NQ], kT[s][:, kb * 128:(kb + 1) * 128], qT[s][:, qo:qo + NQ], start=True, stop=not part),
                 reads=[("kT", s), ("qT", s)], writes=[("ps", b)], signal=not part)
            if part:
                mi = p["mi"]
                S.op("pe", lambda e: e.matmul(bank(b)[:, :NQ], identb, cm[:, mi, :], start=False, stop=True),
                     reads=["identb", "cm"], writes=[("ps", b)], signal=True)
            S.op("act", lambda e: e.activation(out=eb[b], in_=bank(b)[:, :NQ], func=AF.Exp, bias=kbias[:, kb:kb + 1], scale=1.0),
                 reads=[("ps", b), "kbias"], writes=[("eb", b)])
            l = i % 3
            S.op("act", lambda e: e.activation(out=Lh[l], in_=eb[b], func=AF.Ln, bias=onecol, scale=1.0),
                 reads=[("eb", b), "onecol"], writes=[("Lh", l)])
            if not p["last"]:
                rn = (i + 1) % 3
                ro = i % 3
                if p["first"]:
                    S.op("dve", lambda e: e.tensor_copy(out=Rb[rn], in_=Lh[l]), reads=[("Lh", l)], writes=[("Rb", rn)])
                else:
                    S.op("dve", lambda e: e.tensor_tensor(out=Rb[rn], in0=Rb[ro], in1=Lh[l], op=ALU.add),
                         reads=[("Lh", l), ("Rb", ro)], writes=[("Rb", rn)])

        def stageB(i):
            p = pairs[i]
            s, kb, qo = zmm(i, 0)
            b = 2 + i % 2
            l = i % 3
            gi = gidx[i]
            part = p["mi"] is not None
            S.op("pe", lambda e: e.matmul(bank(b)[:, :NQ], kT[s][:, kb * 128:(kb + 1) * 128], qT[s][:, qo:qo + NQ], start=True, stop=False),
                 reads=[("kT", s), ("qT", s)], writes=[("ps", b)], signal=False)
            if part:
                mi = p["mi"]
                S.op("pe", lambda e: e.matmul(bank(b)[:, :NQ], identb, cm[:, mi, :], start=False, stop=False),
                     reads=["identb", "cm"], writes=[("ps", b)], signal=False)
            S.op("pe", lambda e: e.matmul(bank(b)[:, :NQ], negtri, Lh[l], start=False, stop=p["first"]),
                 reads=["negtri", ("Lh", l)], writes=[("ps", b)], signal=p["first"])
            if not p["first"]:
                S.op("pe", lambda e: e.matmul(bank(b)[:, :NQ], negones, Rb[i % 3], start=False, stop=True),
                     reads=["negones", ("Rb", i % 3)], writes=[("ps", b)], signal=True)
            S.op("act", lambda e: e.activation(out=ab[l], in_=bank(b)[:, :NQ], func=AF.Exp, bias=kbias[:, kb:kb + 1], scale=1.0),
                 reads=[("ps", b), "kbias"], writes=[("ab", l)])

        grp = [0]

        def stageO(i):
            p = pairs[i]
            s, kb, qo = zmm(i, 0)
            l = i % 3
            if p["first"]:
                grp[0] += 1
            b = 4 + grp[0] % 2
            S.op("pe", lambda e: e.matmul(bank(b)[:, :NQ], vh[s][:, kb, :], ab[l], start=p["first"], stop=p["last"]),
                 reads=[("vh", s), ("ab", l)], writes=[("ps", b)], signal=p["last"])
            if p["last"]:
                k = grp[0] % 2
                evac_copy(ost[k], bank(b)[:, :NQ], [("ps", b)], [("ost", k)])
                S.dma("sp", attT_s[p["h"], :, qo:qo + NQ], ost[k], ("ost", k), reads=[("ost", k)], writes=[("att", p["h"], qo)])

        for st in range(n + 2):
            if st < n:
                stageA(st)
            if 0 <= st - 1 < n:
                stageB(st - 1)
            if 0 <= st - 2 < n:
                stageO(st - 2)

    s2()
    S.barrier()
    A.release(0)

    def s3():
        SEG = 1024
        tv = A.alloc(SW, F32)
        S.dma("sp", tv, tokvalid_d[:, :], "tv", writes=["tv"])
        wrc = A.alloc(64, F32).rearrange("p (c k) -> p c k", c=16)
        brc = A.alloc(16, F32)
        bga = A.alloc(16, F32)
        bgx = A.alloc(16, F32)
        lam = A.alloc(16, F32)
        c8 = A.alloc(16, F32)
        c16 = A.alloc(16, F32)
        S.dma("sp", wrc, w_rconv_d[:, :, :], "s3c", writes=["s3c"])
        S.dma("sp", brc, b_rconv_d[:, :], "s3c", writes=["s3c1"])
        S.dma("sp", bga, b_rg_a_d[:, :], "s3c", writes=["s3c2"])
        S.dma("sp", bgx, b_rg_x_d[:, :], "s3c", writes=["s3c3"])
        S.dma("sp", lam, lam_d[:, :], "s3c", writes=["s3c4"])
        S.op("act", lambda e: e.activation(out=c8, in_=lam, func=AF.Exp, scale=-1.0), reads=["s3c4"], writes=["c8"])
        S.op("act", lambda e: e.activation(out=c8, in_=c8, func=AF.Ln, bias=onecol, scale=1.0), reads=["c8", "onecol"], writes=["c8"])
        S.op("dve", lambda e: e.tensor_scalar_mul(out=c16, in0=c8, scalar1=-16.0), reads=["c8"], writes=["c16"])
        S.op("dve", lambda e: e.tensor_scalar_mul(out=c8, in0=c8, scalar1=-8.0), reads=["c8", "c16"], writes=["c8"])
        xr = [A.alloc(3 + SW, F32) for _ in range(2)]
        for s in range(2):
            S.op("dve", lambda e, s=s: e.memset(xr[s][:, 0:3], 0.0), writes=[("xr", s)])
        yr = [A.alloc(OH, F32) for _ in range(2)]
        Wa = [A.alloc(128, BF16) for _ in range(2)]
        Wx = [A.alloc(128, BF16) for _ in range(2)]
        hb = [A.alloc(SW, F32) for _ in range(2)]
        xc = [A.alloc(SEG, F32) for _ in range(2)]
        xcb = [A.alloc(SEG, BF16) for _ in range(2)]
        rr = [A.alloc(SEG, F32) for _ in range(2)]
        ii = [A.alloc(SEG, F32) for _ in range(2)]
        aa = [A.alloc(SEG, F32) for _ in range(2)]
        mm = [A.alloc(SEG, F32) for _ in range(2)]
        gy = [A.alloc(OH, F32) for _ in range(2)]
        seg_i = [0]

        def loadc(c):
            s = c % 2
            S.dma("sp", xr[s][:, 3:], xrT_s[c, :, :], ("xr", s), writes=[("xr", s)])
            S.dma("sp", yr[s], yrT_s[c, :, :], ("yr", s), writes=[("yr", s)])
            S.dma("pool", Wa[s], w_rg_a[c, :, :], ("Wg", s), writes=[("Wa", s)])
            S.dma("pool", Wx[s], w_rg_x[c, :, :], ("Wg", s), writes=[("Wx", s)])

        def head(c, sg, u):
            s = c % 2
            s0 = sg * SEG
            X = xr[s]
            S.op("act", lambda e: e.activation(out=xc[u], in_=X[:, 3 + s0:3 + s0 + SEG], func=AF.Identity, bias=brc[:, c:c + 1], scale=wrc[:, c, 3:4]),
                 reads=[("xr", s), "s3c", "s3c1"], writes=[("xc", u)])
            for k in range(3):
                S.op("dve", lambda e, k=k: e.scalar_tensor_tensor(out=xc[u], in0=X[:, k + s0:k + s0 + SEG], scalar=wrc[:, c, k:k + 1], in1=xc[u], op0=ALU.mult, op1=ALU.add),
                     reads=[("xr", s), ("xc", u), "s3c"], writes=[("xc", u)])
            S.op("dve", lambda e: e.tensor_copy(out=xcb[u], in_=xc[u]), reads=[("xc", u)], writes=[("xcb", u)])
            for half in range(2):
                ba, bx = 4 + 2 * half, 5 + 2 * half
                sl = slice(half * 512, (half + 1) * 512)
                S.op("pe", lambda e, sl=sl, ba=ba: e.matmul(bank(ba)[:, :512], Wa[s], xcb[u][:, sl], start=True, stop=True),
                     reads=[("Wa", s), ("xcb", u)], writes=[("ps", ba)])
                S.op("pe", lambda e, sl=sl, bx=bx: e.matmul(bank(bx)[:, :512], Wx[s], xcb[u][:, sl], start=True, stop=True),
                     reads=[("Wx", s), ("xcb", u)], writes=[("ps", bx)])

        def headB(c, sg, u):
            for half in range(2):
                ba, bx = 4 + 2 * half, 5 + 2 * half
                sl = slice(half * 512, (half + 1) * 512)
                S.op("act", lambda e, sl=sl, ba=ba: e.activation(out=rr[u][:, sl], in_=bank(ba)[:, :512], func=AF.Sigmoid, bias=bga[:, c:c + 1], scale=1.0),
                     reads=[("ps", ba), "s3c2"], writes=[("rr", u, half)])
                S.op("act", lambda e, sl=sl, bx=bx: e.activation(out=ii[u][:, sl], in_=bank(bx)[:, :512], func=AF.Sigmoid, bias=bgx[:, c:c + 1], scale=1.0),
                     reads=[("ps", bx), "s3c3"], writes=[("ii", u, half)])

        def tail(c, sg, u):
            s = c % 2
            s0 = sg * SEG
            S.op("act", lambda e: e.activation(out=aa[u], in_=rr[u], func=AF.Exp, scale=c8[:, c:c + 1]),
                 reads=[("rr", u, 0), ("rr", u, 1), "c8"], writes=[("aa", u)])
            S.op("act", lambda e: e.activation(out=mm[u], in_=rr[u], func=AF.Exp, scale=c16[:, c:c + 1]),
                 reads=[("rr", u, 0), ("rr", u, 1), "c16"], writes=[("mm", u)])
            S.op("act", lambda e: e.activation(out=mm[u], in_=mm[u], func=AF.Sqrt, bias=onecol, scale=-1.0),
                 reads=[("mm", u), "onecol"], writes=[("mm", u)])
            S.op("dve", lambda e: e.tensor_tensor(out=ii[u], in0=ii[u], in1=xc[u], op=ALU.mult),
                 reads=[("ii", u, 0), ("ii", u, 1), ("xc", u)], writes=[("ii", u, 0), ("ii", u, 1)])
            S.op("pool", lambda e: e.tensor_tensor(out=ii[u], in0=ii[u], in1=tv[:, s0:s0 + SEG], op=ALU.mult),
                 reads=[("ii", u, 0), ("ii", u, 1), "tv"], writes=[("ii", u, 0), ("ii", u, 1)])
            S.op("dve", lambda e: e.tensor_tensor(out=mm[u], in0=mm[u], in1=ii[u], op=ALU.mult),
                 reads=[("mm", u), ("ii", u, 0), ("ii", u, 1)], writes=[("mm", u)])
            H = hb[s]
            if sg == 0:
                S.op("dve", lambda e: e.tensor_tensor_scan(out=H[:, 0:SEG], data0=aa[u], data1=mm[u], initial=0.0, op0=ALU.mult, op1=ALU.add),
                     reads=[("aa", u), ("mm", u)], writes=[("hb", s)])
            else:
                S.op("dve", lambda e: e.tensor_tensor_scan(out=H[:, s0:s0 + SEG], data0=aa[u], data1=mm[u], initial=H[:, s0 - 1:s0], op0=ALU.mult, op1=ALU.add),
                     reads=[("aa", u), ("mm", u), ("hb", s)], writes=[("hb", s)])
            if sg == 3:
                S.op("act", lambda e: e.activation(out=gy[s], in_=yr[s], func=AF.Gelu_apprx_tanh), reads=[("yr", s)], writes=[("gy", s)])
                S.op("dve", lambda e: e.tensor_tensor(out=gy[s], in0=gy[s], in1=hb[s][:, OH0:SW], op=ALU.mult),
                     reads=[("gy", s), ("hb", s)], writes=[("gy", s)])
                S.dma("sp", recT_s[c, :, :], gy[s], ("gy", s), reads=[("gy", s)], writes=[("rec", c)])

        units = [(c, sg) for c in range(16) for sg in range(4)]
        loadc(0)
        for n_, (c, sg) in enumerate(units):
            head(c, sg, n_ % 2)
            if n_ >= 1:
                pc, psg = units[n_ - 1]
                tail(pc, psg, (n_ - 1) % 2)
            headB(c, sg, n_ % 2)
            if sg == 0 and c + 1 < 16:
                loadc(c + 1)
        pc, psg = units[-1]
        tail(pc, psg, (len(units) - 1) % 2)

    s3()
    S.barrier()
    A.release(0)

    def s4():
        ga = A.alloc(16, F32)
        gr = A.alloc(16, F32)
        S.dma("sp", ga, g_att_d[:, :], "s4g", writes=["ga"])
        S.dma("sp", gr, g_rec_d[:, :], "s4g", writes=["gr"])
        blk = [A.alloc(16 * NQ, F32).rearrange("p (c t) -> p c t", c=16) for _ in range(2)]
        sq = [A.alloc(16 * NQ, BF16).rearrange("p (c t) -> p c t", c=16) for _ in range(2)]
        mx = [A.alloc(16 * NQ, BF16).rearrange("p (c t) -> p c t", c=16) for _ in range(2)]
        rs = [A.alloc(NQ, F32) for _ in range(2)]
        it = 0
        for grp_i, (src, gv, gk) in enumerate(((attT_s, ga, "ga"), (recT_s, gr, "gr"))):
            for tb in range(3):
                s = it % 2
                it += 1
                t0 = tb * NQ
                S.dma("sp", blk[s], src[:, :, t0:t0 + NQ].rearrange("c p t -> p c t"), ("blk", s), writes=[("blk", s)])
                S.op("act", lambda e, s=s: e.activation(out=sq[s], in_=blk[s], func=AF.Square), reads=[("blk", s)], writes=[("sq", s)])
                b = s
                for c in range(16):
                    S.op("pe", lambda e, s=s, c=c, b=b: e.matmul(bank(b)[:, :NQ], onesb, sq[s][:, c, :], start=(c == 0), stop=(c == 15)),
                         reads=["onesb", ("sq", s)], writes=[("ps", b)], signal=(c == 15))
                S.op("act", lambda e, s=s, b=b: e.activation(out=rs[s], in_=bank(b)[:, :NQ], func=AF.Sqrt, bias=epscol, scale=1.0 / 2048.0),
                     reads=[("ps", b), "epscol"], writes=[("rs", s)])
                S.op("dve", lambda e, s=s: e.reciprocal(out=rs[s], in_=rs[s]), reads=[("rs", s)], writes=[("rs", s)])
                for c in range(16):
                    S.op("dve", lambda e, s=s, c=c, gv=gv: e.scalar_tensor_tensor(out=mx[s][:, c, :], in0=blk[s][:, c, :], scalar=gv[:, c:c + 1], in1=rs[s], op0=ALU.mult, op1=ALU.mult),
                         reads=[("blk", s), ("rs", s), gk], writes=[("mx", s)])
                S.dma("sp", mixT_s[grp_i * 16:(grp_i + 1) * 16, :, t0:t0 + NQ].rearrange("c p t -> p c t"), mx[s], ("mx", s),
                      reads=[("mx", s)], writes=[("mix", grp_i, tb)])

    s4()
    S.barrier()
    A.release(0)

    def s5():
        mixT = A.alloc(32 * OH, BF16).rearrange("p (c t) -> p c t", c=32)
        for k in range(0, 32, 8):
            S.dma("sp", mixT[:, k:k + 8, :], mixT_s[k:k + 8, :, :].rearrange("c p t -> p c t"), "mixT", writes=["mixT"])
        wsl = [A.alloc(32 * 512, BF16).rearrange("p (c t) -> p c t", c=32) for _ in range(2)]
        xin = [A.alloc(512, F32) for _ in range(3)]
        load_w(wsl[0], ("wsl", 0), ("wsl", 0), w_out, 0, 32, 0, 512)
        it = 0
        for cg in range(8):
            if cg + 1 < 8:
                load_w(wsl[(cg + 1) % 2], ("wsl", (cg + 1) % 2), ("wsl", (cg + 1) % 2), w_out, 0, 32, (cg + 1) * 512, 512)
            W = wsl[cg % 2]
            for tt in range(9):
                k = it % 3
                b = it % 4
                it += 1
                S.dma("sp", xin[k], xw[OH0 + tt * 128:OH0 + (tt + 1) * 128, cg * 512:(cg + 1) * 512], ("xin", k), writes=[("xin", k)])
                for dc in range(32):
                    S.op("pe", lambda e, b=b, W=W, dc=dc, tt=tt: e.matmul(bank(b)[:, :512], mixT[:, dc, tt * 128:(tt + 1) * 128], W[:, dc, :], start=(dc == 0), stop=(dc == 31)),
                         reads=["mixT", ("wsl", cg % 2)], writes=[("ps", b)], signal=(dc == 31))
                S.op("dve", lambda e, b=b, k=k: e.tensor_tensor(out=xin[k], in0=bank(b)[:, :512], in1=xin[k], op=ALU.add),
                     reads=[("ps", b), ("xin", k)], writes=[("xin", k)])
                S.dma("sp", h1_s[tt * 128:(tt + 1) * 128, cg * 512:(cg + 1) * 512], xin[k], ("xin", k), reads=[("xin", k)], writes=[("h1", tt, cg)])

    s5()
    S.barrier()
    A.release(0)

    def ffn():
        mT = A.alloc(32 * OH, BF16).rearrange("p (c t) -> p c t", c=32)
        m0 = A.mark()
        norm_T(9, lambda tt: h1_s[tt * 128:(tt + 1) * 128, :], g_ffn_d, dst_res=mT, tag="s6")
        S.barrier()
        A.release(m0)
        wfc = A.alloc(192 * 3, F32).rearrange("p (c k) -> p c k", c=192)
        bfc = A.alloc(192, F32)
        S.dma("sp", wfc, w_ffconv_d[:, :, :], "fc", writes=["wfc"])
        S.dma("sp", bfc, b_ffconv_d[:, :], "fc", writes=["bfc"])
        NP, HP = 8, 12
        wg = [A.alloc(32 * 128, BF16).rearrange("p (c t) -> p c t", c=32) for _ in range(2)]
        wu = [A.alloc(32 * 128, BF16).rearrange("p (c t) -> p c t", c=32) for _ in range(2)]
        hidT = A.alloc(HP * OWN, BF16).rearrange("p (c t) -> p c t", c=HP)
        wd = [A.alloc(HP * 512, BF16).rearrange("p (c t) -> p c t", c=HP) for _ in range(2)]
        ug = [A.alloc(2 + OH, F32) for _ in range(2)]
        uu = [A.alloc(2 + OH, F32) for _ in range(2)]
        gc = A.alloc(OWN, F32)
        uc = A.alloc(OWN, F32)
        gl = A.alloc(OWN, F32)
        hin = [A.alloc(512, F32) for _ in range(8)]

        def load_up(hc):
            s = hc % 2
            load_w(wg[s], ("wg", s), ("wg", s), w_up, 0, 32, hc * 128, 128)
            load_w(wu[s], ("wu", s), ("wu", s), w_up, 0, 32, 12288 + hc * 128, 128)

        def load_dn(part, cg):
            s = (part * 8 + cg) % 2
            load_w(wd[s], ("wd", s), ("wd", s), w_down, part * HP, HP, cg * 512, 512)

        pbk = [0]
        hit = [0]
        load_up(0)
        for part in range(NP):
            for hcl in range(HP):
                hc = part * HP + hcl
                s = hc % 2
                if hc + 1 < 96:
                    load_up(hc + 1)
                for tb, (tb0, tbn) in enumerate(((96, 32), (128, 512), (640, 512))):
                    bg = (pbk[0] * 2) % 4
                    bu = bg + 1
                    pbk[0] += 1
                    sl = slice(tb0, tb0 + tbn)
                    for dc in range(32):
                        S.op("pe", lambda e, bg=bg, s=s, dc=dc, sl=sl, tbn=tbn: e.matmul(bank(bg)[:, :tbn], wg[s][:, dc, :], mT[:, dc, sl], start=(dc == 0), stop=(dc == 31)),
                             reads=[("wg", s)], writes=[("ps", bg)], signal=(dc == 31))
                    for dc in range(32):
                        S.op("pe", lambda e, bu=bu, s=s, dc=dc, sl=sl, tbn=tbn: e.matmul(bank(bu)[:, :tbn], wu[s][:, dc, :], mT[:, dc, sl], start=(dc == 0), stop=(dc == 31)),
                             reads=[("wu", s)], writes=[("ps", bu)], signal=(dc == 31))
                    S.op("act", lambda e, bg=bg, s=s, tb0=tb0, tbn=tbn: e.activation(out=ug[s][:, 2 + tb0:2 + tb0 + tbn], in_=bank(bg)[:, :tbn], func=AF.Copy),
                         reads=[("ps", bg)], writes=[("ug", s, tb)])
                    S.op("dve", lambda e, bu=bu, s=s, tb0=tb0, tbn=tbn: e.tensor_copy(out=uu[s][:, 2 + tb0:2 + tb0 + tbn], in_=bank(bu)[:, :tbn]),
                         reads=[("ps", bu)], writes=[("uu", s, tb)])
                ugk = [("ug", s, t) for t in range(3)]
                uuk = [("uu", s, t) for t in range(3)]
                fg, fu = hc, 96 + hc
                S.op("act", lambda e, s=s, fg=fg: e.activation(out=gc, in_=ug[s][:, 130:130 + OWN], func=AF.Identity, bias=bfc[:, fg:fg + 1], scale=wfc[:, fg, 2:3]),
                     reads=ugk + ["wfc", "bfc"], writes=["gc"])
                for k in range(2):
                    S.op("dve", lambda e, s=s, fg=fg, k=k: e.scalar_tensor_tensor(out=gc, in0=ug[s][:, 128 + k:128 + k + OWN], scalar=wfc[:, fg, k:k + 1], in1=gc, op0=ALU.mult, op1=ALU.add),
                         reads=ugk + ["gc", "wfc"], writes=["gc"])
                S.op("act", lambda e, s=s, fu=fu: e.activation(out=uc, in_=uu[s][:, 130:130 + OWN], func=AF.Identity, bias=bfc[:, fu:fu + 1], scale=wfc[:, fu, 2:3]),
                     reads=uuk + ["wfc", "bfc"], writes=["uc"])
                for k in range(2):
                    S.op("dve", lambda e, s=s, fu=fu, k=k: e.scalar_tensor_tensor(out=uc, in0=uu[s][:, 128 + k:128 + k + OWN], scalar=wfc[:, fu, k:k + 1], in1=uc, op0=ALU.mult, op1=ALU.add),
                         reads=uuk + ["uc", "wfc"], writes=["uc"])
                S.op("act", lambda e: e.activation(out=gl, in_=gc, func=AF.Gelu_apprx_tanh), reads=["gc"], writes=["gl"])
                S.op("dve", lambda e, hcl=hcl: e.tensor_tensor(out=hidT[:, hcl, :], in0=gl, in1=uc, op=ALU.mult),
                     reads=["gl", "uc"], writes=["hidT"])
            load_dn(part, 0)

            def load_hin(g):
                cg_, tt_ = g // 8, g % 8
                k_ = (hit[0] + (g - gcur[0])) % 8
                if part == 0:
                    src = h1_s[128 + tt_ * 128:128 + (tt_ + 1) * 128, cg_ * 512:(cg_ + 1) * 512]
                    rk = []
                else:
                    src = hS_s[tt_ * 128:(tt_ + 1) * 128, cg_ * 512:(cg_ + 1) * 512]
                    rk = [("hS", tt_, cg_)]
                S.dma("sp", hin[k_], src, ("hin", k_), reads=rk, writes=[("hin", k_)])

            LA = 4
            gcur = [0]
            for g in range(LA):
                load_hin(g)
            for cg in range(8):
                if cg + 1 < 8:
                    load_dn(part, cg + 1)
                sW = (part * 8 + cg) % 2
                for tt in range(8):
                    gcur[0] = cg * 8 + tt
                    if gcur[0] + LA < 64:
                        load_hin(gcur[0] + LA)
                    k = hit[0] % 8
                    hit[0] += 1
                    b = 4 + k % 4
                    for hcl in range(HP):
                        S.op("pe", lambda e, b=b, hcl=hcl, tt=tt, sW=sW: e.matmul(bank(b)[:, :512], hidT[:, hcl, tt * 128:(tt + 1) * 128], wd[sW][:, hcl, :], start=(hcl == 0), stop=(hcl == HP - 1)),
                             reads=["hidT", ("wd", sW)], writes=[("ps", b)], signal=(hcl == HP - 1))
                    S.op("dve", lambda e, b=b, k=k: e.tensor_tensor(out=hin[k], in0=bank(b)[:, :512], in1=hin[k], op=ALU.add),
                         reads=[("ps", b), ("hin", k)], writes=[("hin", k)])
                    S.dma("sp", hS_s[tt * 128:(tt + 1) * 128, cg * 512:(cg + 1) * 512], hin[k], ("hin", k), reads=[("hin", k)], writes=[("hS", tt, cg)])

    ffn()
    S.barrier()
    A.release(0)

    def ple():
        nT = A.alloc(32 * OWN, BF16).rearrange("p (c t) -> p c t", c=32)
        m0 = A.mark()
        norm_T(8, lambda tt: hS_s[tt * 128:(tt + 1) * 128, :], g_ple_d, dst_res=nT, tag="s9")
        S.barrier()
        A.release(m0)
        pT = A.alloc(2 * OWN, BF16).rearrange("p (c t) -> p c t", c=2)
        pin = [A.alloc(256, F32) for _ in range(2)]
        pbf = [A.alloc(256, BF16) for _ in range(2)]
        for tt in range(8):
            s = tt % 2
            S.dma("sp", pin[s], p_own[tt * 128:(tt + 1) * 128, :], ("pin", s), writes=[("pin", s)])
            S.op("dve", lambda e, s=s: e.tensor_copy(out=pbf[s], in_=pin[s]), reads=[("pin", s)], writes=[("pbf", s)])
            b = s
            pb = bank(b).bitcast(BF16)[:, 0:256].rearrange("p (c t) -> p c t", c=2)
            for c in range(2):
                S.op("pe", lambda e, pb=pb, c=c, s=s: e.transpose(out=pb[:, c, :], in_=pbf[s][:, c * 128:(c + 1) * 128], identity=identb),
                     reads=[("pbf", s), "identb"], writes=[("ps", b)], signal=(c == 1))
            S.op("dve", lambda e, pb=pb, tt=tt: e.tensor_copy(out=pT[:, :, tt * 128:(tt + 1) * 128], in_=pb), reads=[("ps", b)], writes=["pT"])
        wsl = [A.alloc(32 * 512, BF16).rearrange("p (c t) -> p c t", c=32) for _ in range(2)]
        wp = [A.alloc(2 * 512, BF16).rearrange("p (c t) -> p c t", c=2) for _ in range(2)]
        sg = [A.alloc(512, F32) for _ in range(2)]
        hin = [A.alloc(512, F32) for _ in range(3)]

        def loadw(cg):
            s = cg % 2
            load_w(wsl[s], ("wsl", s), ("wsl", s), w_ple_gate, 0, 32, cg * 512, 512)
            load_w(wp[s], ("wp", s), ("wp", s), w_ple, 0, 2, cg * 512, 512)

        loadw(0)
        it = 0
        for cg in range(8):
            if cg + 1 < 8:
                loadw(cg + 1)
            s = cg % 2
            for tt in range(8):
                k = it % 3
                u = it % 2
                bg = (it % 2) * 2
                bp = bg + 1
                it += 1
                S.dma("sp", hin[k], hS_s[tt * 128:(tt + 1) * 128, cg * 512:(cg + 1) * 512], ("hin", k), writes=[("hin", k)])
                for dc in range(32):
                    S.op("pe", lambda e, bg=bg, dc=dc, tt=tt, s=s: e.matmul(bank(bg)[:, :512], nT[:, dc, tt * 128:(tt + 1) * 128], wsl[s][:, dc, :], start=(dc == 0), stop=(dc == 31)),
                         reads=[("wsl", s)], writes=[("ps", bg)], signal=(dc == 31))
                for c in range(2):
                    S.op("pe", lambda e, bp=bp, c=c, tt=tt, s=s: e.matmul(bank(bp)[:, :512], pT[:, c, tt * 128:(tt + 1) * 128], wp[s][:, c, :], start=(c == 0), stop=(c == 1)),
                         reads=["pT", ("wp", s)], writes=[("ps", bp)], signal=(c == 1))
                S.op("act", lambda e, bg=bg, u=u: e.activation(out=sg[u], in_=bank(bg)[:, :512], func=AF.Sigmoid), reads=[("ps", bg)], writes=[("sg", u)])
                S.op("dve", lambda e, bp=bp, u=u: e.tensor_tensor(out=sg[u], in0=bank(bp)[:, :512], in1=sg[u], op=ALU.mult),
                     reads=[("ps", bp), ("sg", u)], writes=[("sg", u)])
                S.op("dve", lambda e, u=u, k=k: e.tensor_tensor(out=hin[k], in0=sg[u], in1=hin[k], op=ALU.add),
                     reads=[("sg", u), ("hin", k)], writes=[("hin", k)])
                S.dma("sp", h1_s[tt * 128:(tt + 1) * 128, cg * 512:(cg + 1) * 512], hin[k], ("hin", k), reads=[("hin", k)], writes=[("h3", tt, cg)])

    ple()
    S.barrier()
    A.release(0)

    def final():
        gf = A.alloc(D, F32)
        S.dma("sp", gf, g_final_d[:, :], "gf", writes=["gf"])
        xt = [A.alloc(D, F32) for _ in range(2)]
        junk = A.alloc(D, BF16)
        ot = [A.alloc(D, F32) for _ in range(2)]
        stat = A.alloc(8, F32)
        for tt in range(8):
            s = tt % 2
            S.dma("sp", xt[s], h1_s[tt * 128:(tt + 1) * 128, :], ("fx", s), writes=[("fx", s)])
            ms, sd, rs = stat[:, 3 * s:3 * s + 1], stat[:, 3 * s + 1:3 * s + 2], stat[:, 3 * s + 2:3 * s + 3]
            S.op("act", lambda e, s=s, ms=ms: e.activation(out=junk, in_=xt[s], func=AF.Square, scale=1.0 / 64.0, accum_out=ms),
                 reads=[("fx", s)], writes=["fjunk", ("fms", s)])
            S.op("act", lambda e, ms=ms, sd=sd: e.activation(out=sd, in_=ms, func=AF.Sqrt, bias=epscol, scale=1.0),
                 reads=[("fms", s), "epscol"], writes=[("fsd", s)])
            S.op("dve", lambda e, sd=sd, rs=rs: e.reciprocal(out=rs, in_=sd), reads=[("fsd", s)], writes=[("frs", s)])
            S.op("dve", lambda e, s=s, rs=rs: e.scalar_tensor_tensor(out=ot[s], in0=xt[s], scalar=rs, in1=gf, op0=ALU.mult, op1=ALU.mult),
                 reads=[("fx", s), ("frs", s), "gf"], writes=[("fo", s)])
            S.dma("sp", out_d[tt * 128:(tt + 1) * 128, :], ot[s], ("fo", s), reads=[("fo", s)], writes=[("out", tt)])

    final()
    S.barrier()

    keys = S.semkeys()
    sems = {}
    for i, k in enumerate(keys):
        sems[k] = stack.enter_context(nc.semaphore("s%d" % i))
    with nc.Block() as block:
        @block.tensor
        def _(e):
            S.emit("pe", e, sems)

        @block.scalar
        def _(e):
            S.emit("act", e, sems)

        @block.vector
        def _(e):
            S.emit("dve", e, sems)

        @block.gpsimd
        def _(e):
            S.emit("pool", e, sems)

        @block.sync
        def _(e):
            S.emit("sp", e, sems)
    stack.close()
    return nc


def _pc(v, n):
    v = np.asarray(v, np.float32).reshape(n // 128, 128)
    return np.ascontiguousarray(v.T)


def kernel(_debug=False, **inp):
    x = np.asarray(inp["x"], np.float32)
    p = np.asarray(inp["p"], np.float32)
    ident = np.eye(128, dtype=np.float32)
    jj, ss = np.meshgrid(np.arange(128), np.arange(128), indexing="ij")
    negtri = np.where(jj >= ss, -1.0, 0.0).astype(np.float32)
    negones = -np.ones((128, 128), np.float32)
    cmask = np.zeros((128, 3, NQ), np.float32)
    sp_ = np.arange(128)[:, None]
    tq = np.arange(NQ)[None, :]
    for mi in range(3):
        Dd = -128 * mi
        cmask[:, mi, :] = np.where(sp_ - tq < Dd, 0.0, NEG)
    common = {
        "ident": ident, "negtri": negtri, "negones": negones, "cmask": cmask,
        "g_mix": _pc(inp["g_mix"][0], 4096),
        "w_in": np.ascontiguousarray(inp["w_in"][0], np.float32),
        "w_rconv": np.ascontiguousarray(np.asarray(inp["w_rconv"][0], np.float32).reshape(4, 16, 128).transpose(2, 1, 0)),
        "b_rconv": _pc(inp["b_rconv"][0], 2048),
        "w_rg_a": np.ascontiguousarray(inp["w_rg_a"][0], np.float32),
        "b_rg_a": _pc(inp["b_rg_a"][0], 2048),
        "w_rg_x": np.ascontiguousarray(inp["w_rg_x"][0], np.float32),
        "b_rg_x": _pc(inp["b_rg_x"][0], 2048),
        "lam": _pc(inp["lam"][0], 2048),
        "g_att_out": _pc(inp["g_att_out"][0], 2048),
        "g_rec_out": _pc(inp["g_rec_out"][0], 2048),
        "w_out": np.ascontiguousarray(inp["w_out"][0], np.float32),
        "g_ffn": _pc(inp["g_ffn"][0], 4096),
        "w_up": np.ascontiguousarray(inp["w_up"][0], np.float32),
        "w_ffconv": np.ascontiguousarray(np.asarray(inp["w_ffconv"][0], np.float32).reshape(3, 192, 128).transpose(2, 1, 0)),
        "b_ffconv": _pc(inp["b_ffconv"][0], 24576),
        "w_down": np.ascontiguousarray(inp["w_down"][0], np.float32),
        "g_ple": _pc(inp["g_ple"][0], 4096),
        "w_ple": np.ascontiguousarray(inp["w_ple"][0], np.float32),
        "w_ple_gate": np.ascontiguousarray(inp["w_ple_gate"][0], np.float32),
        "g_final": np.ascontiguousarray(np.broadcast_to(np.asarray(inp["g_final"], np.float32)[None, :], (128, D))),
    }
    in_maps = []
    for c in range(8):
        b, j = c // 4, c % 4
        end = 1024 * (j + 1)
        P = SW - end
        xwin = np.zeros((SW, D), np.float32)
        xwin[P:] = x[b, :end]
        kb = np.zeros((128, 32), np.float32)
        kb[:, : P // 128] = NEG
        tv = np.ones((128, SW), np.float32)
        tv[:, :P] = 0.0
        m = dict(common)
        m.update({"xw": xwin, "p_own": np.ascontiguousarray(p[0, b, end - 1024:end]), "kbias": kb, "tokvalid": tv})
        in_maps.append(m)
    nc = build_program(debug=_debug)
    res = run_bass_kernel_spmd(nc, in_maps, core_ids=list(range(8)))
    if _debug:
        return res
    out = np.zeros((2, 4096, D), np.float32)
    for c in range(8):
        b, j = c // 4, c % 4
        out[b, 1024 * j:1024 * (j + 1)] = res.results[c]["out"]
    return out
```
